# Optimizing a Trainium2 kernel written in Bass

```python
import jax, jax.numpy as jnp
from jax import lax
import numpy as np

D_MODEL = 1024
BATCH = 8
SEQ = 4096
DEPTH = 2
DEC_BATCH = 16
DEC_SEQ = 16
PAST_LEN = 1024

CHUNK = 64
HEAD_DIM = 64
H_A = 8
H_B = 8
H_C = 8
H_D = 8
Q_RANK = 256
KV_RANK = 128
NOPE_DIM = 64
ROPE_DIM = 32
V_DIM_A = 64
ROPE_BASE = 10000.0
BAND_CHUNKS = 8
BAND_ROWS = BAND_CHUNKS * CHUNK
REL_CLIP = 128
D_FF = 2816
CONV_W = 3
Q_BLOCK = 128
EPS = 1e-6
NEG_INF = -1e30
N_EVEN = (DEPTH + 1) // 2
N_ODD = DEPTH // 2
AB_SECTIONS = (Q_RANK, KV_RANK, ROPE_DIM, H_B * HEAD_DIM, H_B * HEAD_DIM, H_B * HEAD_DIM, H_B)
AB_WIDTH = sum(AB_SECTIONS)
AB_SPLITS = tuple(int(v) for v in np.cumsum(AB_SECTIONS)[:-1])
AB_OUT = H_A * V_DIM_A + H_B * HEAD_DIM
CD_SECTIONS = (H_C * HEAD_DIM,) * 3 + (H_D * HEAD_DIM,) * 3
CD_WIDTH = sum(CD_SECTIONS)
CD_SPLITS = tuple(int(v) for v in np.cumsum(CD_SECTIONS)[:-1])
CD_OUT = (H_C + H_D) * HEAD_DIM
STATE_KEYS = ('a_ckv', 'a_krope', 'b_k', 'b_v', 'b_logf', 'c_k', 'c_v', 'd_k', 'd_v', 'ffn_conv')

kernel_name = 'hybrid_streaming_encoder_step'


def rmsnorm(x, g):
    xf = x.astype(jnp.float32)
    y = xf * lax.rsqrt(jnp.mean(xf * xf, axis=-1, keepdims=True) + EPS)
    return (y * g.astype(jnp.float32)).astype(x.dtype)


def modulate(h, shift, scale):
    return h * (1.0 + scale[:, None, :]) + shift[:, None, :]


def rope(x, pos):
    half = x.shape[-1] // 2
    inv = ROPE_BASE ** (-jnp.arange(half, dtype=jnp.float32) / half)
    ang = pos.astype(jnp.float32)[:, None] * inv[None, :]
    cos = jnp.cos(ang)[None, :, None, :].astype(x.dtype)
    sin = jnp.sin(ang)[None, :, None, :].astype(x.dtype)
    x1, x2 = x[..., :half], x[..., half:]
    return jnp.concatenate([x1 * cos - x2 * sin, x1 * sin + x2 * cos], axis=-1)


def softmax_av(s, mask, v):
    p = jax.nn.softmax(jnp.where(mask, s, NEG_INF), axis=-1).astype(v.dtype)
    o = jnp.einsum('bhqk,bkhd->bqhd', p, v)
    return o.reshape(o.shape[0], o.shape[1], -1)


def mla_attend(q_args, k_args, qpos, kpos):
    q_nope, q_rope = q_args
    k_nope, k_rope, v = k_args
    s = (jnp.einsum('bqhd,bkhd->bhqk', q_nope, k_nope)
         + jnp.einsum('bqhr,bkr->bhqk', q_rope, k_rope)).astype(jnp.float32) * (NOPE_DIM + ROPE_DIM) ** -0.5
    mask = (kpos[None, :] // CHUNK) <= (qpos[:, None] // CHUNK)
    return softmax_av(s, mask, v)


def fox_attend(q_args, k_args, qpos, kpos):
    q, cum_q = q_args
    k, v, cum_k = k_args
    s = jnp.einsum('bqhd,bkhd->bhqk', q, k).astype(jnp.float32) * HEAD_DIM ** -0.5
    s = s + jnp.swapaxes(cum_q, 1, 2)[..., :, None] - jnp.swapaxes(cum_k, 1, 2)[..., None, :]
    mask = kpos[None, :] <= qpos[:, None]
    return softmax_av(s, mask, v)


def band_attend(q_args, k_args, qpos, kpos, rel_table):
    (q,) = q_args
    k, v = k_args
    s = jnp.einsum('bqhd,bkhd->bhqk', q, k).astype(jnp.float32) * HEAD_DIM ** -0.5
    rel = jnp.clip(qpos[:, None] - kpos[None, :], -REL_CLIP, REL_CLIP) + REL_CLIP
    s = s + jnp.transpose(rel_table[rel], (2, 0, 1))[None].astype(jnp.float32)
    qc, kc = qpos // CHUNK, kpos // CHUNK
    mask = (kc[None, :] <= qc[:, None]) & (kc[None, :] >= qc[:, None] - BAND_CHUNKS) & (kpos[None, :] >= 0)
    return softmax_av(s, mask, v)


def sb_attend(q_args, k_args, qpos, kpos):
    (q,) = q_args
    k, v = k_args
    z = jnp.einsum('bqhd,bkhd->bhqk', q, k).astype(jnp.float32) * HEAD_DIM ** -0.5
    mask = kpos[None, :] < qpos[:, None]
    log_keep = jnp.where(mask, jax.nn.log_sigmoid(-z), 0.0)
    after = lax.cumsum(log_keep, axis=3, reverse=True) - log_keep
    w = jnp.where(mask, jnp.exp(jax.nn.log_sigmoid(z) + after), 0.0).astype(v.dtype)
    o = jnp.einsum('bhqk,bkhd->bqhd', w, v)
    return o.reshape(o.shape[0], o.shape[1], -1)


def attend(fn, q_args, k_args, qpos, kpos):
    T = qpos.shape[0]
    if T <= Q_BLOCK:
        return fn(q_args, k_args, qpos, kpos)
    nb = T // Q_BLOCK

    def split(a):
        return jnp.moveaxis(a.reshape((a.shape[0], nb, Q_BLOCK) + a.shape[2:]), 1, 0)

    qs = tuple(split(a) for a in q_args)
    out = lax.map(lambda xs: fn(xs[0], k_args, xs[1], kpos), (qs, qpos.reshape(nb, Q_BLOCK)))
    return jnp.moveaxis(out, 0, 1).reshape(out.shape[1], T, out.shape[-1])


def band_sweep(q, k, v, rel_table):
    B, T = q.shape[0], q.shape[1]
    nc = T // CHUNK
    pad = jnp.zeros((B, BAND_ROWS) + k.shape[2:], k.dtype)
    kp = jnp.concatenate([pad, k], axis=1)
    vp = jnp.concatenate([pad.astype(v.dtype), v], axis=1)
    qs = jnp.moveaxis(q.reshape((B, nc, CHUNK) + q.shape[2:]), 1, 0)

    def one(xs):
        qc, ci = xs
        start = ci * CHUNK
        kb = lax.dynamic_slice_in_dim(kp, start, BAND_ROWS + CHUNK, axis=1)
        vb = lax.dynamic_slice_in_dim(vp, start, BAND_ROWS + CHUNK, axis=1)
        qpos = start + jnp.arange(CHUNK)
        kpos = start - BAND_ROWS + jnp.arange(BAND_ROWS + CHUNK)
        return band_attend((qc,), (kb, vb), qpos, kpos, rel_table)

    out = lax.map(one, (qs, jnp.arange(nc)))
    return jnp.moveaxis(out, 0, 1).reshape(B, T, out.shape[-1])


def mixer_ab(h, qpos, past, W, i):
    B, T, _ = h.shape
    c_q, c_kv, k_r, q_b, k_b, v_b, f_b = jnp.split(h @ W['w_in_ab'][i], AB_SPLITS, axis=-1)
    q_a = (rmsnorm(c_q, W['q_a_g'][i]) @ W['w_uq'][i]).reshape(B, T, H_A, NOPE_DIM + ROPE_DIM)
    q_nope, q_rope = q_a[..., :NOPE_DIM], rope(q_a[..., NOPE_DIM:], qpos)
    c_kv = rmsnorm(c_kv, W['kv_a_g'][i])
    k_r = rope(k_r[:, :, None, :], qpos)[:, :, 0, :]
    q_b, k_b, v_b = (a.reshape(B, T, H_B, HEAD_DIM) for a in (q_b, k_b, v_b))
    logf = jax.nn.log_sigmoid(f_b.astype(jnp.float32) + W['b_f'][i].astype(jnp.float32)).astype(h.dtype)
    if past is None:
        ckv_all, kr_all, kb_all, vb_all, logf_all, kpos = c_kv, k_r, k_b, v_b, logf, qpos
    else:
        ckv_all = jnp.concatenate([past['a_ckv'][i], c_kv], axis=1)
        kr_all = jnp.concatenate([past['a_krope'][i], k_r], axis=1)
        kb_all = jnp.concatenate([past['b_k'][i], k_b], axis=1)
        vb_all = jnp.concatenate([past['b_v'][i], v_b], axis=1)
        logf_all = jnp.concatenate([past['b_logf'][i], logf], axis=1)
        kpos = jnp.arange(PAST_LEN + T)
    kv = (ckv_all @ W['w_ukv'][i]).reshape(B, -1, H_A, NOPE_DIM + V_DIM_A)
    cum = jnp.cumsum(logf_all.astype(jnp.float32), axis=1)
    o_a = attend(mla_attend, (q_nope, q_rope), (kv[..., :NOPE_DIM], kr_all, kv[..., NOPE_DIM:]), qpos, kpos)
    o_b = attend(fox_attend, (q_b, cum[:, -T:]), (kb_all, vb_all, cum), qpos, kpos)
    o = jnp.concatenate([o_a, o_b], axis=-1) @ W['w_out_ab'][i]
    return o, (c_kv, k_r, k_b, v_b, logf)


def mixer_cd(h, qpos, past, W, i):
    B, T, _ = h.shape
    parts = jnp.split(h @ W['w_in_cd'][i], CD_SPLITS, axis=-1)
    q_c, k_c, v_c = (a.reshape(B, T, H_C, HEAD_DIM) for a in parts[:3])
    q_d, k_d, v_d = (a.reshape(B, T, H_D, HEAD_DIM) for a in parts[3:])
    rel = W['rel_bias_c'][i]
    if past is None:
        o_c = band_sweep(q_c, k_c, v_c, rel)
        keep = min(BAND_ROWS, T)
        c_rows = (k_c[:, T - keep:], v_c[:, T - keep:])
        kd_all, vd_all, kpos = k_d, v_d, qpos
    else:
        n_c = past['c_k'].shape[2]
        kc_all = jnp.concatenate([past['c_k'][i], k_c], axis=1)
        vc_all = jnp.concatenate([past['c_v'][i], v_c], axis=1)
        kpos_c = PAST_LEN - n_c + jnp.arange(n_c + T)
        o_c = band_attend((q_c,), (kc_all, vc_all), qpos, kpos_c, rel)
        c_rows = (k_c, v_c)
        kd_all = jnp.concatenate([past['d_k'][i], k_d], axis=1)
        vd_all = jnp.concatenate([past['d_v'][i], v_d], axis=1)
        kpos = jnp.arange(PAST_LEN + T)
    o_d = attend(sb_attend, (q_d,), (kd_all, vd_all), qpos, kpos)
    o = jnp.concatenate([o_c, o_d], axis=-1) @ W['w_out_cd'][i]
    return o, (c_rows[0], c_rows[1], k_d, v_d)


def conv_ffn(h, prev, w_gate, w_up, conv_w, conv_b, w_down):
    T = h.shape[1]
    g = h @ w_gate
    u = h @ w_up
    g_ext = jnp.concatenate([prev.astype(g.dtype), g], axis=1)
    gc = conv_b + sum(conv_w[j] * g_ext[:, j:j + T] for j in range(CONV_W))
    out = (jax.nn.silu(gc) * u) @ w_down
    return out, g_ext[:, T:]


def trunk(x, c, past, W):
    B, T, _ = x.shape
    qpos = (0 if past is None else PAST_LEN) + jnp.arange(T)
    cond = jax.nn.silu(c)
    new = {n: [] for n in STATE_KEYS}
    for l in range(DEPTH):
        sh_m, sc_m, g_m, sh_f, sc_f, g_f = jnp.split(cond @ W['ada_w'][l] + W['ada_b'][l], 6, axis=-1)
        h = modulate(rmsnorm(x, W['mix_pre_g'][l]), sh_m, sc_m)
        if l % 2 == 0:
            o, rows = mixer_ab(h, qpos, past, W, l // 2)
            names = STATE_KEYS[:5]
        else:
            o, rows = mixer_cd(h, qpos, past, W, l // 2)
            names = STATE_KEYS[5:9]
        for n, r in zip(names, rows):
            new[n].append(r)
        x = x + g_m[:, None, :] * rmsnorm(o, W['mix_post_g'][l])
        h = modulate(rmsnorm(x, W['ffn_pre_g'][l]), sh_f, sc_f)
        prev = jnp.zeros((B, CONV_W - 1, D_FF), x.dtype) if past is None else past['ffn_conv'][l]
        o, conv_rows = conv_ffn(h, prev, W['ffn_w_gate'][l], W['ffn_w_up'][l], W['ffn_conv_w'][l],
                                W['ffn_conv_b'][l], W['ffn_w_down'][l])
        new['ffn_conv'].append(conv_rows)
        x = x + g_f[:, None, :] * rmsnorm(o, W['ffn_post_g'][l])
    return x, {n: jnp.stack(v) for n, v in new.items()}


def setup_inputs(seed: int = 0) -> dict:
    key = jax.random.key(seed)
    ks = iter(jax.random.split(key, 64))

    def nrm(shape, scale=1.0):
        return scale * jax.random.normal(next(ks), shape, jnp.float32)

    def gain(shape):
        return 1.0 + nrm(shape, 0.05)

    c_rows = min(BAND_ROWS, PAST_LEN)
    hd = HEAD_DIM
    return {
        'x_prompt': nrm((BATCH, SEQ, D_MODEL)),
        'x_sample': nrm((DEC_BATCH, DEC_SEQ, D_MODEL)),
        'c_prompt': nrm((BATCH, D_MODEL)),
        'c_sample': nrm((DEC_BATCH, D_MODEL)),
        'cache_a_ckv': nrm((N_EVEN, DEC_BATCH, PAST_LEN, KV_RANK)),
        'cache_a_krope': nrm((N_EVEN, DEC_BATCH, PAST_LEN, ROPE_DIM)),
        'cache_b_k': nrm((N_EVEN, DEC_BATCH, PAST_LEN, H_B, hd)),
        'cache_b_v': nrm((N_EVEN, DEC_BATCH, PAST_LEN, H_B, hd)),
        'cache_b_logf': jax.nn.log_sigmoid(3.0 + nrm((N_EVEN, DEC_BATCH, PAST_LEN, H_B))),
        'cache_c_k': nrm((N_ODD, DEC_BATCH, c_rows, H_C, hd)),
        'cache_c_v': nrm((N_ODD, DEC_BATCH, c_rows, H_C, hd)),
        'cache_d_k': nrm((N_ODD, DEC_BATCH, PAST_LEN, H_D, hd)),
        'cache_d_v': nrm((N_ODD, DEC_BATCH, PAST_LEN, H_D, hd)),
        'state_ffn_conv': nrm((DEPTH, DEC_BATCH, CONV_W - 1, D_FF), 0.5),
        'ada_w': nrm((DEPTH, D_MODEL, 6 * D_MODEL), 0.5 * D_MODEL ** -0.5),
        'ada_b': nrm((DEPTH, 6 * D_MODEL), 0.02),
        'mix_pre_g': gain((DEPTH, D_MODEL)),
        'mix_post_g': gain((DEPTH, D_MODEL)),
        'ffn_pre_g': gain((DEPTH, D_MODEL)),
        'ffn_post_g': gain((DEPTH, D_MODEL)),
        'w_in_ab': nrm((N_EVEN, D_MODEL, AB_WIDTH), D_MODEL ** -0.5),
        'b_f': jnp.linspace(1.0, 4.0, H_B, dtype=jnp.float32)[None, :] + nrm((N_EVEN, H_B), 0.1),
        'q_a_g': gain((N_EVEN, Q_RANK)),
        'kv_a_g': gain((N_EVEN, KV_RANK)),
        'w_uq': nrm((N_EVEN, Q_RANK, H_A * (NOPE_DIM + ROPE_DIM)), Q_RANK ** -0.5),
        'w_ukv': nrm((N_EVEN, KV_RANK, H_A * (NOPE_DIM + V_DIM_A)), KV_RANK ** -0.5),
        'w_out_ab': nrm((N_EVEN, AB_OUT, D_MODEL), AB_OUT ** -0.5),
        'w_in_cd': nrm((N_ODD, D_MODEL, CD_WIDTH), D_MODEL ** -0.5),
        'rel_bias_c': nrm((N_ODD, 2 * REL_CLIP + 1, H_C), 0.5),
        'w_out_cd': nrm((N_ODD, CD_OUT, D_MODEL), CD_OUT ** -0.5),
        'ffn_w_gate': nrm((DEPTH, D_MODEL, D_FF), D_MODEL ** -0.5),
        'ffn_w_up': nrm((DEPTH, D_MODEL, D_FF), D_MODEL ** -0.5),
        'ffn_conv_w': nrm((DEPTH, CONV_W, D_FF), CONV_W ** -0.5),
        'ffn_conv_b': nrm((DEPTH, D_FF), 0.02),
        'ffn_w_down': nrm((DEPTH, D_FF, D_MODEL), D_FF ** -0.5),
    }


def reference(x_prompt, x_sample, c_prompt, c_sample, cache_a_ckv, cache_a_krope, cache_b_k, cache_b_v,
              cache_b_logf, cache_c_k, cache_c_v, cache_d_k, cache_d_v, state_ffn_conv, ada_w, ada_b,
              mix_pre_g, mix_post_g, ffn_pre_g, ffn_post_g, w_in_ab, b_f, q_a_g, kv_a_g, w_uq, w_ukv,
              w_out_ab, w_in_cd, rel_bias_c, w_out_cd, ffn_w_gate, ffn_w_up, ffn_conv_w, ffn_conv_b,
              ffn_w_down):
    W = {'ada_w': ada_w, 'ada_b': ada_b, 'mix_pre_g': mix_pre_g, 'mix_post_g': mix_post_g,
         'ffn_pre_g': ffn_pre_g, 'ffn_post_g': ffn_post_g, 'w_in_ab': w_in_ab, 'b_f': b_f,
         'q_a_g': q_a_g, 'kv_a_g': kv_a_g, 'w_uq': w_uq, 'w_ukv': w_ukv, 'w_out_ab': w_out_ab,
         'w_in_cd': w_in_cd, 'rel_bias_c': rel_bias_c, 'w_out_cd': w_out_cd, 'ffn_w_gate': ffn_w_gate,
         'ffn_w_up': ffn_w_up, 'ffn_conv_w': ffn_conv_w, 'ffn_conv_b': ffn_conv_b, 'ffn_w_down': ffn_w_down}
    past = {'a_ckv': cache_a_ckv, 'a_krope': cache_a_krope, 'b_k': cache_b_k, 'b_v': cache_b_v,
            'b_logf': cache_b_logf, 'c_k': cache_c_k, 'c_v': cache_c_v, 'd_k': cache_d_k, 'd_v': cache_d_v,
            'ffn_conv': state_ffn_conv}
    y_prompt, sp = trunk(x_prompt, c_prompt, None, W)
    y_sample, ss = trunk(x_sample, c_sample, past, W)
    return (y_prompt, y_sample,
            sp['a_ckv'], ss['a_ckv'], sp['a_krope'], ss['a_krope'],
            sp['b_k'], ss['b_k'], sp['b_v'], ss['b_v'], sp['b_logf'], ss['b_logf'],
            sp['c_k'], ss['c_k'], sp['c_v'], ss['c_v'],
            sp['d_k'], ss['d_k'], sp['d_v'], ss['d_v'],
            sp['ffn_conv'], ss['ffn_conv'])
```

```python
import numpy as np
import ml_dtypes
import concourse.bass as bass
import concourse.mybir as mybir
from concourse.bass_utils import run_bass_kernel_spmd

F32 = mybir.dt.float32
BF16 = mybir.dt.bfloat16
AF = mybir.ActivationFunctionType
ALU = mybir.AluOpType
AX = mybir.AxisListType
NPBF = ml_dtypes.bfloat16


class Buf:
    __slots__ = ("w", "r", "excl", "name")

    def __init__(self, name="", excl=False):
        self.w = {}
        self.r = {}
        self.excl = excl
        self.name = name


class KB:
    ENG = ("pe", "act", "dve", "pool", "sp")
    NDS = 8

    def __init__(self, nc):
        self.nc = nc
        self.ops = {e: [] for e in self.ENG}
        self.seen = {e: {} for e in self.ENG}
        self.dma_n = {e: 0 for e in self.ENG}
        self.phase = "setup"
        self.ophase = {e: [] for e in self.ENG}

    @staticmethod
    def _kv(t):
        if t[0] == "E":
            return ("E", t[1]), t[2]
        return ("D", t[1], t[2]), t[3]

    def _deps(self, eng, reads, writes):
        toks = []
        for b in reads:
            toks.extend(b.w.values())
            if b.excl:
                for k, t in b.r.items():
                    if k != ("E", eng):
                        toks.append(t)
        for b in writes:
            toks.extend(b.w.values())
            toks.extend(b.r.values())
        best = {}
        for t in toks:
            k, v = self._kv(t)
            if k == ("E", "pe") and eng == "pe":
                continue
            if self.seen[eng].get(k, -1) >= v:
                continue
            if k not in best or best[k][1] < v:
                best[k] = (t, v)
        waits = []
        for k, (t, v) in best.items():
            self.seen[eng][k] = v
            waits.append(t)
        return waits

    def op(self, eng, fn, reads=(), writes=()):
        waits = self._deps(eng, reads, writes)
        idx = len(self.ops[eng])
        tok = ("E", eng, idx)
        self.ops[eng].append((fn, waits, None))
        self.ophase[eng].append(self.phase)
        for b in reads:
            b.r[("E", eng)] = tok
        for b in writes:
            b.w[("E", eng)] = tok
            b.r = {}
        return tok

    def dma(self, eng, out, in_, reads=(), writes=(), **kw):
        n = self.dma_n[eng]
        self.dma_n[eng] += 1
        slot = n % self.NDS
        val = 16 * (n // self.NDS + 1)
        waits = self._deps(eng, reads, writes)
        k = ("D", eng, slot)
        if n >= self.NDS and self.seen[eng].get(k, -1) < val - 16:
            waits.append(("D", eng, slot, val - 16))
            self.seen[eng][k] = val - 16
        tok = ("D", eng, slot, val)
        self.ops[eng].append((lambda h: h.dma_start(out=out, in_=in_, **kw), waits, slot))
        self.ophase[eng].append(self.phase)
        for b in reads:
            b.r[k] = tok
        for b in writes:
            b.w[k] = tok
            b.r = {}
        return tok

    def finish(self):
        self.barrier()

    def emit(self):
        nc = self.nc
        tg = {e: set() for e in self.ENG}
        for e in self.ENG:
            for (_, waits, _) in self.ops[e]:
                for t in waits:
                    if t[0] == "E":
                        tg[t[1]].add(t[2])
        rank = {e: {idx: i + 1 for i, idx in enumerate(sorted(tg[e]))} for e in self.ENG}
        esem = {e: nc.alloc_semaphore("es_" + e) for e in self.ENG}
        dsem = {e: [nc.alloc_semaphore("ds_%s%d" % (e, i)) for i in range(self.NDS)]
                for e in self.ENG if self.dma_n[e] > 0}

        import os
        scoped = bool(os.environ.get("KSCOPE"))

        def run(e, h):
            cur = [None, None]
            for idx, (fn, waits, d) in enumerate(self.ops[e]):
                if scoped and self.ophase[e][idx] != cur[0]:
                    if cur[1] is not None:
                        cur[1].__exit__(None, None, None)
                    cur[0] = self.ophase[e][idx]
                    cur[1] = nc.named_scope(cur[0])
                    cur[1].__enter__()
                for t in waits:
                    if t[0] == "E":
                        h.wait_ge(esem[t[1]], rank[t[1]][t[2]])
                    else:
                        h.wait_ge(dsem[t[1]][t[2]], t[3])
                if fn is None:
                    continue
                ins = fn(h)
                if d is not None:
                    ins.then_inc(dsem[e][d], 16)
                elif idx in rank[e]:
                    ins.then_inc(esem[e], 1)
            if cur[1] is not None:
                cur[1].__exit__(None, None, None)

        with nc.Block() as block:
            @block.tensor
            def _(h):
                run("pe", h)

            @block.scalar
            def _(h):
                run("act", h)

            @block.vector
            def _(h):
                run("dve", h)

            @block.gpsimd
            def _(h):
                run("pool", h)

            @block.sync
            def _(h):
                run("sp", h)


D = 1024
DFF = 2816
NFC = DFF // 128
PAST = 1024
CPAST = 512
EPS = 1e-6
NEG = -30000.0


class Arena:
    def __init__(self, t, n):
        self.t, self.n, self.off = t, n, 0

    def reset(self):
        self.off = 0

    def alloc(self, shape, dt, name=""):
        per = int(np.prod(shape[1:]))
        nb = per * (2 if dt == F32 else 1)
        nb = (nb + 15) // 16 * 16
        assert self.off + nb <= self.n, ("arena overflow", name, self.off, nb, self.n)
        v = self.t[:, self.off:self.off + nb]
        self.off += nb
        v = v.bitcast(F32)[:, 0:per] if dt == F32 else v[:, 0:per]
        if len(shape) == 3:
            v = v.rearrange("p (a b) -> p a b", b=shape[2])
        elif len(shape) == 4:
            v = v.rearrange("p (a b c) -> p a b c", b=shape[2], c=shape[3])
        if shape[0] < 128:
            v = v[0:shape[0]]
        return v, Buf(name)


class Rot:
    def __init__(self, items):
        self.items, self.i = items, 0

    def next(self):
        it = self.items[self.i % len(self.items)]
        self.i += 1
        return it


def _barrier(kb):
    toks = []
    for e in kb.ENG:
        n = kb.dma_n[e]
        for slot in range(min(n, kb.NDS)):
            last = ((n - 1 - slot) // kb.NDS) * kb.NDS + slot
            toks.append(("D", e, slot, 16 * (last // kb.NDS + 1)))
        if kb.ops[e]:
            j = len(kb.ops[e]) - 1
            while j >= 0 and (kb.ops[e][j][0] is None or kb.ops[e][j][2] is not None):
                j -= 1
            if j >= 0:
                toks.append(("E", e, j))
    for e in kb.ENG:
        waits = []
        for t in toks:
            k, v = kb._kv(t)
            if k == ("E", e):
                continue
            if kb.seen[e].get(k, -1) >= v:
                continue
            kb.seen[e][k] = v
            waits.append(t)
        kb.ops[e].append((None, waits, None))
        kb.ophase[e].append(kb.phase)


KB.barrier = _barrier


def _weight_layout(w, kc):
    n = w.shape[1]
    return np.ascontiguousarray(w.reshape(kc, 128, n).transpose(1, 0, 2).reshape(128, kc * n))


def _fm(v):
    return np.ascontiguousarray(v.reshape(-1, 128).T)


def _consts(T):
    NT = T // 128
    NTT = NT + 1
    TOK = T + 64
    c = {}
    k = np.arange(128)[:, None]
    q = np.arange(128)[None, :]
    c["ident_bf"] = np.eye(128, dtype=np.float32).astype(NPBF)
    c["ident_f"] = np.eye(128, dtype=np.float32)
    c["uf"] = (k <= q).astype(np.float32)
    c["onesf"] = np.ones((128, 128), np.float32)
    c["nu"] = (-(k >= q).astype(np.float32)).astype(NPBF)
    c["nl"] = (-(k < q).astype(np.float32)).astype(NPBF)
    c["nones"] = (-np.ones((128, 128), np.float32)).astype(NPBF)
    c["m_mla"] = np.where((k >= 64) & (q < 64), NEG, 0.0).astype(np.float32).astype(NPBF)
    c["m_fox"] = np.where(k > q, NEG, 0.0).astype(np.float32).astype(NPBF)
    c["m_sb"] = np.where(k >= q, NEG, 0.0).astype(np.float32).astype(NPBF)
    selh = np.zeros((128, 8, 128), np.float32)
    for h in range(8):
        selh[h, h, :] = 1.0
    c["selh"] = selh.astype(NPBF)
    pos_tok = np.zeros(TOK, np.float64)
    pos_tok[:T] = np.arange(T)
    for j in range(64):
        pos_tok[T + j] = 1024 + (j % 32)
    half = 16
    inv = 10000.0 ** (-np.arange(half, dtype=np.float32) / half)
    ang = pos_tok.astype(np.float32)[:, None] * inv[None, :]
    cos = np.cos(ang).astype(np.float32)
    sin = np.sin(ang).astype(np.float32)
    ct = np.zeros((128, NTT, 16), np.float32)
    st = np.zeros((128, NTT, 16), np.float32)
    for i in range(NT):
        ct[:, i] = cos[i * 128:(i + 1) * 128]
        st[:, i] = sin[i * 128:(i + 1) * 128]
    ct[:64, NT] = cos[T:T + 64]
    st[:64, NT] = sin[T:T + 64]
    c["ct"] = ct
    c["st"] = st
    sc = np.float32(96.0 ** -0.5)
    c["cos2"] = np.ascontiguousarray(np.concatenate([cos.T, cos.T], 0) * sc)
    c["sin2"] = np.ascontiguousarray(np.concatenate([-sin.T, sin.T], 0) * sc)
    md = np.zeros((5, 128, 128), np.float32)
    md[0] = np.where((k >= 64) & (q < 64), NEG, 0.0)
    md[4] = np.where((k < 64) & (q >= 64), NEG, 0.0)
    c["md"] = md
    return c


def _shared_inputs(inp):
    s = {}
    w = inp["w_in_ab"][0]
    perm = np.concatenate([np.arange(0, 416), np.arange(1952, 1960), np.arange(416, 1952)])
    s["win0"] = _weight_layout(w[:, perm], 8)
    wuq = inp["w_uq"][0]
    swc = []
    for h in range(8):
        swc += list(range(h * 96 + 80, h * 96 + 96)) + list(range(h * 96 + 64, h * 96 + 80))
    s["wuq"] = _weight_layout(np.concatenate([wuq, wuq[:, swc]], 1), 2)
    wukv = inp["w_ukv"][0]
    nope = np.concatenate([np.arange(h * 128, h * 128 + 64) for h in range(8)])
    vv = np.concatenate([np.arange(h * 128 + 64, h * 128 + 128) for h in range(8)])
    s["wukv"] = np.ascontiguousarray(wukv[:, np.concatenate([nope, vv])])
    s["wout0"] = _weight_layout(inp["w_out_ab"][0], 8)
    s["wout1"] = _weight_layout(inp["w_out_cd"][0], 8)
    s["win1"] = _weight_layout(inp["w_in_cd"][0], 8)
    for l in range(2):
        s["wg%d" % l] = _weight_layout(inp["ffn_w_gate"][l], 8)
        s["wu%d" % l] = _weight_layout(inp["ffn_w_up"][l], 8)
        s["wd%d" % l] = _weight_layout(inp["ffn_w_down"][l], 22)
    adaw = np.zeros((2, 6, 128, 8 * 1024), np.float32)
    adabf = np.zeros((2, 6, 128, 8), np.float32)
    for l in range(2):
        for kd in range(6):
            adaw[l, kd] = _weight_layout(inp["ada_w"][l][:, kd * 1024:(kd + 1) * 1024], 8)
            adabf[l, kd] = _fm(inp["ada_b"][l][kd * 1024:(kd + 1) * 1024])
    s["adaw"] = adaw
    s["adabf"] = adabf
    s["adab"] = np.ascontiguousarray(inp["ada_b"])
    s["gpre_fm"] = np.stack([np.stack([_fm(inp["mix_pre_g"][l]), _fm(inp["ffn_pre_g"][l])]) for l in range(2)])
    s["gpost"] = np.stack([np.stack([inp["mix_post_g"][l], inp["ffn_post_g"][l]]) for l in range(2)])
    s["b_f"] = np.ascontiguousarray(inp["b_f"][0])
    s["q_a_g"] = np.ascontiguousarray(inp["q_a_g"][0])
    s["kv_a_g"] = np.ascontiguousarray(inp["kv_a_g"][0])
    cw = np.zeros((2, 128, NFC, 3), np.float32)
    cb = np.zeros((2, 128, NFC), np.float32)
    for l in range(2):
        cw[l] = inp["ffn_conv_w"][l].T.reshape(NFC, 128, 3).transpose(1, 0, 2)
        cb[l] = inp["ffn_conv_b"][l].reshape(NFC, 128).T
    s["convw"] = cw
    s["convb"] = cb
    tab = inp["rel_bias_c"][0]
    kk = np.arange(128)[:, None]
    qq = np.arange(128)[None, :]
    bd = np.zeros((8, 5, 128, 128), np.float32)
    for d in range(5):
        idx = np.clip(128 * d + qq - kk, -128, 128) + 128
        for h in range(8):
            bd[h, d] = tab[idx, h]
    s["bd"] = bd
    return s


def _core_inputs(inp, b, T):
    m = {}
    m["xp"] = np.ascontiguousarray(inp["x_prompt"][b])
    xs = np.zeros((64, D), np.float32)
    xs[0:16] = inp["x_sample"][2 * b]
    xs[32:48] = inp["x_sample"][2 * b + 1]
    m["xs"] = xs
    cc = np.stack([inp["c_prompt"][b], inp["c_sample"][2 * b], inp["c_sample"][2 * b + 1]], 0)
    m["ct3"] = np.ascontiguousarray(cc.reshape(3, 8, 128).transpose(2, 1, 0))
    sl = slice(2 * b, 2 * b + 2)
    m["ca_ckv"] = np.ascontiguousarray(inp["cache_a_ckv"][0, sl])
    m["ca_kr"] = np.ascontiguousarray(inp["cache_a_krope"][0, sl])
    m["cb_k"] = np.ascontiguousarray(inp["cache_b_k"][0, sl].reshape(2, PAST, 512))
    m["cb_v"] = np.ascontiguousarray(inp["cache_b_v"][0, sl].reshape(2, PAST, 512))
    m["cb_lf"] = np.ascontiguousarray(inp["cache_b_logf"][0, sl])
    m["cc_k"] = np.ascontiguousarray(inp["cache_c_k"][0, sl].reshape(2, CPAST, 512))
    m["cc_v"] = np.ascontiguousarray(inp["cache_c_v"][0, sl].reshape(2, CPAST, 512))
    m["cd_k"] = np.ascontiguousarray(inp["cache_d_k"][0, sl].reshape(2, PAST, 512))
    m["cd_v"] = np.ascontiguousarray(inp["cache_d_v"][0, sl].reshape(2, PAST, 512))
    m["st_conv"] = np.ascontiguousarray(inp["state_ffn_conv"][:, sl])
    return m


class Prog:
    def __init__(self, T, nlayers=2, stop=None):
        self.T = T
        self.NT = T // 128
        self.NTT = self.NT + 1
        self.TOK = T + 64
        self.TOKK = self.TOK + 2 * PAST
        self.VROWS = self.TOK + 2 * PAST
        self.nlayers = nlayers
        self.stop = stop
        self.nc = bass.Bass("TRN2", target_bir_lowering=False)
        self.kb = KB(self.nc)
        self.dr = {}
        self.shapes = {}

    def din(self, name, arr):
        dt = BF16 if arr.dtype == NPBF else F32
        self.dr[name] = self.nc.dram_tensor(name, list(arr.shape), dt, kind="ExternalInput").ap()
        return self.dr[name]

    def dout(self, name, shape):
        self.dr[name] = self.nc.dram_tensor(name, list(shape), F32, kind="ExternalOutput").ap()
        self.shapes[name] = tuple(shape)
        return self.dr[name]

    def dscr(self, name, shape, dt):
        self.dr[name] = self.nc.dram_tensor(name, list(shape), dt, kind="Internal").ap()
        return self.dr[name]

    def act(self, out, in_, func, r, w, **kw):
        return self.kb.op("act", lambda h: h.activation(out=out, in_=in_, func=func, **kw), r, w)

    def mm(self, out, lhsT, rhs, start, stop, r, w, **kw):
        return self.kb.op("pe", lambda h: h.matmul(out, lhsT=lhsT, rhs=rhs, start=start, stop=stop, **kw), r, w)

    def tr(self, out, in_, ident, r, w):
        return self.kb.op("pe", lambda h: h.matmul(out, lhsT=in_, rhs=ident, start=True, stop=True, is_transpose=True), r, w)

    def tt(self, eng, out, in0, in1, op, r, w):
        return self.kb.op(eng, lambda h: h.tensor_tensor(out=out, in0=in0, in1=in1, op=op), r, w)

    def ts(self, eng, out, in0, s1, s2, op0, op1, r, w):
        if s2 is None:
            return self.kb.op(eng, lambda h: h.tensor_scalar(out=out, in0=in0, scalar1=s1, scalar2=None, op0=op0), r, w)
        return self.kb.op(eng, lambda h: h.tensor_scalar(out=out, in0=in0, scalar1=s1, scalar2=s2, op0=op0, op1=op1), r, w)

    def stt(self, eng, out, in0, scalar, in1, op0, op1, r, w):
        return self.kb.op(eng, lambda h: h.scalar_tensor_tensor(out=out, in0=in0, scalar=scalar, in1=in1,
                                                                op0=op0, op1=op1), r, w)

    def cp(self, eng, out, in_, r, w):
        if eng == "act":
            return self.act(out, in_, AF.Copy, r, w)
        return self.kb.op(eng, lambda h: h.tensor_copy(out=out, in_=in_), r, w)

    def memset(self, eng, ap, val, w):
        return self.kb.op(eng, lambda h: h.memset(ap, val), (), w)

    def dma(self, out, in_, r=(), w=(), **kw):
        return self.kb.dma("sp", out, in_, r, w, **kw)

    def setup(self, consts, shared, core):
        nc = self.nc
        T, NT, NTT, TOK = self.T, self.NT, self.NTT, self.TOK
        for k, v in list(consts.items()) + list(shared.items()) + list(core.items()):
            self.din(k, v)
        o = self.dout
        o("y_p", [T, D]); o("y_s", [64, D])
        o("a_ckv_p", [T, 128]); o("a_ckv_s", [64, 128])
        o("a_kr_p", [T, 32]); o("a_kr_s", [64, 32])
        o("b_k_p", [T, 512]); o("b_k_s", [64, 512])
        o("b_v_p", [T, 512]); o("b_v_s", [64, 512])
        o("b_lf_p", [T, 8]); o("b_lf_s", [64, 8])
        o("c_k_p", [T, 512]); o("c_k_s", [64, 512])
        o("c_v_p", [T, 512]); o("c_v_s", [64, 512])
        o("d_k_p", [T, 512]); o("d_k_s", [64, 512])
        o("d_v_p", [T, 512]); o("d_v_s", [64, 512])
        o("conv_p", [2, 128, NFC, 2]); o("conv_s", [2, 2, 128, NFC, 2])
        s = self.dscr
        s("x1", [TOK, D], F32); s("x2", [TOK, D], F32)
        s("qs_a", [8, 96, TOK], BF16); s("ks_an", [512, self.TOKK], BF16); s("kr", [32, self.TOKK], BF16)
        s("vs_a", [self.VROWS, 8 * 65], BF16)
        s("qs_b", [512, TOK], BF16); s("ks_b", [512, self.TOKK], BF16); s("vs_b", [self.VROWS, 8 * 65], BF16)
        s("os", [TOK, D], BF16)
        s("qs_c", [512, TOK], BF16); s("ks_c", [512, self.TOKK], BF16)
        self.PS = [nc.alloc_psum_tensor("ps%d" % i, [128, 512], F32) for i in range(8)]
        self.PB = [Buf("ps%d" % i, excl=True) for i in range(8)]
        self.pers = {}

        def pt(name, shape, dt, src=None):
            t = nc.alloc_sbuf_tensor("sb_" + name, list(shape), dt)
            b = Buf(name)
            self.pers[name] = (t, b)
            if src is not None:
                self.dma(t[:], src, w=[b])
            return t, b
        d = self.dr
        pt("ident_bf", [128, 128], BF16, d["ident_bf"]); pt("ident_f", [128, 128], F32, d["ident_f"])
        pt("uf", [128, 128], F32, d["uf"]); pt("onesf", [128, 128], F32, d["onesf"])
        pt("nu", [128, 128], BF16, d["nu"]); pt("nl", [128, 128], BF16, d["nl"]); pt("nones", [128, 128], BF16, d["nones"])
        pt("m_mla", [128, 128], BF16, d["m_mla"]); pt("m_fox", [128, 128], BF16, d["m_fox"]); pt("m_sb", [128, 128], BF16, d["m_sb"])
        pt("selh", [128, 8, 128], BF16, d["selh"])
        pt("ct3", [128, 8, 3], F32, d["ct3"])
        pt("condT", [128, 8, 3], BF16)
        pt("ce_p", [128, 8, 128], BF16); pt("ce_s", [128, 8, 64], BF16)
        pt("lall", [128, NTT, 8], F32)
        pt("lpast", [128, 2, 8, 8], F32)
        pt("ncum", [128, NTT, 8], F32)
        pt("ncump", [128, 2, 8, 8], F32)
        pt("cumt", [128, TOK], BF16)
        pt("halo", [128, NFC, 3, 2], F32)
        pt("small", [128, 64], F32)
        na = (nc.sbuf_bytes_remaining - 2048) // 2
        na = na // 16 * 16
        self.arena_t = nc.alloc_sbuf_tensor("arena", [128, na], BF16)
        self.ar = Arena(self.arena_t, na)
        ct3, bct3 = self.pers["ct3"]; condT, bcond = self.pers["condT"]
        cep, bcep = self.pers["ce_p"]; ces, bces = self.pers["ce_s"]
        self.act(condT[:], ct3[:], AF.Silu, [bct3], [bcond])
        self.memset("pool", ces[:], 0.0, [bces])
        for kc in range(8):
            self.cp("dve", cep[:, kc, :], condT[:, kc, 0:1].to_broadcast([128, 128]), [bcond], [bcep])
            self.cp("dve", ces[:, kc, 0:16], condT[:, kc, 1:2].to_broadcast([128, 16]), [bcond, bces], [bces])
            self.cp("dve", ces[:, kc, 32:48], condT[:, kc, 2:3].to_broadcast([128, 16]), [bcond, bces], [bces])
        self.kb.barrier()

    def P(self, name):
        return self.pers[name]

    def load_w(self, dst, bdst, src, ncols, stg):
        c0 = 0
        while c0 < ncols:
            n = min(2048, ncols - c0)
            s_ap, s_b = stg.next()
            self.dma(s_ap[:, 0:n], src[:, c0:c0 + n], w=[s_b])
            self._cast_i = getattr(self, "_cast_i", 0) + 1
            self.cp("dve" if self._cast_i % 2 else "act", dst[:, c0:c0 + n], s_ap[:, 0:n], [s_b], [bdst])
            c0 += n

    def mod_fm(self, l, kinds, stg, wbuf, out_tiles):
        condT, bcond = self.P("condT")
        w_ap, w_b = wbuf
        for kd, (o_ap, o_b) in zip(kinds, out_tiles):
            self.load_w(w_ap, w_b, self.dr["adaw"][l, kd], 8192, stg)
            bias_ap, bias_b = self.ar_small_fm
            self.dma(bias_ap, self.dr["adabf"][l, kd], w=[bias_b])
            ps, pb = self.PS[7], self.PB[7]
            for nck in range(8):
                for kc in range(8):
                    self.mm(ps[:, nck * 4:nck * 4 + 3], w_ap[:, kc * 1024 + nck * 128: kc * 1024 + nck * 128 + 128],
                            condT[:, kc, :], kc == 0, kc == 7, [w_b, bcond], [pb])
            for nck in range(8):
                self.ts("dve", o_ap[:, nck, :], ps[:, nck * 4:nck * 4 + 3], bias_ap[:, nck:nck + 1], None,
                        ALU.add, None, [pb, bias_b], [o_b])

    def mod_bc(self, l, kd, stg, wbuf, gp_ap, gp_b, gs_ap, gs_b, gain_row):
        cep, bcep = self.P("ce_p"); ces, bces = self.P("ce_s")
        w_ap, w_b = wbuf
        self.load_w(w_ap, w_b, self.dr["adaw"][l, kd], 8192, stg)
        t1, b1 = self.ar_bc1
        t2, b2 = self.ar_bc2
        self.dma(t1, self.dr["adab"][l, kd * 1024:(kd + 1) * 1024].partition_broadcast(128), w=[b1])
        self.dma(t2, gain_row.partition_broadcast(128), w=[b2])
        for (ce, bce, rows, g_ap, g_b) in ((cep, bcep, 128, gp_ap, gp_b), (ces, bces, 64, gs_ap, gs_b)):
            for half in range(2):
                ps, pb = self.PS[6 + half], self.PB[6 + half]
                for kc in range(8):
                    self.mm(ps[0:rows, :], ce[:, kc, 0:rows], w_ap[:, kc * 1024 + half * 512: kc * 1024 + half * 512 + 512],
                            kc == 0, kc == 7, [w_b, bce], [pb])
                sl = slice(half * 512, half * 512 + 512)
                self.tt("dve", g_ap[0:rows, sl], ps[0:rows, :], t1[0:rows, sl], ALU.add, [pb, b1], [g_b])
                self.tt("pool", g_ap[0:rows, sl], g_ap[0:rows, sl], t2[0:rows, sl], ALU.mult, [g_b, b2], [g_b])

    def tiles(self):
        return [(i, 128, i * 128) for i in range(self.NT)] + [(self.NT, 64, self.T)]

    def blocks(self):
        bl = [[(i, 128, i * 128) for i in range(b * 4, b * 4 + 4)] for b in range(self.NT // 4)]
        bl.append([(self.NT, 64, self.T)])
        return bl

    def xrows(self, src_p, src_s, i, rows):
        return src_p[i * 128:(i + 1) * 128, :] if i < self.NT else src_s[0:64, :]

    def rstd(self, ss_ap, ss_b, n, col, rows, inv_n):
        sm = ss_ap
        self.act(sm[0:rows, col + n:col + 2 * n], sm[0:rows, col:col + n], AF.Ln, [ss_b], [ss_b], scale=inv_n, bias=self.eps_ap[0:rows, :])
        self.act(sm[0:rows, col + 2 * n:col + 3 * n], sm[0:rows, col + n:col + 2 * n], AF.Exp, [ss_b], [ss_b], scale=-0.5)
        return sm[0:rows, col + 2 * n:col + 3 * n]

    def norm_tile(self, x_ap, x_b, rows, htb, htb_b, c0, afm, bfm, fm_b, is_sample, rs):
        idb, idb_b = self.P("ident_bf")
        sm, sm_b = rs["sm"].next()
        jk, jk_b = rs["jk"].next()
        xn, xn_b = rs["xn"].next()
        self.act(jk[0:rows, :], x_ap[0:rows, :], AF.Square, [x_b], [jk_b, sm_b], accum_out=sm[0:rows, 0:1])
        r = self.rstd(sm, sm_b, 1, 0, rows, 1.0 / D)
        self.ts("dve", xn[0:rows, :], x_ap[0:rows, :], r, None, ALU.mult, None, [x_b, sm_b], [xn_b])
        ps, pb = rs["pst"].next()
        pbf = ps[:].bitcast(BF16)
        for c in range(8):
            self.tr(pbf[:, c * 128:c * 128 + rows], xn[0:rows, c * 128:(c + 1) * 128], idb[0:rows, 0:rows], [xn_b, idb_b], [pb])
        segs = [(0, 128, 0)] if not is_sample else [(0, 32, 1), (32, 32, 2)]
        for c in range(8):
            for (cs, n, sq) in segs:
                o_ = htb[:, c, c0 + cs:c0 + cs + n]
                i_ = pbf[:, c * 128 + cs:c * 128 + cs + n]
                if c % 2 == 0:
                    self.act(o_, i_, AF.Identity, [pb, fm_b], [htb_b], scale=afm[:, c, sq:sq + 1], bias=bfm[:, c, sq:sq + 1])
                else:
                    self.ts("dve", o_, i_, afm[:, c, sq:sq + 1], bfm[:, c, sq:sq + 1], ALU.mult, ALU.add, [pb, fm_b], [htb_b])

    def pre_common(self, l):
        ar = self.ar
        ar.reset()
        A = lambda shape, dt, name="": ar.alloc(shape, dt, name)
        rs = {}
        rs["stg"] = Rot([A([128, 2048], F32, "stg%d" % i) for i in range(2)])
        self.ar_small_fm = A([128, 8], F32, "adabfm")
        eps, eps_b = A([128, 1], F32, "eps")
        self.memset("pool", eps, EPS, [eps_b])
        self.eps_ap = eps
        afm = A([128, 8, 3], F32, "afm")
        bfm = A([128, 8, 3], F32, "bfm")
        gfm = A([128, 8], F32, "gfm")
        wada = A([128, 8192], BF16, "wada")
        self.mod_fm(l, [1, 0], rs["stg"], wada, [afm, bfm])
        self.dma(gfm[0], self.dr["gpre_fm"][l, 0], w=[gfm[1]])
        for sq in range(3):
            self.stt("dve", afm[0][:, :, sq], afm[0][:, :, sq], 1.0, gfm[0], ALU.add, ALU.mult, [afm[1], gfm[1]], [afm[1]])
        fm_b = Buf("fmb")
        self.cp("dve", bfm[0][:, 0, 0:1], bfm[0][:, 0, 0:1], [afm[1], bfm[1]], [fm_b, bfm[1]])
        rs["afm"], rs["bfm"], rs["fm_b"] = afm[0], bfm[0], fm_b
        rs["x"] = Rot([A([128, D], F32, "x%d" % i) for i in range(3)])
        rs["sm"] = Rot([A([128, 16], F32, "sm%d" % i) for i in range(4)])
        rs["jk"] = Rot([A([128, D], BF16, "jk%d" % i) for i in range(2)])
        rs["xn"] = Rot([A([128, D], BF16, "xn%d" % i) for i in range(2)])
        rs["htb"] = Rot([A([128, 8, 512], BF16, "htb%d" % i) for i in range(2)])
        rs["pst"] = Rot([(self.PS[0], self.PB[0]), (self.PS[1], self.PB[1])])
        rs["psa"] = Rot([(self.PS[i], self.PB[i]) for i in (2, 3, 4, 5)])
        rs["psb"] = Rot([(self.PS[i], self.PB[i]) for i in (6, 7)])
        return rs, A

    def fm_proj(self, w_ap, w_b, wcol, kcn, wstride, rhs_fn, rhs_b, n, M, rs, scale, dst_dram, stg_rot, negate=False):
        ps, pb = rs["psa"].next()
        for kc in range(kcn):
            self.mm(ps[0:M, 0:n], w_ap[:, kc * wstride + wcol: kc * wstride + wcol + M], rhs_fn(kc), kc == 0, kc == kcn - 1,
                    [w_b, rhs_b], [pb])
        st, st_b = stg_rot.next()
        self.act(st[0:M, 0:n], ps[0:M, 0:n], AF.Copy, [pb], [st_b], scale=scale)
        self.dma(dst_dram, st[0:M, 0:n], r=[st_b])

    def past_kT(self, cache_ap, ntile, dst_dram_fn, rs, A_ld, A_bf, stgT):
        idb, idb_b = self.P("ident_bf")
        ld, ld_b = A_ld
        bf, bf_b = A_bf
        self.dma(ld[:, 0:ntile, :], cache_ap.rearrange("(m p) c -> p m c", p=128), w=[ld_b])
        self.cp("pool", bf[:, 0:ntile, :], ld[:, 0:ntile, :], [ld_b], [bf_b])
        for pr in range(4):
            for half in range(ntile // 4):
                ps, pb = rs["pst"].next()
                pbf = ps[:].bitcast(BF16)
                for m in range(4):
                    self.tr(pbf[:, m * 128:(m + 1) * 128], bf[:, half * 4 + m, pr * 128:(pr + 1) * 128], idb[:], [bf_b, idb_b], [pb])
                st, st_b = stgT.next()
                self.cp("dve", st[:, 0:512], pbf[:, 0:512], [pb], [st_b])
                self.dma(dst_dram_fn(pr, half * 512, 512), st[:, 0:512], r=[st_b])

    def past_v(self, cache_ap, ntile, dst_dram, A_ld, A_v):
        ld, ld_b = A_ld
        v, v_b = A_v
        self.dma(ld[:, 0:ntile, :], cache_ap.rearrange("(m p) c -> p m c", p=128), w=[ld_b])
        for m in range(ntile):
            self.cp("pool", v[:, m, :, 0:64], ld[:, m, :].rearrange("p (h d) -> p h d", d=64), [ld_b], [v_b])
        self.dma(dst_dram.rearrange("(m p) c -> p m c", p=128), v[:, 0:ntile].rearrange("p m h d -> p m (h d)"), r=[v_b])

    def tm_out(self, ps, pb, rows, ncols, out_dram, rot, eng):
        st, st_b = rot.next()
        self.cp(eng, st[0:rows, 0:ncols], ps[0:rows, 0:ncols], [pb], [st_b])
        self.dma(out_dram, st[0:rows, 0:ncols], r=[st_b])
        return st, st_b

    def v_store(self, ps, pb, rows, vrot, dst_dram):
        v, v_b = vrot.next()
        self.cp("dve", v[0:rows, :, 0:64], ps[0:rows, :].rearrange("p (h d) -> p h d", d=64), [pb], [v_b])
        self.dma(dst_dram, v[0:rows].rearrange("p h d -> p (h d)"), r=[v_b])

    def phase_pre0(self, xsrc_p, xsrc_s):
        T, NT, TOK = self.T, self.NT, self.TOK
        d = self.dr
        rs, A = self.pre_common(0)
        idb, idb_b = self.P("ident_bf")
        win = A([128, 8 * 1960], BF16, "win"); wuq = A([128, 2 * 1024], BF16, "wuq"); wukv = A([128, 1024], BF16, "wukv")
        self.load_w(win[0], win[1], d["win0"], 8 * 1960, rs["stg"])
        self.load_w(wuq[0], wuq[1], d["wuq"], 2048, rs["stg"])
        self.load_w(wukv[0], wukv[1], d["wukv"], 1024, rs["stg"])
        gq = A([128, 256], F32, "gq"); gkv = A([128, 128], F32, "gkv"); bfb = A([128, 8], F32, "bfb")
        self.dma(gq[0], d["q_a_g"].partition_broadcast(128), w=[gq[1]])
        self.dma(gkv[0], d["kv_a_g"].partition_broadcast(128), w=[gkv[1]])
        self.dma(bfb[0], d["b_f"].partition_broadcast(128), w=[bfb[1]])
        ct = A([128, self.NTT, 16], F32, "ct"); st_ = A([128, self.NTT, 16], F32, "st")
        self.dma(ct[0], d["ct"], w=[ct[1]]); self.dma(st_[0], d["st"], w=[st_[1]])
        c2 = Rot([A([128, 512], F32, "c2_%d" % i) for i in range(2)])
        s2 = Rot([A([128, 512], F32, "s2_%d" % i) for i in range(2)])
        o512 = Rot([A([128, 512], F32, "o512_%d" % i) for i in range(3)])
        o128 = Rot([A([128, 128], F32, "o128_%d" % i) for i in range(2)])
        o32 = Rot([A([128, 32], F32, "o32_%d" % i) for i in range(2)])
        tmp = Rot([A([128, 64], F32, "tmp%d" % i) for i in range(2)])
        ckvb = Rot([A([128, 128], BF16, "ckvb%d" % i) for i in range(2)])
        cqb = Rot([A([128, 256], BF16, "cqb%d" % i) for i in range(2)])
        krb = Rot([A([128, 32], BF16, "krb%d" % i) for i in range(2)])
        vst = [A([128, 8, 65], BF16, "vst%d" % i) for i in range(3)]
        for v, vb in vst:
            self.memset("pool", v[:], 1.0, [vb])
        vrot = Rot(vst)
        ckvt = Rot([A([128, 512], BF16, "ckvt%d" % i) for i in range(2)])
        cqt = Rot([A([128, 2, 512], BF16, "cqt%d" % i) for i in range(2)])
        krt = Rot([A([32, 512], BF16, "krt%d" % i) for i in range(2)])
        fst = Rot([A([128, 512], BF16, "fst%d" % i) for i in range(3)])
        rtmp = Rot([A([128, 512], F32, "rtmp%d" % i) for i in range(2)])
        lall, lall_b = self.P("lall")
        xs_p = ("a_ckv_p", "a_kr_p", "b_k_p", "b_v_p", "b_lf_p")

        ptasks = []
        for blk in self.blocks():
            cell = {}

            def blk_s1(blk=blk, cell=cell):
                is_s = blk[0][0] == NT
                n = 64 if is_s else 512
                cb0 = blk[0][2]
                htb, htb_b = rs["htb"].next()
                for (i, rows, c0) in blk:
                    x, x_b = rs["x"].next()
                    self.dma(x[0:rows, :], self.xrows(xsrc_p, xsrc_s, i, rows), w=[x_b])
                    self.norm_tile(x, x_b, rows, htb, htb_b, c0 - cb0, rs["afm"], rs["bfm"], rs["fm_b"], is_s, rs)
                cell["htb"] = (htb, htb_b)

            def blk_s2(blk=blk, cell=cell):
                is_s = blk[0][0] == NT
                n = 64 if is_s else 512
                cb0 = blk[0][2]
                ckvt_t, ckvt_b = ckvt.next()
                cqt_t, cqt_b = cqt.next()
                krt_t, krt_b = krt.next()
                htb, htb_b = cell["htb"]
                for (i, rows, c0) in blk:
                    lc = c0 - cb0
                    rsl = slice(i * 128, i * 128 + rows) if not is_s else slice(0, 64)
                    sfx = "_s" if is_s else "_p"
                    ps, pb = rs["psa"].next()
                    for kc in range(8):
                        self.mm(ps[0:rows, 0:424], htb[:, kc, lc:lc + rows], win[0][:, kc * 1960: kc * 1960 + 424], kc == 0, kc == 7,
                                [htb_b, win[1]], [pb])
                    sm, sm_b = rs["sm"].next()
                    jk, jk_b = rs["jk"].next()
                    self.act(jk[0:rows, 0:256], ps[0:rows, 0:256], AF.Square, [pb], [jk_b, sm_b], accum_out=sm[0:rows, 0:1])
                    self.act(jk[0:rows, 256:384], ps[0:rows, 256:384], AF.Square, [pb], [jk_b, sm_b], accum_out=sm[0:rows, 1:2])
                    self.act(sm[0:rows, 2:3], sm[0:rows, 0:1], AF.Ln, [sm_b], [sm_b], scale=1.0 / 256, bias=self.eps_ap[0:rows, :])
                    self.act(sm[0:rows, 3:4], sm[0:rows, 1:2], AF.Ln, [sm_b], [sm_b], scale=1.0 / 128, bias=self.eps_ap[0:rows, :])
                    self.act(sm[0:rows, 4:6], sm[0:rows, 2:4], AF.Exp, [sm_b], [sm_b], scale=-0.5)
                    oc, oc_b = o128.next()
                    self.stt("dve", oc[0:rows, :], ps[0:rows, 256:384], sm[0:rows, 5:6], gkv[0][0:rows, :], ALU.mult, ALU.mult,
                             [pb, sm_b, gkv[1]], [oc_b])
                    self.dma(d["a_ckv" + sfx][rsl, :], oc[0:rows, :], r=[oc_b])
                    cb_, cb_b = ckvb.next()
                    self.cp("pool", cb_[0:rows, :], oc[0:rows, :], [oc_b], [cb_b])
                    cq_, cq_b = cqb.next()
                    self.stt("dve", cq_[0:rows, :], ps[0:rows, 0:256], sm[0:rows, 4:5], gq[0][0:rows, :], ALU.mult, ALU.mult,
                             [pb, sm_b, gq[1]], [cq_b])
                    t_, t_b = tmp.next()
                    ok, ok_b = o32.next()
                    cs_ = ct[0][0:rows, i, :]; sn_ = st_[0][0:rows, i, :]
                    x1 = ps[0:rows, 384:400]; x2 = ps[0:rows, 400:416]
                    self.tt("dve", t_[0:rows, 0:16], x1, cs_, ALU.mult, [pb, ct[1]], [t_b])
                    self.tt("dve", t_[0:rows, 16:32], x2, sn_, ALU.mult, [pb, st_[1]], [t_b])
                    self.tt("dve", t_[0:rows, 32:48], x1, sn_, ALU.mult, [pb, st_[1]], [t_b])
                    self.tt("dve", t_[0:rows, 48:64], x2, cs_, ALU.mult, [pb, ct[1]], [t_b])
                    self.tt("pool", ok[0:rows, 0:16], t_[0:rows, 0:16], t_[0:rows, 16:32], ALU.subtract, [t_b], [ok_b])
                    self.tt("pool", ok[0:rows, 16:32], t_[0:rows, 32:48], t_[0:rows, 48:64], ALU.add, [t_b, ok_b], [ok_b])
                    self.dma(d["a_kr" + sfx][rsl, :], ok[0:rows, :], r=[ok_b])
                    kb_, kb_b = krb.next()
                    self.cp("pool", kb_[0:rows, :], ok[0:rows, :], [ok_b], [kb_b])
                    t2, t2_b = tmp.next()
                    self.tt("dve", t2[0:rows, 0:8], ps[0:rows, 416:424], bfb[0][0:rows, :], ALU.add, [pb, bfb[1]], [t2_b])
                    self.act(t2[0:rows, 8:16], t2[0:rows, 0:8], AF.Exp, [t2_b], [t2_b], scale=-1.0)
                    self.act(t2[0:rows, 16:24], t2[0:rows, 8:16], AF.Ln, [t2_b], [t2_b], scale=1.0, bias=1.0)
                    self.ts("dve", lall[0:rows, i, :], t2[0:rows, 16:24], -1.0, None, ALU.mult, None, [t2_b], [lall_b])
                    self.dma(d["b_lf" + sfx][rsl, :], lall[0:rows, i, :], r=[lall_b])
                    pt_, ptb = rs["pst"].next()
                    pbf = pt_[:].bitcast(BF16)
                    self.tr(pbf[:, 0:rows], cb_[0:rows, :], idb[0:rows, 0:rows], [cb_b, idb_b], [ptb])
                    self.tr(pbf[:, 128:128 + rows], cq_[0:rows, 0:128], idb[0:rows, 0:rows], [cq_b, idb_b], [ptb])
                    self.tr(pbf[:, 256:256 + rows], cq_[0:rows, 128:256], idb[0:rows, 0:rows], [cq_b, idb_b], [ptb])
                    self.tr(pbf[0:32, 384:384 + rows], kb_[0:rows, :], idb[0:rows, 0:rows], [kb_b, idb_b], [ptb])
                    self.cp("dve", ckvt_t[:, lc:lc + rows], pbf[:, 0:rows], [ptb], [ckvt_b])
                    self.cp("dve", cqt_t[:, 0, lc:lc + rows], pbf[:, 128:128 + rows], [ptb], [cqt_b])
                    self.cp("dve", cqt_t[:, 1, lc:lc + rows], pbf[:, 256:256 + rows], [ptb], [cqt_b])
                    self.cp("dve", krt_t[0:32, lc:lc + rows], pbf[0:32, 384:384 + rows], [ptb], [krt_b])
                    ps, pb = rs["psa"].next()
                    for kc in range(8):
                        self.mm(ps[0:rows, :], htb[:, kc, lc:lc + rows], win[0][:, kc * 1960 + 936: kc * 1960 + 1448], kc == 0, kc == 7,
                                [htb_b, win[1]], [pb])
                    self.tm_out(ps, pb, rows, 512, d["b_k" + sfx][rsl, :], o512, "act")
                    ps, pb = rs["psa"].next()
                    for kc in range(8):
                        self.mm(ps[0:rows, :], htb[:, kc, lc:lc + rows], win[0][:, kc * 1960 + 1448: kc * 1960 + 1960], kc == 0, kc == 7,
                                [htb_b, win[1]], [pb])
                    self.tm_out(ps, pb, rows, 512, d["b_v" + sfx][rsl, :], o512, "act")
                    self.v_store(ps, pb, rows, vrot, d["vs_b"][c0:c0 + rows, :])
                    ps, pb = rs["psa"].next()
                    self.mm(ps[0:rows, :], ckvt_t[:, lc:lc + rows], wukv[0][:, 512:1024], True, True, [ckvt_b, wukv[1]], [pb])
                    self.v_store(ps, pb, rows, vrot, d["vs_a"][c0:c0 + rows, :])
                self.dma(d["kr"][:, cb0:cb0 + n], krt_t[0:32, 0:n], r=[krt_b])
                for pr in range(4):
                    self.fm_proj(win[0], win[1], 424 + pr * 128, 8, 1960, lambda kc: htb[:, kc, 0:n], htb_b, n, 128, rs, 0.125,
                                 d["qs_b"][pr * 128:(pr + 1) * 128, cb0:cb0 + n], fst)
                    self.fm_proj(win[0], win[1], 936 + pr * 128, 8, 1960, lambda kc: htb[:, kc, 0:n], htb_b, n, 128, rs, 1.0,
                                 d["ks_b"][pr * 128:(pr + 1) * 128, cb0:cb0 + n], fst)
                    self.fm_proj(wukv[0], wukv[1], pr * 128, 1, 0, lambda kc: ckvt_t[:, 0:n], ckvt_b, n, 128, rs, 1.0,
                                 d["ks_an"][pr * 128:(pr + 1) * 128, cb0:cb0 + n], fst)
                c2t, c2b = c2.next(); s2t, s2b = s2.next()
                self.dma(c2t[64:96, 0:n], d["cos2"][:, cb0:cb0 + n], w=[c2b])
                self.dma(s2t[64:96, 0:n], d["sin2"][:, cb0:cb0 + n], w=[s2b])
                for h in range(8):
                    ps, pb = rs["psa"].next()
                    ps2, pb2 = rs["psb"].next()
                    for kc in range(2):
                        self.mm(ps[0:96, 0:n], wuq[0][:, kc * 1024 + h * 96: kc * 1024 + h * 96 + 96], cqt_t[:, kc, 0:n], kc == 0, kc == 1,
                                [wuq[1], cqt_b], [pb])
                    for kc in range(2):
                        self.mm(ps2[64:96, 0:n], wuq[0][:, kc * 1024 + 768 + h * 32: kc * 1024 + 768 + h * 32 + 32], cqt_t[:, kc, 0:n],
                                kc == 0, kc == 1, [wuq[1], cqt_b], [pb2])
                    qa, qa_b = fst.next()
                    self.act(qa[0:64, 0:n], ps[0:64, 0:n], AF.Copy, [pb], [qa_b], scale=float(96.0 ** -0.5))
                    r1, r1_b = rtmp.next()
                    r2, r2_b = rtmp.next()
                    self.tt("dve", r1[64:96, 0:n], ps[64:96, 0:n], c2t[64:96, 0:n], ALU.mult, [pb, c2b], [r1_b])
                    self.tt("dve", r2[64:96, 0:n], ps2[64:96, 0:n], s2t[64:96, 0:n], ALU.mult, [pb2, s2b], [r2_b])
                    self.tt("pool", qa[64:96, 0:n], r1[64:96, 0:n], r2[64:96, 0:n], ALU.add, [r1_b, r2_b, qa_b], [qa_b])
                    self.dma(d["qs_a"][h, :, cb0:cb0 + n], qa[0:96, 0:n], r=[qa_b])
            ptasks.append((blk_s1, blk_s2))
        self.run_pipeline(ptasks, 1, 0)

        ld = A([128, 8, 512], F32, "pld"); bf = A([128, 8, 512], BF16, "pbf")
        vpast = A([128, 8, 8, 65], BF16, "vpast")
        self.memset("pool", vpast[0][:], 1.0, [vpast[1]])
        lpast, lpast_b = self.P("lpast")
        for s_ in range(2):
            pc0 = TOK + s_ * PAST
            self.past_kT(d["cb_k"][s_], 8, lambda pr, c, w_: d["ks_b"][pr * 128:(pr + 1) * 128, pc0 + c:pc0 + c + w_], rs, ld, bf, fst)
            self.past_v(d["cb_v"][s_], 8, d["vs_b"][pc0:pc0 + PAST, :], ld, vpast)
            self.dma(lpast[:, s_, :, :], d["cb_lf"][s_].rearrange("(m p) h -> p m h", p=128), w=[lpast_b])
            self.dma(ld[0][:, :, 0:128], d["ca_ckv"][s_].rearrange("(m p) c -> p m c", p=128), w=[ld[1]])
            self.dma(ld[0][:, :, 128:160], d["ca_kr"][s_].rearrange("(m p) c -> p m c", p=128), w=[ld[1]])
            self.cp("pool", bf[0][:, :, 0:160], ld[0][:, :, 0:160], [ld[1]], [bf[1]])
            for half in range(2):
                ck_t, ck_b = ckvt.next()
                kr_t, kr_b = krt.next()
                ps, pb = rs["pst"].next()
                pbf = ps[:].bitcast(BF16)
                ps2, pb2 = rs["pst"].next()
                pbf2 = ps2[:].bitcast(BF16)
                for m in range(4):
                    self.tr(pbf[:, m * 128:(m + 1) * 128], bf[0][:, half * 4 + m, 0:128], idb[:], [bf[1], idb_b], [pb])
                    self.tr(pbf2[0:32, m * 128:(m + 1) * 128], bf[0][:, half * 4 + m, 128:160], idb[:], [bf[1], idb_b], [pb2])
                self.cp("dve", ck_t[:, 0:512], pbf[:, 0:512], [pb], [ck_b])
                self.cp("dve", kr_t[0:32, 0:512], pbf2[0:32, 0:512], [pb2], [kr_b])
                cc0 = pc0 + half * 512
                self.dma(d["kr"][:, cc0:cc0 + 512], kr_t[0:32, 0:512], r=[kr_b])
                for pr in range(4):
                    self.fm_proj(wukv[0], wukv[1], pr * 128, 1, 0, lambda kc: ck_t[:, 0:512], ck_b, 512, 128, rs, 1.0,
                                 d["ks_an"][pr * 128:(pr + 1) * 128, cc0:cc0 + 512], fst)
                for m in range(4):
                    ps3, pb3 = rs["psa"].next()
                    self.mm(ps3[:, :], ck_t[:, m * 128:(m + 1) * 128], wukv[0][:, 512:1024], True, True, [ck_b, wukv[1]], [pb3])
                    r0 = pc0 + (half * 4 + m) * 128
                    self.v_store(ps3, pb3, 128, vrot, d["vs_a"][r0:r0 + 128, :])
        self.fox_cum(rs, A)
        self.kb.barrier()

    def fox_cum(self, rs, A):
        NT, T = self.NT, self.T
        lall, lall_b = self.P("lall"); lpast, lpast_b = self.P("lpast")
        ncum, ncum_b = self.P("ncum"); ncump, ncump_b = self.P("ncump")
        cumt, cumt_b = self.P("cumt")
        uf, uf_b = self.P("uf"); onesf, onesf_b = self.P("onesf"); idf, idf_b = self.P("ident_f")
        tot = A([128, 33, 8], F32, "tot"); car = A([128, 34, 8], F32, "car")
        self.memset("pool", ncum[:], 0.0, [ncum_b])
        self.memset("pool", cumt[:], 0.0, [cumt_b])

        def seq_cum(l_ap, l_b, nt, out_ap, out_b):
            ps, pb = rs["psa"].next()
            ps2, pb2 = rs["psa"].next()
            lf = l_ap.rearrange("p m h -> p (m h)")
            self.mm(ps[:, 0:nt * 8], onesf[:], lf, True, True, [onesf_b, l_b], [pb])
            self.mm(ps2[:, 0:nt * 8], uf[:], lf, True, True, [uf_b, l_b], [pb2])
            self.cp("dve", tot[0][:, 0:nt, :], ps[:, 0:nt * 8].rearrange("p (m h) -> p m h", h=8), [pb], [tot[1]])
            self.memset("dve", car[0][:, 0, :], 0.0, [car[1]])
            for j in range(1, nt + 1):
                self.tt("dve", car[0][:, j, :], car[0][:, j - 1, :], tot[0][:, j - 1, :], ALU.add, [car[1], tot[1]], [car[1]])
            self.stt("dve", out_ap, ps2[:, 0:nt * 8].rearrange("p (m h) -> p m h", h=8), -1.0, car[0][:, 0:nt, :],
                     ALU.mult, ALU.subtract, [pb2, car[1]], [out_b])
        seq_cum(lall[:, 0:NT, :], lall_b, NT, ncum[:, 0:NT, :], ncum_b)
        for s_ in range(2):
            seq_cum(lpast[:, s_, :, :], lpast_b, 8, ncump[:, s_, :, :], ncump_b)
            ps, pb = rs["psa"].next()
            b0 = 32 * s_
            self.mm(ps[b0:b0 + 16, 0:8], uf[b0:b0 + 16, b0:b0 + 16], lall[b0:b0 + 16, NT, :], True, True, [uf_b, lall_b], [pb])
            self.stt("dve", ncum[b0:b0 + 16, NT, :], ps[b0:b0 + 16, 0:8], -1.0, car[0][b0:b0 + 16, 8, :], ALU.mult, ALU.subtract,
                     [pb, car[1]], [ncum_b])
        for g in range(0, self.NTT, 4):
            ps, pb = rs["psa"].next()
            cnt = min(4, self.NTT - g)
            wtot = 0
            for m in range(cnt):
                i = g + m
                rows = 128 if i < NT else 64
                self.tr(ps[0:8, m * 128:m * 128 + rows], ncum[0:rows, i, :], idf[0:rows, 0:rows], [ncum_b, idf_b], [pb])
                wtot = m * 128 + rows
            self.act(cumt[0:8, g * 128:g * 128 + wtot], ps[0:8, 0:wtot], AF.Copy, [pb], [cumt_b], scale=-1.0)

    def run_pipeline(self, tasks, L=2, L2=1):
        n = len(tasks)
        for i in range(n + L + L2):
            if i < n:
                tasks[i][0]()
            if 0 <= i - L < n:
                tasks[i - L][1]()
            if 0 <= i - L - L2 < n and len(tasks[i - L - L2]) > 2:
                tasks[i - L - L2][2]()

    def softmax_tasks(self, qT, Kd, nq, qtiles, ktiles, res, obuf_fn):
        st = {"first": True, "ob": None}
        nkt = len(ktiles)
        tasks = []
        for ti, kt in enumerate(ktiles):
            cell = {}

            def s1(kt=kt, cell=cell, ti=ti):
                if ti == 0:
                    st["ob"] = res["obank"].next()
                nk, pb0, clo = kt["nk"], kt["pbase"], kt["clo"]
                ps, pb = res["sbank"].next()
                cell["ps"] = (ps, pb)
                extra = kt.get("extra", [])
                masks = kt.get("masks", [])
                nmm = 1 + len(extra) + len(masks)
                k_ = 0
                self.mm(ps[pb0:pb0 + nk, clo:nq], kt["kT"], qT[:, clo:nq], True, nmm == 1, kt["rb"], [pb], skip_group_check=nmm > 1)
                for (l_ap, r_ap, rb) in extra:
                    k_ += 1
                    self.mm(ps[pb0:pb0 + nk, clo:nq], l_ap, r_ap[:, clo:nq], False, k_ == nmm - 1, rb, [pb], skip_group_check=True)
                for (co, wd, l_ap, r_ap, rb) in masks:
                    k_ += 1
                    self.mm(ps[pb0:pb0 + nk, co:co + wd], l_ap, r_ap, False, k_ == nmm - 1, rb, [pb], skip_group_check=True)

            def s2(kt=kt, cell=cell, ti=ti):
                nk, pb0, clo = kt["nk"], kt["pbase"], kt["clo"]
                ps, pb = cell["ps"]
                ob, ob_b = st["ob"]
                pt, pt_b = res["pt"].next()
                kw = {}
                rd = [pb]
                if kt.get("bias") is not None:
                    kw["bias"] = kt["bias"]
                    rd = rd + kt["bias_b"]
                self.act(pt[pb0:pb0 + nk, clo:nq], ps[pb0:pb0 + nk, clo:nq], AF.Exp, rd, [pt_b], **kw)
                for qi, (ql, qn, qpb) in enumerate(qtiles):
                    if ql + qn <= clo:
                        continue
                    self.mm(ob[qpb:qpb + qn, qi * 65:qi * 65 + 65], pt[pb0:pb0 + nk, ql:ql + qn], kt["v"], st["first"], False,
                            [pt_b] + kt["vb"], [ob_b], skip_group_check=True)
                    st["first"] = False
                if ti == nkt - 1:
                    for qi, (ql, qn, qpb) in enumerate(qtiles):
                        rc, rc_b = res["rc"].next()
                        self.kb.op("dve", lambda h, rc=rc, ob=ob, qpb=qpb, qn=qn, qi=qi: h.reciprocal(
                            out=rc[qpb:qpb + qn, 0:1], in_=ob[qpb:qpb + qn, qi * 65 + 64:qi * 65 + 65]), [ob_b], [rc_b])
                        dst, dst_b = obuf_fn(qi)
                        self.ts("dve", dst, ob[qpb:qpb + qn, qi * 65:qi * 65 + 64], rc[qpb:qpb + qn, 0:1], None, ALU.mult, None,
                                [ob_b, rc_b], [dst_b])
            tasks.append((s1, s2))
        return tasks

    def softmax_block(self, qT, Kd, nq, qtiles, ktiles, res, obuf_fn):
        self.run_pipeline(self.softmax_tasks(qT, Kd, nq, qtiles, ktiles, res, obuf_fn))

    def att_alloc(self):
        ar = self.ar
        ar.reset()
        A = lambda shape, dt, name="": ar.alloc(shape, dt, name)
        res = {}
        res["qt"] = Rot([A([128, self.TOK], BF16, "qt%d" % i) for i in range(2)])
        res["kt"] = Rot([A([128, self.TOKK], BF16, "kt%d" % i) for i in range(2)])
        nvt = self.NTT + 16
        res["vt"] = Rot([A([128, nvt, 4, 65], BF16, "vt%d" % i) for i in range(2)])
        res["obuf"] = Rot([A([128, self.NTT, 256], BF16, "obuf%d" % i) for i in range(2)])
        res["pt"] = Rot([A([128, 512], BF16, "pt%d" % i) for i in range(4)])
        res["rc"] = Rot([A([128, 1], F32, "rc%d" % i) for i in range(4)])
        res["sbank"] = Rot([(self.PS[i], self.PB[i]) for i in (0, 1, 2)])
        res["obank"] = Rot([(self.PS[i], self.PB[i]) for i in (3, 4)])
        res["xbank"] = Rot([(self.PS[i], self.PB[i]) for i in (5, 6)])
        res["zbank"] = Rot([(self.PS[i], self.PB[i]) for i in (7,)])
        return res, A

    def load_v(self, res, vs_name, hg):
        NT, T, TOK = self.NT, self.T, self.TOK
        vt, vt_b = res["vt"].next()
        src = self.dr[vs_name]
        cs = slice(hg * 260, hg * 260 + 260)
        self.dma(vt[:, 0:NT].rearrange("p m h d -> p m (h d)"), src[0:T, cs].rearrange("(m p) c -> p m c", p=128), w=[vt_b])
        self.dma(vt[0:64, NT].rearrange("p h d -> p (h d)"), src[T:T + 64, cs], w=[vt_b])
        self.dma(vt[:, NT + 1:NT + 17].rearrange("p m h d -> p m (h d)"),
                 src[TOK:TOK + 2 * PAST, cs].rearrange("(m p) c -> p m c", p=128), w=[vt_b])
        return vt, vt_b

    def flush_obuf(self, obuf, obuf_b, colbase):
        NT, T = self.NT, self.T
        dst = self.dr["os"]
        self.dma(dst[0:T, colbase:colbase + 256].rearrange("(m p) c -> p m c", p=128), obuf[:, 0:NT, :], r=[obuf_b])
        self.dma(dst[T:T + 64, colbase:colbase + 256], obuf[0:64, NT, :], r=[obuf_b])

    def head_plan(self, res, typ, h, qname, kname, vname):
        d = self.dr
        st = self._hp
        loads = []
        if h % 4 == 0:
            st["vt"] = res["vt"].next()
            st["obuf"] = res["obuf"].next()
            vt, vt_b = st["vt"]
            obuf, obuf_b = st["obuf"]
            loads.append(lambda vt=vt, vt_b=vt_b, hg=h // 4: self.load_v_into(vt, vt_b, vname, hg))
            loads.append(lambda obuf=obuf, obuf_b=obuf_b: self.memset("pool", obuf[:], 0.0, [obuf_b]))
        if typ == "mla":
            st["qt"] = res["qt"].next(); st["kt"] = res["kt"].next()
            qt, qt_b = st["qt"]; kt, kt_b = st["kt"]
            loads.append(lambda: self.dma(qt[0:96, :], d["qs_a"][h], w=[qt_b]))
            loads.append(lambda: self.dma(kt[0:64, :], d["ks_an"][h * 64:(h + 1) * 64, :], w=[kt_b]))
            loads.append(lambda: self.dma(kt[64:96, :], d["kr"], w=[kt_b]))
            Kd, p0 = 96, 0
        else:
            st["qt"] = res["qt"].next(); st["kt"] = res["kt"].next()
            qt, qt_b = st["qt"]; kt, kt_b = st["kt"]
            cumt, cumt_b = self.P("cumt")
            loads.append(lambda: self.dma(qt[0:64, :], d[qname][h * 64:(h + 1) * 64, :], w=[qt_b]))
            loads.append(lambda: self.dma(kt[0:64, :], d[kname][h * 64:(h + 1) * 64, :], w=[kt_b]))
            loads.append(lambda: self.memset("pool", kt[64:65, :], 1.0, [kt_b]))
            if typ == "fox":
                loads.append(lambda: self.dma(qt[64:65, :], cumt[h:h + 1, :], r=[cumt_b], w=[qt_b]))
            else:
                loads.append(lambda: self.memset("pool", qt[64:65, :], 0.0, [qt_b]))
            Kd, p0 = 65, 0
        qt, qt_b = st["qt"]; kt, kt_b = st["kt"]
        vt, vt_b = st["vt"]; obuf, obuf_b = st["obuf"]
        ctx = dict(qv=qt[p0:p0 + Kd, :], kv=kt[p0:p0 + Kd, :], qt_b=qt_b, kt_b=kt_b, vt=vt, vt_b=vt_b,
                   obuf=obuf, obuf_b=obuf_b, Kd=Kd, hh=h % 4)

        def emit_loads():
            for f in loads:
                f()
        ctx["emit_loads"] = emit_loads
        return ctx

    def load_v_into(self, vt, vt_b, vs_name, hg):
        NT, T, TOK = self.NT, self.T, self.TOK
        src = self.dr[vs_name]
        cs = slice(hg * 260, hg * 260 + 260)
        self.dma(vt[:, 0:NT].rearrange("p m h d -> p m (h d)"), src[0:T, cs].rearrange("(m p) c -> p m c", p=128), w=[vt_b])
        self.dma(vt[0:64, NT].rearrange("p h d -> p (h d)"), src[T:T + 64, cs], w=[vt_b])
        self.dma(vt[:, NT + 1:NT + 17].rearrange("p m h d -> p m (h d)"),
                 src[TOK:TOK + 2 * PAST, cs].rearrange("(m p) c -> p m c", p=128), w=[vt_b])

    def att_drive(self, res, typ, names, task_fn, oscol, L=2, post_fn=None):
        self._hp = {}
        ctxs = [self.head_plan(res, typ, h, *names) for h in range(8)]
        tasks = []
        ctxs[0]["emit_loads"]()
        for h in range(8):
            ht = task_fn(h, ctxs[h])
            if h + 1 < 8:
                s1 = ht[0][0]
                nxt = ctxs[h + 1]["emit_loads"]
                ht[0] = ((lambda s1=s1, nxt=nxt: (nxt(), s1())),) + tuple(ht[0][1:])
            c = ctxs[h]
            fl = (lambda c=c, h=h: self.flush_obuf(c["obuf"], c["obuf_b"], oscol + (h // 4) * 256)) if h % 4 == 3 else None
            if post_fn is not None:
                self.run_pipeline(ht, L)
                post_fn(h, c)
                if fl:
                    fl()
            else:
                if fl:
                    ht.append((lambda: None, lambda: None, fl))
                tasks += ht
        if tasks:
            self.run_pipeline(tasks, L)

    def phase_att0(self):
        T, NT, TOK = self.T, self.NT, self.TOK
        res, A = self.att_alloc()
        idb, idb_b = self.P("ident_bf")
        mm_, mm_b = self.P("m_mla"); mf_, mf_b = self.P("m_fox")
        selh, selh_b = self.P("selh"); cumt, cumt_b = self.P("cumt")
        ncum, ncum_b = self.P("ncum"); ncump, ncump_b = self.P("ncump")

        def mk(typ):
            msk, msk_b = (mm_, mm_b) if typ == "mla" else (mf_, mf_b)

            def task_fn(h, c):
                qv, kv, vt, vt_b, obuf, obuf_b, hh = c["qv"], c["kv"], c["vt"], c["vt_b"], c["obuf"], c["obuf_b"], c["hh"]
                rb = [c["kt_b"], c["qt_b"]]
                tasks = []
                for qb in range(NT // 4):
                    qc = qb * 512
                    ktl = []
                    for j in range(4 * qb + 4):
                        r = j - 4 * qb
                        kd = dict(nk=128, pbase=0, clo=max(r, 0) * 128, kT=kv[:, j * 128:(j + 1) * 128], rb=rb,
                                  v=vt[:, j, hh, :], vb=[vt_b])
                        if typ == "fox":
                            kd["bias"] = ncum[:, j, h:h + 1]; kd["bias_b"] = [ncum_b]
                        if r >= 0:
                            kd["masks"] = [(r * 128, 128, idb[:], msk[:], [idb_b, msk_b])]
                        ktl.append(kd)
                    tasks += self.softmax_tasks(qv[:, qc:qc + 512], c["Kd"], 512, [(m * 128, 128, 0) for m in range(4)], ktl, res,
                                                lambda qi, qb=qb: (obuf[:, 4 * qb + qi, hh * 64:(hh + 1) * 64], obuf_b))
                for s_ in range(2):
                    b0 = 32 * s_
                    qc = T + b0
                    ktl = []
                    for m in range(8):
                        kc0 = TOK + s_ * PAST + m * 128
                        kd = dict(nk=128, pbase=0, clo=0, kT=kv[:, kc0:kc0 + 128], rb=rb,
                                  v=vt[:, NT + 1 + s_ * 8 + m, hh, :], vb=[vt_b])
                        if typ == "fox":
                            kd["bias"] = ncump[:, s_, m, h:h + 1]; kd["bias_b"] = [ncump_b]
                        ktl.append(kd)
                    kd = dict(nk=16, pbase=b0, clo=0, kT=kv[:, qc:qc + 16], rb=rb, v=vt[b0:b0 + 16, NT, hh, :], vb=[vt_b])
                    if typ == "fox":
                        kd["bias"] = ncum[b0:b0 + 16, NT, h:h + 1]; kd["bias_b"] = [ncum_b]
                        kd["masks"] = [(0, 16, idb[:, b0:b0 + 16], msk[:, b0:b0 + 16], [idb_b, msk_b])]
                    ktl.append(kd)
                    tasks += self.softmax_tasks(qv[:, qc:qc + 16], c["Kd"], 16, [(0, 16, b0)], ktl, res,
                                                lambda qi, b0=b0: (obuf[b0:b0 + 16, NT, hh * 64:(hh + 1) * 64], obuf_b))
                return tasks
            return task_fn
        self.att_drive(res, "mla", (None, None, "vs_a"), mk("mla"), 0)
        self.att_drive(res, "fox", ("qs_b", "ks_b", "vs_b"), mk("fox"), 512)
        self.kb.barrier()

    def phase_proj(self, l, xsrc_p, xsrc_s, xdst):
        T, NT, TOK = self.T, self.NT, self.TOK
        d = self.dr
        ar = self.ar
        ar.reset()
        A = lambda shape, dt, name="": ar.alloc(shape, dt, name)
        idb, idb_b = self.P("ident_bf")
        stg = Rot([A([128, 2048], F32, "stg%d" % i) for i in range(2)])
        eps, eps_b = A([128, 1], F32, "eps")
        self.memset("pool", eps, EPS, [eps_b])
        self.eps_ap = eps
        wada = A([128, 8192], BF16, "wada")
        self.ar_bc1 = A([128, D], F32, "bc1"); self.ar_bc2 = A([128, D], F32, "bc2")
        gp = A([128, D], F32, "gp"); gs = A([128, D], F32, "gs")
        self.mod_bc(l, 2, stg, wada, gp[0], gp[1], gs[0], gs[1], d["gpost"][l, 0])
        wout = A([128, 8192], BF16, "wout")
        self.load_w(wout[0], wout[1], d["wout%d" % l], 8192, stg)
        osb = Rot([A([128, D], BF16, "osb%d" % i) for i in range(2)])
        ot = Rot([A([128, 8, 128], BF16, "ot%d" % i) for i in range(2)])
        xr = Rot([A([128, D], F32, "x%d" % i) for i in range(3)])
        tmp = Rot([A([128, 512], F32, "t%d" % i) for i in range(2)])
        sm = Rot([A([128, 16], F32, "sm%d" % i) for i in range(4)])
        jk = Rot([A([128, 512], BF16, "jk%d" % i) for i in range(2)])
        pst = Rot([(self.PS[0], self.PB[0]), (self.PS[1], self.PB[1])])
        pso = Rot([(self.PS[i], self.PB[i]) for i in (2, 3, 4, 5)])
        ptasks = []
        for (i, rows, c0) in self.tiles():
            cell = {}

            def s1(i=i, rows=rows, c0=c0, cell=cell):
                o_, o_b = osb.next()
                self.dma(o_[0:rows, :], d["os"][c0:c0 + rows, :], w=[o_b])
                x, x_b = xr.next()
                self.dma(x[0:rows, :], self.xrows(xsrc_p, xsrc_s, i, rows), w=[x_b])
                ps, pb = pst.next()
                pbf = ps[:].bitcast(BF16)
                for c in range(8):
                    self.tr(pbf[:, c * 128:c * 128 + rows], o_[0:rows, c * 128:(c + 1) * 128], idb[0:rows, 0:rows], [o_b, idb_b], [pb])
                ot_, ot_b = ot.next()
                self.cp("dve", ot_[:, :, 0:rows], pbf.rearrange("p (c t) -> p c t", t=128)[:, :, 0:rows], [pb], [ot_b])
                cell["st"] = self.resid_mm(lambda kc, half: (ot_[:, kc, 0:rows], wout[0][:, kc * 1024 + half * 512: kc * 1024 + half * 512 + 512]),
                                           8, [ot_b, wout[1]], rows, pso, sm, jk)
                cell["x"] = (x, x_b)

            def s2(i=i, rows=rows, c0=c0, cell=cell):
                is_s = i == NT
                g_ap, g_b = (gs if is_s else gp)
                x, x_b = cell["x"]
                self.resid_fin(cell["st"], rows, x, x_b, g_ap, g_b, tmp, xdst[c0:c0 + rows, :])
            ptasks.append((s1, s2))
        self.run_pipeline(ptasks, 1, 0)
        self.kb.barrier()

    def resid_out(self, opfn, nk, rb, rows, x, x_b, g_ap, g_b, pso, sm, jk, tmp, dst):
        stt_ = self.resid_mm(opfn, nk, rb, rows, pso, sm, jk)
        self.resid_fin(stt_, rows, x, x_b, g_ap, g_b, tmp, dst)

    def resid_mm(self, opfn, nk, rb, rows, pso, sm, jk):
        pss = []
        s_, s_b = sm.next()
        for half in range(2):
            ps, pb = pso.next()
            for kc in range(nk):
                l_ap, r_ap = opfn(kc, half)
                self.mm(ps[0:rows, :], l_ap, r_ap, kc == 0, kc == nk - 1, rb, [pb])
            j_, j_b = jk.next()
            self.act(j_[0:rows, 0:512], ps[0:rows, :], AF.Square, [pb], [j_b, s_b], accum_out=s_[0:rows, half:half + 1])
            pss.append((ps, pb))
        return (pss, s_, s_b)

    def resid_fin(self, stt_, rows, x, x_b, g_ap, g_b, tmp, dst):
        pss, s_, s_b = stt_
        self.tt("dve", s_[0:rows, 2:3], s_[0:rows, 0:1], s_[0:rows, 1:2], ALU.add, [s_b], [s_b])
        self.act(s_[0:rows, 3:4], s_[0:rows, 2:3], AF.Ln, [s_b], [s_b], scale=1.0 / D, bias=self.eps_ap[0:rows, :])
        self.act(s_[0:rows, 4:5], s_[0:rows, 3:4], AF.Exp, [s_b], [s_b], scale=-0.5)
        for half in range(2):
            ps, pb = pss[half]
            sl = slice(half * 512, half * 512 + 512)
            t_, t_b = tmp.next()
            self.stt("dve", t_[0:rows, :], ps[0:rows, :], s_[0:rows, 4:5], g_ap[0:rows, sl], ALU.mult, ALU.mult,
                     [pb, s_b, g_b], [t_b])
            self.tt("pool", x[0:rows, sl], x[0:rows, sl], t_[0:rows, :], ALU.add, [x_b, t_b], [x_b])
        self.dma(dst, x[0:rows, :], r=[x_b])

    def phase_ffn(self, l, xsrc, xdst_p, xdst_s):
        T, NT, TOK = self.T, self.NT, self.TOK
        d = self.dr
        ar = self.ar
        ar.reset()
        A = lambda shape, dt, name="": ar.alloc(shape, dt, name)
        idb, idb_b = self.P("ident_bf")
        halo, halo_b = self.P("halo")
        wg = A([128, 8 * DFF], BF16, "wg"); wu = A([128, 8 * DFF], BF16, "wu"); wd = A([128, NFC * D], BF16, "wd")
        gp = A([128, D], F32, "gp"); gs = A([128, D], F32, "gs")
        afm = A([128, 8, 3], F32, "afm"); bfm = A([128, 8, 3], F32, "bfm"); gfm = A([128, 8], F32, "gfm")
        cw = A([128, NFC, 3], F32, "cw"); cbi = A([128, NFC], F32, "cb")
        eps, eps_b = A([128, 1], F32, "eps")
        self.memset("pool", eps, EPS, [eps_b])
        self.eps_ap = eps
        mark = ar.off
        stg = Rot([A([128, 2048], F32, "stg%d" % i) for i in range(2)])
        wada = A([128, 8192], BF16, "wada")
        self.ar_small_fm = A([128, 8], F32, "adabfm")
        self.ar_bc1 = A([128, D], F32, "bc1"); self.ar_bc2 = A([128, D], F32, "bc2")
        self.mod_fm(l, [4, 3], stg, wada, [afm, bfm])
        self.dma(gfm[0], d["gpre_fm"][l, 1], w=[gfm[1]])
        for sq in range(3):
            self.stt("dve", afm[0][:, :, sq], afm[0][:, :, sq], 1.0, gfm[0], ALU.add, ALU.mult, [afm[1], gfm[1]], [afm[1]])
        fm_b = Buf("fmb")
        self.cp("dve", bfm[0][:, 0, 0:1], bfm[0][:, 0, 0:1], [afm[1], bfm[1]], [fm_b, bfm[1]])
        self.mod_bc(l, 5, stg, wada, gp[0], gp[1], gs[0], gs[1], d["gpost"][l, 1])
        self.load_w(wg[0], wg[1], d["wg%d" % l], 8 * DFF, stg)
        self.load_w(wu[0], wu[1], d["wu%d" % l], 8 * DFF, stg)
        self.load_w(wd[0], wd[1], d["wd%d" % l], NFC * D, stg)
        self.dma(cw[0], d["convw"][l], w=[cw[1]])
        self.dma(cbi[0], d["convb"][l], w=[cbi[1]])
        self.memset("pool", halo[:, :, 0, :], 0.0, [halo_b])
        for s_ in range(2):
            self.dma(halo[:, :, 1 + s_, :], d["st_conv_fm"][l, s_], w=[halo_b])
        self.kb.barrier()
        ar.off = mark
        NB = 256
        rs = {}
        rs["x"] = Rot([A([128, D], F32, "x%d" % i) for i in range(2)])
        rs["sm"] = Rot([A([128, 16], F32, "sm%d" % i) for i in range(4)])
        rs["jk"] = Rot([A([128, D], BF16, "jk%d" % i) for i in range(1)])
        rs["xn"] = Rot([A([128, D], BF16, "xn%d" % i) for i in range(2)])
        rs["pst"] = Rot([(self.PS[0], self.PB[0])])
        h2t, h2t_b = A([128, 8, NB], BF16, "h2t")
        gbr = Rot([A([128, NB + 2], F32, "gb%d" % i) for i in range(3)])
        a1r = Rot([A([128, NB], F32, "a1_%d" % i) for i in range(3)])
        a2r = Rot([A([128, NB], F32, "a2_%d" % i) for i in range(3)])
        actt, actt_b = A([128, NFC, NB], BF16, "actt")
        tmp = Rot([A([128, 512], F32, "t%d" % i) for i in range(2)])
        jk5 = rs["jk"]
        psg = Rot([(self.PS[i], self.PB[i]) for i in (1, 2)])
        psu = Rot([(self.PS[i], self.PB[i]) for i in (3, 4, 5, 6)])
        pso = Rot([(self.PS[i], self.PB[i]) for i in (7, 0)])
        blocks = [[(i, 128, i * 128) for i in range(b * 2, b * 2 + 2)] for b in range(NT // 2)] + [[(NT, 64, T)]]
        for bi, blk in enumerate(blocks):
            is_s = blk[0][0] == NT
            n = 64 if is_s else NB
            cb0 = blk[0][2]
            for (i, rows, c0) in blk:
                x, x_b = rs["x"].next()
                self.dma(x[0:rows, :], xsrc[c0:c0 + rows, :], w=[x_b])
                self.norm_tile(x, x_b, rows, h2t, h2t_b, c0 - cb0, afm[0], bfm[0], fm_b, is_s, rs)
            tasks = []
            for fc in range(NFC):
                cell = {}

                def s1(fc=fc, cell=cell, n=n, is_s=is_s):
                    pg, pgb = psg.next()
                    pu, pub = psu.next()
                    for kc in range(8):
                        self.mm(pg[:, 0:n], wg[0][:, kc * DFF + fc * 128: kc * DFF + fc * 128 + 128], h2t[:, kc, 0:n], kc == 0, kc == 7,
                                [wg[1], h2t_b], [pgb])
                    for kc in range(8):
                        self.mm(pu[:, 0:n], wu[0][:, kc * DFF + fc * 128: kc * DFF + fc * 128 + 128], h2t[:, kc, 0:n], kc == 0, kc == 7,
                                [wu[1], h2t_b], [pub])
                    gb, gb_b = gbr.next()
                    self.act(gb[:, 2:2 + n], pg[:, 0:n], AF.Copy, [pgb], [gb_b])
                    if not is_s:
                        self.cp("pool", gb[:, 0:2], halo[:, fc, 0, :], [halo_b, gb_b], [gb_b])
                        self.cp("pool", halo[:, fc, 0, :], gb[:, n:n + 2], [gb_b], [halo_b])
                    else:
                        self.cp("pool", gb[:, 0:2], halo[:, fc, 1, :], [halo_b, gb_b], [gb_b])
                        self.cp("pool", gb[:, 32:34], halo[:, fc, 2, :], [halo_b, gb_b], [gb_b])
                        self.cp("pool", halo[:, fc, 1, :], gb[:, 16:18], [gb_b], [halo_b])
                        self.cp("pool", halo[:, fc, 2, :], gb[:, 48:50], [gb_b], [halo_b])
                    a1, a1_b = a1r.next()
                    self.ts("pool", a1[:, 0:n], gb[:, 0:n], cw[0][:, fc, 0:1], cbi[0][:, fc:fc + 1], ALU.mult, ALU.add,
                            [gb_b, cw[1], cbi[1]], [a1_b])
                    cell.update(pu=(pu, pub), gb=(gb, gb_b), a1=(a1, a1_b))

                def s2(fc=fc, cell=cell, n=n):
                    gb, gb_b = cell["gb"]; a1, a1_b = cell["a1"]
                    a2, a2_b = a2r.next()
                    self.stt("dve", a2[:, 0:n], gb[:, 1:n + 1], cw[0][:, fc, 1:2], a1[:, 0:n], ALU.mult, ALU.add, [gb_b, cw[1], a1_b], [a2_b])
                    self.stt("dve", a1[:, 0:n], gb[:, 2:n + 2], cw[0][:, fc, 2:3], a2[:, 0:n], ALU.mult, ALU.add, [gb_b, cw[1], a2_b], [a1_b])
                    self.act(a2[:, 0:n], a1[:, 0:n], AF.Silu, [a1_b], [a2_b])
                    cell["a2"] = (a2, a2_b)

                def s3(fc=fc, cell=cell, n=n):
                    a2, a2_b = cell["a2"]; pu, pub = cell["pu"]
                    self.tt("dve", actt[:, fc, 0:n], a2[:, 0:n], pu[:, 0:n], ALU.mult, [a2_b, pub], [actt_b])
                tasks.append((s1, s2, s3))
            self.run_pipeline(tasks, 1, 1)
            for (i, rows, c0) in blk:
                lc = c0 - cb0
                x, x_b = rs["x"].next()
                self.dma(x[0:rows, :], xsrc[c0:c0 + rows, :], w=[x_b])
                g_ap, g_b = (gs if is_s else gp)
                dst = xdst_s[0:64, :] if is_s else xdst_p[c0:c0 + rows, :]
                self.resid_out(lambda kc, half: (actt[:, kc, lc:lc + rows], wd[0][:, kc * D + half * 512: kc * D + half * 512 + 512]),
                               NFC, [actt_b, wd[1]], rows, x, x_b, g_ap, g_b, pso, rs["sm"], jk5, tmp, dst)
        self.dma(d["conv_p"][l], halo[:, :, 0, :], r=[halo_b])
        for s_ in range(2):
            self.dma(d["conv_s"][l, s_], halo[:, :, 1 + s_, :], r=[halo_b])
        self.kb.barrier()

    def phase_pre1(self, xsrc_p, xsrc_s):
        T, NT, TOK = self.T, self.NT, self.TOK
        d = self.dr
        rs, A = self.pre_common(1)
        win = A([128, 8 * 3072], BF16, "win")
        self.load_w(win[0], win[1], d["win1"], 8 * 3072, rs["stg"])
        o512 = Rot([A([128, 512], F32, "o512_%d" % i) for i in range(3)])
        vst = [A([128, 8, 65], BF16, "vst%d" % i) for i in range(3)]
        for v, vb in vst:
            self.memset("pool", v[:], 1.0, [vb])
        vrot = Rot(vst)
        fst = Rot([A([128, 512], BF16, "fst%d" % i) for i in range(3)])
        ptasks = []
        for blk in self.blocks():
            cell = {}

            def blk_s1(blk=blk, cell=cell):
                is_s = blk[0][0] == NT
                n = 64 if is_s else 512
                cb0 = blk[0][2]
                htb, htb_b = rs["htb"].next()
                for (i, rows, c0) in blk:
                    x, x_b = rs["x"].next()
                    self.dma(x[0:rows, :], self.xrows(xsrc_p, xsrc_s, i, rows), w=[x_b])
                    self.norm_tile(x, x_b, rows, htb, htb_b, c0 - cb0, rs["afm"], rs["bfm"], rs["fm_b"], is_s, rs)
                cell["htb"] = (htb, htb_b)

            def blk_s2(blk=blk, cell=cell):
                is_s = blk[0][0] == NT
                n = 64 if is_s else 512
                cb0 = blk[0][2]
                htb, htb_b = cell["htb"]
                for (i, rows, c0) in blk:
                    lc = c0 - cb0
                    rsl = slice(i * 128, i * 128 + rows) if not is_s else slice(0, 64)
                    sfx = "_s" if is_s else "_p"
                    for (wc, oname, vname) in ((512, "c_k", None), (1024, "c_v", "vs_a"), (2048, "d_k", None), (2560, "d_v", "vs_b")):
                        ps, pb = rs["psa"].next()
                        for kc in range(8):
                            self.mm(ps[0:rows, :], htb[:, kc, lc:lc + rows], win[0][:, kc * 3072 + wc: kc * 3072 + wc + 512], kc == 0, kc == 7,
                                    [htb_b, win[1]], [pb])
                        self.tm_out(ps, pb, rows, 512, d[oname + sfx][rsl, :], o512, "act")
                        if vname is not None:
                            self.v_store(ps, pb, rows, vrot, d[vname][c0:c0 + rows, :])
                for pr in range(4):
                    for (wc, dn, sc_) in ((0, "qs_c", 0.125), (512, "ks_c", 1.0), (1536, "qs_b", 0.125), (2048, "ks_b", 1.0)):
                        self.fm_proj(win[0], win[1], wc + pr * 128, 8, 3072, lambda kc: htb[:, kc, 0:n], htb_b, n, 128, rs, sc_,
                                     d[dn][pr * 128:(pr + 1) * 128, cb0:cb0 + n], fst)
            ptasks.append((blk_s1, blk_s2))
        self.run_pipeline(ptasks, 1, 0)

        ld = A([128, 8, 512], F32, "pld"); bf = A([128, 8, 512], BF16, "pbf")
        vpast = A([128, 8, 8, 65], BF16, "vpast")
        self.memset("pool", vpast[0][:], 1.0, [vpast[1]])
        for s_ in range(2):
            pc0 = TOK + s_ * PAST
            self.past_kT(d["cc_k"][s_], 4, lambda pr, c, w_: d["ks_c"][pr * 128:(pr + 1) * 128, pc0 + c:pc0 + c + w_], rs, ld, bf, fst)
            self.past_v(d["cc_v"][s_], 4, d["vs_a"][pc0:pc0 + CPAST, :], ld, vpast)
            self.past_kT(d["cd_k"][s_], 8, lambda pr, c, w_: d["ks_b"][pr * 128:(pr + 1) * 128, pc0 + c:pc0 + c + w_], rs, ld, bf, fst)
            self.past_v(d["cd_v"][s_], 8, d["vs_b"][pc0:pc0 + PAST, :], ld, vpast)
        self.kb.barrier()

    def phase_att1(self):
        T, NT, TOK = self.T, self.NT, self.TOK
        d = self.dr
        res, A = self.att_alloc()
        idb, idb_b = self.P("ident_bf"); idf, idf_b = self.P("ident_f")
        msb, msb_b = self.P("m_sb")
        nu, nu_b = self.P("nu"); nl, nl_b = self.P("nl"); nones, nones_b = self.P("nones")
        mdt, mdt_b = A([128, 5, 128], F32, "md")
        self.dma(mdt, d["md"].rearrange("e k q -> k e q"), w=[mdt_b])
        bdh, bdh_b = A([128, 40, 128], BF16, "bdh")
        bdl, bdl_b = A([128, 40, 128], BF16, "bdl")
        bdt = Rot([A([128, 5, 128], F32, "bdt%d" % i) for i in range(2)])
        for g in range(8):
            bd, bd_b = bdt.next()
            sl = slice(g * 5, g * 5 + 5)
            self.dma(bd, d["bd"][g].rearrange("e k q -> k e q"), w=[bd_b])
            for e in (0, 4):
                self.tt("pool", bd[:, e, :], bd[:, e, :], mdt[:, e, :], ALU.add, [bd_b, mdt_b], [bd_b])
            self.cp("dve", bdh[:, sl, :], bd[:, :, :], [bd_b], [bdh_b])
            self.tt("pool", bd[:, :, :], bd[:, :, :], bdh[:, sl, :], ALU.subtract, [bd_b, bdh_b], [bd_b])
            self.cp("dve", bdl[:, sl, :], bd[:, :, :], [bd_b], [bdl_b])
        et = Rot([A([128, 512], F32, "e%d" % i) for i in range(5)])
        spt = Rot([A([128, 512], BF16, "sp%d" % i) for i in range(6)])
        ext = Rot([A([128, 512], F32, "ex%d" % i) for i in range(2)])
        es_r = Rot([A([128, 144], F32, "es%d" % i) for i in range(2)])
        sps_r = Rot([A([128, 144], BF16, "sps%d" % i) for i in range(2)])
        exs_r = Rot([A([128, 16], F32, "exs%d" % i) for i in range(2)])
        ws_r = Rot([A([128, 16], BF16, "ws%d" % i) for i in range(3)])
        oacc_r = Rot([A([128, 64], F32, "oacc%d" % i) for i in range(2)])
        ps7, pb7 = self.PS[7], self.PB[7]

        def band_tasks(h, c):
            qv, kv, vt, vt_b, obuf, obuf_b, hh = c["qv"], c["kv"], c["vt"], c["vt_b"], c["obuf"], c["obuf_b"], c["hh"]
            rb = [c["kt_b"], c["qt_b"]]
            tasks = []
            for i in range(NT):
                ktl = []
                for j in range(max(0, i - 4), i + 1):
                    e = i - j
                    ktl.append(dict(nk=128, pbase=0, clo=0, kT=kv[:, j * 128:(j + 1) * 128], rb=rb, v=vt[:, j, hh, :], vb=[vt_b],
                                    masks=[(0, 128, idb[:], bdh[:, h * 5 + e, :], [idb_b, bdh_b]),
                                           (0, 128, idb[:], bdl[:, h * 5 + e, :], [idb_b, bdl_b])]))
                tasks += self.softmax_tasks(qv[:, i * 128:(i + 1) * 128], 64, 128, [(0, 128, 0)], ktl, res,
                                            lambda qi, i=i: (obuf[:, i, hh * 64:(hh + 1) * 64], obuf_b))
            for s_ in range(2):
                b0 = 32 * s_
                qc = T + b0
                ktl = []
                for m in range(4):
                    kc0 = TOK + s_ * PAST + m * 128
                    e = 1 if m == 3 else 2
                    ktl.append(dict(nk=128, pbase=0, clo=0, kT=kv[:, kc0:kc0 + 128], rb=rb,
                                    v=vt[:, NT + 1 + s_ * 8 + m, hh, :], vb=[vt_b],
                                    masks=[(0, 16, idb[:], bdh[:, h * 5 + e, 0:16], [idb_b, bdh_b]),
                                           (0, 16, idb[:], bdl[:, h * 5 + e, 0:16], [idb_b, bdl_b])]))
                ktl.append(dict(nk=16, pbase=b0, clo=0, kT=kv[:, qc:qc + 16], rb=rb, v=vt[b0:b0 + 16, NT, hh, :], vb=[vt_b],
                                masks=[(0, 16, idb[:, b0:b0 + 16], bdh[:, h * 5, b0:b0 + 16], [idb_b, bdh_b]),
                                       (0, 16, idb[:, b0:b0 + 16], bdl[:, h * 5, b0:b0 + 16], [idb_b, bdl_b])]))
                tasks += self.softmax_tasks(qv[:, qc:qc + 16], 64, 16, [(0, 16, b0)], ktl, res,
                                            lambda qi, b0=b0: (obuf[b0:b0 + 16, NT, hh * 64:(hh + 1) * 64], obuf_b))
            return tasks
        self.att_drive(res, "band", ("qs_c", "ks_c", "vs_a"), band_tasks, 0)

        def sb_tasks(h, c):
            qv, kv, vt, vt_b, obuf, obuf_b, hh = c["qv"], c["kv"], c["vt"], c["vt_b"], c["obuf"], c["obuf_b"], c["hh"]
            rb = [c["kt_b"], c["qt_b"]]
            tasks = []
            for qb in range(NT // 4):
                qc = qb * 512
                st = {"firstx": True, "firsto": True, "prev": None, "xb": None, "ob": None}
                js = list(range(4 * qb + 3, -1, -1))
                for ji, j in enumerate(js):
                    r = j - 4 * qb
                    clo = max(r, 0) * 128
                    cell = {}

                    def s1(j=j, r=r, clo=clo, cell=cell, st=st, ji=ji, qc=qc):
                        if ji == 0:
                            st["xb"] = res["xbank"].next()
                            st["ob"] = res["obank"].next()
                        zb, zb_b = res["sbank"].next()
                        self.mm(zb[:, clo:512], kv[:, j * 128:(j + 1) * 128], qv[:, qc + clo:qc + 512], True, r < 0, rb, [zb_b],
                                skip_group_check=r >= 0)
                        if r >= 0:
                            self.mm(zb[:, clo:clo + 128], idb[:], msb[:], False, True, [idb_b, msb_b], [zb_b], skip_group_check=True)
                        e_, e_b = et.next()
                        sp_, sp_b = spt.next()
                        self.act(e_[:, clo:512], zb[:, clo:512], AF.Exp, [zb_b], [e_b])
                        self.act(sp_[:, clo:512], e_[:, clo:512], AF.Ln, [e_b], [sp_b], scale=1.0, bias=1.0)
                        cell["e"] = (e_, e_b); cell["sp"] = (sp_, sp_b)

                    def s2(j=j, r=r, clo=clo, cell=cell, st=st, ji=ji):
                        xb, xb_b = st["xb"]
                        e_, e_b = cell["e"]; sp_, sp_b = cell["sp"]
                        if st["prev"] is not None:
                            psp, psp_b, pclo = st["prev"]
                            self.mm(xb[:, pclo:512], nl[:], psp[:, pclo:512], st["firstx"], False, [nl_b, psp_b], [xb_b], skip_group_check=True)
                            st["firstx"] = False
                        self.mm(xb[:, clo:512], nu[:], sp_[:, clo:512], st["firstx"], False, [nu_b, sp_b], [xb_b], skip_group_check=True)
                        st["firstx"] = False
                        st["prev"] = (sp_, sp_b, clo)
                        ex_, ex_b = ext.next()
                        self.act(ex_[:, clo:512], xb[:, clo:512], AF.Exp, [xb_b], [ex_b])
                        w_, w_b = res["pt"].next()
                        self.tt("dve", w_[:, clo:512], ex_[:, clo:512], e_[:, clo:512], ALU.mult, [ex_b, e_b], [w_b])
                        cell["w"] = (w_, w_b)

                    def s3(j=j, r=r, cell=cell, st=st, last=(ji == len(js) - 1), qb=qb):
                        ob, ob_b = st["ob"]
                        w_, w_b = cell["w"]
                        for m in range(max(r, 0), 4):
                            self.mm(ob[:, m * 65:m * 65 + 65], w_[:, m * 128:(m + 1) * 128], vt[:, j, hh, :], st["firsto"], False,
                                    [w_b, vt_b], [ob_b], skip_group_check=True)
                            st["firsto"] = False
                        if last:
                            for m in range(4):
                                self.cp("dve", obuf[:, 4 * qb + m, hh * 64:(hh + 1) * 64], ob[:, m * 65:m * 65 + 64], [ob_b], [obuf_b])
                    tasks.append((s1, s2, s3))

            def samp(s_):
                b0 = 32 * s_
                qc = T + b0
                e_, e_b = es_r.next()
                sp_, sp_b = sps_r.next()
                oa, oa_b = oacc_r.next()
                tl = []
                for m in range(8):
                    kc0 = TOK + s_ * PAST + m * 128
                    tl.append((kv[:, kc0:kc0 + 128], 0, 128, vt[:, NT + 1 + s_ * 8 + m, hh, :]))
                tl.append((kv[:, qc:qc + 16], b0, 16, vt[b0:b0 + 16, NT, hh, :]))
                for t, (kT, pb0, nk, v_) in enumerate(tl):
                    self.mm(ps7[pb0:pb0 + nk, 256:272], kT, qv[:, qc:qc + 16], True, t < 8, rb, [pb7], skip_group_check=t == 8)
                    if t == 8:
                        self.mm(ps7[pb0:pb0 + nk, 256:272], idb[:, b0:b0 + 16], msb[:, b0:b0 + 16], False, True,
                                [idb_b, msb_b], [pb7], skip_group_check=True)
                    self.act(e_[pb0:pb0 + nk, t * 16:t * 16 + 16], ps7[pb0:pb0 + nk, 256:272], AF.Exp, [pb7], [e_b])
                    self.act(sp_[pb0:pb0 + nk, t * 16:t * 16 + 16], e_[pb0:pb0 + nk, t * 16:t * 16 + 16], AF.Ln, [e_b], [sp_b],
                             scale=1.0, bias=1.0)
                for t, (kT, pb0, nk, v_) in enumerate(tl):
                    self.mm(ps7[pb0:pb0 + nk, 0:16], nu[pb0:pb0 + nk, pb0:pb0 + nk], sp_[pb0:pb0 + nk, t * 16:t * 16 + 16], True, t == 8,
                            [nu_b, sp_b], [pb7], skip_group_check=t < 8)
                    if t < 8:
                        for t2 in range(t + 1, 9):
                            p2, n2 = tl[t2][1], tl[t2][2]
                            self.mm(ps7[0:128, 0:16], nones[p2:p2 + n2, 0:128], sp_[p2:p2 + n2, t2 * 16:t2 * 16 + 16], False, t2 == 8,
                                    [nones_b, sp_b], [pb7], skip_group_check=True)
                    ex_, ex_b = exs_r.next()
                    self.act(ex_[pb0:pb0 + nk, 0:16], ps7[pb0:pb0 + nk, 0:16], AF.Exp, [pb7], [ex_b])
                    w_, w_b = ws_r.next()
                    self.tt("dve", w_[pb0:pb0 + nk, 0:16], ex_[pb0:pb0 + nk, 0:16], e_[pb0:pb0 + nk, t * 16:t * 16 + 16], ALU.mult,
                            [ex_b, e_b], [w_b])
                    self.mm(ps7[b0:b0 + 16, 64:128], w_[pb0:pb0 + nk, 0:16], v_[:, 0:64], True, True, [w_b, vt_b], [pb7])
                    if t == 0:
                        self.cp("dve", oa[b0:b0 + 16, :], ps7[b0:b0 + 16, 64:128], [pb7], [oa_b])
                    else:
                        self.tt("dve", oa[b0:b0 + 16, :], oa[b0:b0 + 16, :], ps7[b0:b0 + 16, 64:128], ALU.add, [pb7, oa_b], [oa_b])
                self.cp("dve", obuf[b0:b0 + 16, NT, hh * 64:(hh + 1) * 64], oa[b0:b0 + 16, :], [oa_b], [obuf_b])
            for s_ in range(2):
                tasks.append((lambda s_=s_: samp(s_), lambda: None, lambda: None))
            return tasks

        self.att_drive(res, "sb", ("qs_b", "ks_b", "vs_b"), sb_tasks, 512)
        self.kb.barrier()

    def build(self, consts, shared, core):
        self.setup(consts, shared, core)
        d = self.dr
        T = self.T
        import os
        stop = int(os.environ.get("KSTOP", "99"))
        phases = [lambda: self.phase_pre0(d["xp"], d["xs"]),
                  lambda: self.phase_att0(),
                  lambda: self.phase_proj(0, d["xp"], d["xs"], d["x1"]),
                  lambda: self.phase_ffn(0, d["x1"], d["x2"][0:T, :], d["x2"][T:T + 64, :]),
                  lambda: self.phase_pre1(d["x2"][0:T, :], d["x2"][T:T + 64, :]),
                  lambda: self.phase_att1(),
                  lambda: self.phase_proj(1, d["x2"][0:T, :], d["x2"][T:T + 64, :], d["x1"]),
                  lambda: self.phase_ffn(1, d["x1"], d["y_p"], d["y_s"])]
        pnames = ["pre0", "att0", "proj0", "ffn0", "pre1", "att1", "proj1", "ffn1"]
        for i, ph in enumerate(phases):
            if i < stop:
                self.kb.phase = pnames[i]
                ph()
        print("ops:", {e: len(v) for e, v in self.kb.ops.items()}, "dma:", self.kb.dma_n, flush=True)
        self.kb.finish()
        self.kb.emit()
        return self.nc


def kernel(**inputs):
    inp = {k: np.asarray(v) for k, v in inputs.items()}
    B, T = inp["x_prompt"].shape[0], inp["x_prompt"].shape[1]
    ncores = 8
    assert B == ncores
    consts = _consts(T)
    shared = _shared_inputs(inp)
    cores = []
    for b in range(ncores):
        m = _core_inputs(inp, b, T)
        stc = m.pop("st_conv")
        m["st_conv_fm"] = np.ascontiguousarray(stc.reshape(2, 2, 2, NFC, 128).transpose(0, 1, 4, 3, 2))
        cores.append(m)
    prog = Prog(T)
    nc = prog.build(consts, shared, cores[0])
    in_maps = []
    for b in range(ncores):
        m = {}
        m.update(consts)
        m.update(shared)
        m.update(cores[b])
        in_maps.append(m)
    res = run_bass_kernel_spmd(nc, in_maps, core_ids=list(range(ncores)))
    R = res.results

    def samp(name, tail):
        out = []
        for b in range(ncores):
            a = R[b][name]
            out.append(a[0:16]); out.append(a[32:48])
        return np.stack(out, 0).reshape((2 * ncores, 16) + tail)

    def prm(name, tail, last=None):
        a = np.stack([R[b][name] for b in range(ncores)], 0)
        if last is not None:
            a = a[:, T - last:]
        return a.reshape((ncores, a.shape[1]) + tail)

    keep = min(512, T)
    outs = [prm("y_p", (D,)), samp("y_s", (D,)),
            prm("a_ckv_p", (128,))[None], samp("a_ckv_s", (128,))[None],
            prm("a_kr_p", (32,))[None], samp("a_kr_s", (32,))[None],
            prm("b_k_p", (8, 64))[None], samp("b_k_s", (8, 64))[None],
            prm("b_v_p", (8, 64))[None], samp("b_v_s", (8, 64))[None],
            prm("b_lf_p", (8,))[None], samp("b_lf_s", (8,))[None],
            prm("c_k_p", (8, 64), keep)[None], samp("c_k_s", (8, 64))[None],
            prm("c_v_p", (8, 64), keep)[None], samp("c_v_s", (8, 64))[None],
            prm("d_k_p", (8, 64))[None], samp("d_k_s", (8, 64))[None],
            prm("d_v_p", (8, 64))[None], samp("d_v_s", (8, 64))[None]]
    cp_ = np.stack([R[b]["conv_p"] for b in range(ncores)], 1)
    outs.append(np.ascontiguousarray(cp_.transpose(0, 1, 4, 3, 2).reshape(2, ncores, 2, DFF)))
    cs_ = np.stack([R[b]["conv_s"] for b in range(ncores)], 1)
    cs_ = cs_.transpose(0, 1, 2, 5, 4, 3).reshape(2, 2 * ncores, 2, DFF)
    outs.append(np.ascontiguousarray(cs_))
    return tuple(np.ascontiguousarray(o, dtype=np.float32) for o in outs)
```

```python
import numpy as np
import ml_dtypes
import concourse.bass as bass
import concourse.mybir as mybir
from concourse.bass_utils import run_bass_kernel_spmd

F32 = mybir.dt.float32
BF16 = mybir.dt.bfloat16
AF = mybir.ActivationFunctionType
ALU = mybir.AluOpType
AX = mybir.AxisListType
NPBF = ml_dtypes.bfloat16


class Buf:
    __slots__ = ("w", "r", "excl", "name")

    def __init__(self, name="", excl=False):
        self.w = {}
        self.r = {}
        self.excl = excl
        self.name = name


class KB:
    ENG = ("pe", "act", "dve", "pool", "sp")
    NDS = 8

    def __init__(self, nc):
        self.nc = nc
        self.ops = {e: [] for e in self.ENG}
        self.seen = {e: {} for e in self.ENG}
        self.dma_n = {e: 0 for e in self.ENG}
        self.phase = "setup"
        self.ophase = {e: [] for e in self.ENG}

    @staticmethod
    def _kv(t):
        if t[0] == "E":
            return ("E", t[1]), t[2]
        return ("D", t[1], t[2]), t[3]

    def _deps(self, eng, reads, writes):
        toks = []
        for b in reads:
            toks.extend(b.w.values())
            if b.excl:
                for k, t in b.r.items():
                    if k != ("E", eng):
                        toks.append(t)
        for b in writes:
            toks.extend(b.w.values())
            toks.extend(b.r.values())
        best = {}
        for t in toks:
            k, v = self._kv(t)
            if k == ("E", "pe") and eng == "pe":
                continue
            if self.seen[eng].get(k, -1) >= v:
                continue
            if k not in best or best[k][1] < v:
                best[k] = (t, v)
        waits = []
        for k, (t, v) in best.items():
            self.seen[eng][k] = v
            waits.append(t)
        return waits

    def op(self, eng, fn, reads=(), writes=()):
        waits = self._deps(eng, reads, writes)
        idx = len(self.ops[eng])
        tok = ("E", eng, idx)
        self.ops[eng].append((fn, waits, None))
        self.ophase[eng].append(self.phase)
        for b in reads:
            b.r[("E", eng)] = tok
        for b in writes:
            b.w[("E", eng)] = tok
            b.r = {}
        return tok

    def dma(self, eng, out, in_, reads=(), writes=(), **kw):
        n = self.dma_n[eng]
        self.dma_n[eng] += 1
        slot = n % self.NDS
        val = 16 * (n // self.NDS + 1)
        waits = self._deps(eng, reads, writes)
        k = ("D", eng, slot)
        if n >= self.NDS and self.seen[eng].get(k, -1) < val - 16:
            waits.append(("D", eng, slot, val - 16))
            self.seen[eng][k] = val - 16
        tok = ("D", eng, slot, val)
        self.ops[eng].append((lambda h: h.dma_start(out=out, in_=in_, **kw), waits, slot))
        self.ophase[eng].append(self.phase)
        for b in reads:
            b.r[k] = tok
        for b in writes:
            b.w[k] = tok
            b.r = {}
        return tok

    def finish(self):
        self.barrier()

    def emit(self):
        nc = self.nc
        tg = {e: set() for e in self.ENG}
        for e in self.ENG:
            for (_, waits, _) in self.ops[e]:
                for t in waits:
                    if t[0] == "E":
                        tg[t[1]].add(t[2])
        rank = {e: {idx: i + 1 for i, idx in enumerate(sorted(tg[e]))} for e in self.ENG}
        esem = {e: nc.alloc_semaphore("es_" + e) for e in self.ENG}
        dsem = {e: [nc.alloc_semaphore("ds_%s%d" % (e, i)) for i in range(self.NDS)]
                for e in self.ENG if self.dma_n[e] > 0}

        import os
        scoped = bool(os.environ.get("KSCOPE"))

        def run(e, h):
            cur = [None, None]
            for idx, (fn, waits, d) in enumerate(self.ops[e]):
                if scoped and self.ophase[e][idx] != cur[0]:
                    if cur[1] is not None:
                        cur[1].__exit__(None, None, None)
                    cur[0] = self.ophase[e][idx]
                    cur[1] = nc.named_scope(cur[0])
                    cur[1].__enter__()
                for t in waits:
                    if t[0] == "E":
                        h.wait_ge(esem[t[1]], rank[t[1]][t[2]])
                    else:
                        h.wait_ge(dsem[t[1]][t[2]], t[3])
                if fn is None:
                    continue
                ins = fn(h)
                if d is not None:
                    ins.then_inc(dsem[e][d], 16)
                elif idx in rank[e]:
                    ins.then_inc(esem[e], 1)
            if cur[1] is not None:
                cur[1].__exit__(None, None, None)

        with nc.Block() as block:
            @block.tensor
            def _(h):
                run("pe", h)

            @block.scalar
            def _(h):
                run("act", h)

            @block.vector
            def _(h):
                run("dve", h)

            @block.gpsimd
            def _(h):
                run("pool", h)

            @block.sync
            def _(h):
                run("sp", h)


D = 1024
DFF = 2816
NFC = DFF // 128
PAST = 1024
CPAST = 512
EPS = 1e-6
NEG = -30000.0


class Arena:
    def __init__(self, t, n):
        self.t, self.n, self.off = t, n, 0

    def reset(self):
        self.off = 0

    def alloc(self, shape, dt, name=""):
        per = int(np.prod(shape[1:]))
        nb = per * (2 if dt == F32 else 1)
        nb = (nb + 15) // 16 * 16
        assert self.off + nb <= self.n, ("arena overflow", name, self.off, nb, self.n)
        v = self.t[:, self.off:self.off + nb]
        self.off += nb
        v = v.bitcast(F32)[:, 0:per] if dt == F32 else v[:, 0:per]
        if len(shape) == 3:
            v = v.rearrange("p (a b) -> p a b", b=shape[2])
        elif len(shape) == 4:
            v = v.rearrange("p (a b c) -> p a b c", b=shape[2], c=shape[3])
        if shape[0] < 128:
            v = v[0:shape[0]]
        return v, Buf(name)


class Rot:
    def __init__(self, items):
        self.items, self.i = items, 0

    def next(self):
        it = self.items[self.i % len(self.items)]
        self.i += 1
        return it


def _barrier(kb):
    toks = []
    for e in kb.ENG:
        n = kb.dma_n[e]
        for slot in range(min(n, kb.NDS)):
            last = ((n - 1 - slot) // kb.NDS) * kb.NDS + slot
            toks.append(("D", e, slot, 16 * (last // kb.NDS + 1)))
        if kb.ops[e]:
            j = len(kb.ops[e]) - 1
            while j >= 0 and (kb.ops[e][j][0] is None or kb.ops[e][j][2] is not None):
                j -= 1
            if j >= 0:
                toks.append(("E", e, j))
    for e in kb.ENG:
        waits = []
        for t in toks:
            k, v = kb._kv(t)
            if k == ("E", e):
                continue
            if kb.seen[e].get(k, -1) >= v:
                continue
            kb.seen[e][k] = v
            waits.append(t)
        kb.ops[e].append((None, waits, None))
        kb.ophase[e].append(kb.phase)


KB.barrier = _barrier


def _weight_layout(w, kc):
    n = w.shape[1]
    return np.ascontiguousarray(w.reshape(kc, 128, n).transpose(1, 0, 2).reshape(128, kc * n))


def _fm(v):
    return np.ascontiguousarray(v.reshape(-1, 128).T)


def _consts(T):
    NT = T // 128
    NTT = NT + 1
    TOK = T + 64
    c = {}
    k = np.arange(128)[:, None]
    q = np.arange(128)[None, :]
    c["ident_bf"] = np.eye(128, dtype=np.float32).astype(NPBF)
    c["ident_f"] = np.eye(128, dtype=np.float32)
    c["uf"] = (k <= q).astype(np.float32)
    c["onesf"] = np.ones((128, 128), np.float32)
    c["nu"] = (-(k >= q).astype(np.float32)).astype(NPBF)
    c["nl"] = (-(k < q).astype(np.float32)).astype(NPBF)
    c["nones"] = (-np.ones((128, 128), np.float32)).astype(NPBF)
    c["m_mla"] = np.where((k >= 64) & (q < 64), NEG, 0.0).astype(np.float32).astype(NPBF)
    c["m_fox"] = np.where(k > q, NEG, 0.0).astype(np.float32).astype(NPBF)
    c["m_sb"] = np.where(k >= q, NEG, 0.0).astype(np.float32).astype(NPBF)
    selh = np.zeros((128, 8, 128), np.float32)
    for h in range(8):
        selh[h, h, :] = 1.0
    c["selh"] = selh.astype(NPBF)
    pos_tok = np.zeros(TOK, np.float64)
    pos_tok[:T] = np.arange(T)
    for j in range(64):
        pos_tok[T + j] = 1024 + (j % 32)
    half = 16
    inv = 10000.0 ** (-np.arange(half, dtype=np.float32) / half)
    ang = pos_tok.astype(np.float32)[:, None] * inv[None, :]
    cos = np.cos(ang).astype(np.float32)
    sin = np.sin(ang).astype(np.float32)
    ct = np.zeros((128, NTT, 16), np.float32)
    st = np.zeros((128, NTT, 16), np.float32)
    for i in range(NT):
        ct[:, i] = cos[i * 128:(i + 1) * 128]
        st[:, i] = sin[i * 128:(i + 1) * 128]
    ct[:64, NT] = cos[T:T + 64]
    st[:64, NT] = sin[T:T + 64]
    c["ct"] = ct
    c["st"] = st
    sc = np.float32(96.0 ** -0.5)
    c["cos2"] = np.ascontiguousarray(np.concatenate([cos.T, cos.T], 0) * sc)
    c["sin2"] = np.ascontiguousarray(np.concatenate([-sin.T, sin.T], 0) * sc)
    md = np.zeros((5, 128, 128), np.float32)
    md[0] = np.where((k >= 64) & (q < 64), NEG, 0.0)
    md[4] = np.where((k < 64) & (q >= 64), NEG, 0.0)
    c["md"] = md
    return c


def _shared_inputs(inp):
    s = {}
    w = inp["w_in_ab"][0]
    perm = np.concatenate([np.arange(0, 416), np.arange(1952, 1960), np.arange(416, 1952)])
    s["win0"] = _weight_layout(w[:, perm], 8)
    wuq = inp["w_uq"][0]
    swc = []
    for h in range(8):
        swc += list(range(h * 96 + 80, h * 96 + 96)) + list(range(h * 96 + 64, h * 96 + 80))
    s["wuq"] = _weight_layout(np.concatenate([wuq, wuq[:, swc]], 1), 2)
    wukv = inp["w_ukv"][0]
    nope = np.concatenate([np.arange(h * 128, h * 128 + 64) for h in range(8)])
    vv = np.concatenate([np.arange(h * 128 + 64, h * 128 + 128) for h in range(8)])
    s["wukv"] = np.ascontiguousarray(wukv[:, np.concatenate([nope, vv])])
    s["wout0"] = _weight_layout(inp["w_out_ab"][0], 8)
    s["wout1"] = _weight_layout(inp["w_out_cd"][0], 8)
    s["win1"] = _weight_layout(inp["w_in_cd"][0], 8)
    for l in range(2):
        s["wg%d" % l] = _weight_layout(inp["ffn_w_gate"][l], 8)
        s["wu%d" % l] = _weight_layout(inp["ffn_w_up"][l], 8)
        s["wd%d" % l] = _weight_layout(inp["ffn_w_down"][l], 22)
    adaw = np.zeros((2, 6, 128, 8 * 1024), np.float32)
    adabf = np.zeros((2, 6, 128, 8), np.float32)
    for l in range(2):
        for kd in range(6):
            adaw[l, kd] = _weight_layout(inp["ada_w"][l][:, kd * 1024:(kd + 1) * 1024], 8)
            adabf[l, kd] = _fm(inp["ada_b"][l][kd * 1024:(kd + 1) * 1024])
    s["adaw"] = adaw
    s["adabf"] = adabf
    s["adab"] = np.ascontiguousarray(inp["ada_b"])
    s["gpre_fm"] = np.stack([np.stack([_fm(inp["mix_pre_g"][l]), _fm(inp["ffn_pre_g"][l])]) for l in range(2)])
    s["gpost"] = np.stack([np.stack([inp["mix_post_g"][l], inp["ffn_post_g"][l]]) for l in range(2)])
    s["b_f"] = np.ascontiguousarray(inp["b_f"][0])
    s["q_a_g"] = np.ascontiguousarray(inp["q_a_g"][0])
    s["kv_a_g"] = np.ascontiguousarray(inp["kv_a_g"][0])
    cw = np.zeros((2, 128, NFC, 3), np.float32)
    cb = np.zeros((2, 128, NFC), np.float32)
    for l in range(2):
        cw[l] = inp["ffn_conv_w"][l].T.reshape(NFC, 128, 3).transpose(1, 0, 2)
        cb[l] = inp["ffn_conv_b"][l].reshape(NFC, 128).T
    s["convw"] = cw
    s["convb"] = cb
    tab = inp["rel_bias_c"][0]
    kk = np.arange(128)[:, None]
    qq = np.arange(128)[None, :]
    bd = np.zeros((8, 5, 128, 128), np.float32)
    for d in range(5):
        idx = np.clip(128 * d + qq - kk, -128, 128) + 128
        for h in range(8):
            bd[h, d] = tab[idx, h]
    s["bd"] = bd
    return s


def _core_inputs(inp, b, T):
    m = {}
    m["xp"] = np.ascontiguousarray(inp["x_prompt"][b])
    xs = np.zeros((64, D), np.float32)
    xs[0:16] = inp["x_sample"][2 * b]
    xs[32:48] = inp["x_sample"][2 * b + 1]
    m["xs"] = xs
    cc = np.stack([inp["c_prompt"][b], inp["c_sample"][2 * b], inp["c_sample"][2 * b + 1]], 0)
    m["ct3"] = np.ascontiguousarray(cc.reshape(3, 8, 128).transpose(2, 1, 0))
    sl = slice(2 * b, 2 * b + 2)
    m["ca_ckv"] = np.ascontiguousarray(inp["cache_a_ckv"][0, sl])
    m["ca_kr"] = np.ascontiguousarray(inp["cache_a_krope"][0, sl])
    m["cb_k"] = np.ascontiguousarray(inp["cache_b_k"][0, sl].reshape(2, PAST, 512))
    m["cb_v"] = np.ascontiguousarray(inp["cache_b_v"][0, sl].reshape(2, PAST, 512))
    m["cb_lf"] = np.ascontiguousarray(inp["cache_b_logf"][0, sl])
    m["cc_k"] = np.ascontiguousarray(inp["cache_c_k"][0, sl].reshape(2, CPAST, 512))
    m["cc_v"] = np.ascontiguousarray(inp["cache_c_v"][0, sl].reshape(2, CPAST, 512))
    m["cd_k"] = np.ascontiguousarray(inp["cache_d_k"][0, sl].reshape(2, PAST, 512))
    m["cd_v"] = np.ascontiguousarray(inp["cache_d_v"][0, sl].reshape(2, PAST, 512))
    m["st_conv"] = np.ascontiguousarray(inp["state_ffn_conv"][:, sl])
    return m


class Prog:
    def __init__(self, T, nlayers=2, stop=None):
        self.T = T
        self.NT = T // 128
        self.NTT = self.NT + 1
        self.TOK = T + 64
        self.TOKK = self.TOK + 2 * PAST
        self.VROWS = self.TOK + 2 * PAST
        self.nlayers = nlayers
        self.stop = stop
        self.nc = bass.Bass("TRN2", target_bir_lowering=False)
        self.kb = KB(self.nc)
        self.dr = {}
        self.shapes = {}

    def din(self, name, arr):
        dt = BF16 if arr.dtype == NPBF else F32
        self.dr[name] = self.nc.dram_tensor(name, list(arr.shape), dt, kind="ExternalInput").ap()
        return self.dr[name]

    def dout(self, name, shape):
        self.dr[name] = self.nc.dram_tensor(name, list(shape), F32, kind="ExternalOutput").ap()
        self.shapes[name] = tuple(shape)
        return self.dr[name]

    def dscr(self, name, shape, dt):
        self.dr[name] = self.nc.dram_tensor(name, list(shape), dt, kind="Internal").ap()
        return self.dr[name]

    def act(self, out, in_, func, r, w, **kw):
        return self.kb.op("act", lambda h: h.activation(out=out, in_=in_, func=func, **kw), r, w)

    def mm(self, out, lhsT, rhs, start, stop, r, w, **kw):
        return self.kb.op("pe", lambda h: h.matmul(out, lhsT=lhsT, rhs=rhs, start=start, stop=stop, **kw), r, w)

    def tr(self, out, in_, ident, r, w):
        return self.kb.op("pe", lambda h: h.matmul(out, lhsT=in_, rhs=ident, start=True, stop=True, is_transpose=True), r, w)

    def tt(self, eng, out, in0, in1, op, r, w):
        return self.kb.op(eng, lambda h: h.tensor_tensor(out=out, in0=in0, in1=in1, op=op), r, w)

    def ts(self, eng, out, in0, s1, s2, op0, op1, r, w):
        if s2 is None:
            return self.kb.op(eng, lambda h: h.tensor_scalar(out=out, in0=in0, scalar1=s1, scalar2=None, op0=op0), r, w)
        return self.kb.op(eng, lambda h: h.tensor_scalar(out=out, in0=in0, scalar1=s1, scalar2=s2, op0=op0, op1=op1), r, w)

    def stt(self, eng, out, in0, scalar, in1, op0, op1, r, w):
        return self.kb.op(eng, lambda h: h.scalar_tensor_tensor(out=out, in0=in0, scalar=scalar, in1=in1,
                                                                op0=op0, op1=op1), r, w)

    def cp(self, eng, out, in_, r, w):
        if eng == "act":
            return self.act(out, in_, AF.Copy, r, w)
        return self.kb.op(eng, lambda h: h.tensor_copy(out=out, in_=in_), r, w)

    def memset(self, eng, ap, val, w):
        return self.kb.op(eng, lambda h: h.memset(ap, val), (), w)

    def dma(self, out, in_, r=(), w=(), **kw):
        return self.kb.dma("sp", out, in_, r, w, **kw)

    def setup(self, consts, shared, core):
        nc = self.nc
        T, NT, NTT, TOK = self.T, self.NT, self.NTT, self.TOK
        for k, v in list(consts.items()) + list(shared.items()) + list(core.items()):
            self.din(k, v)
        o = self.dout
        o("y_p", [T, D]); o("y_s", [64, D])
        o("a_ckv_p", [T, 128]); o("a_ckv_s", [64, 128])
        o("a_kr_p", [T, 32]); o("a_kr_s", [64, 32])
        o("b_k_p", [T, 512]); o("b_k_s", [64, 512])
        o("b_v_p", [T, 512]); o("b_v_s", [64, 512])
        o("b_lf_p", [T, 8]); o("b_lf_s", [64, 8])
        o("c_k_p", [T, 512]); o("c_k_s", [64, 512])
        o("c_v_p", [T, 512]); o("c_v_s", [64, 512])
        o("d_k_p", [T, 512]); o("d_k_s", [64, 512])
        o("d_v_p", [T, 512]); o("d_v_s", [64, 512])
        o("conv_p", [2, 128, NFC, 2]); o("conv_s", [2, 2, 128, NFC, 2])
        s = self.dscr
        s("x1", [TOK, D], F32); s("x2", [TOK, D], F32)
        s("qs_a", [8, 96, TOK], BF16); s("ks_an", [512, self.TOKK], BF16); s("kr", [32, self.TOKK], BF16)
        s("vs_a", [self.VROWS, 8 * 65], BF16)
        s("qs_b", [512, TOK], BF16); s("ks_b", [512, self.TOKK], BF16); s("vs_b", [self.VROWS, 8 * 65], BF16)
        s("os", [TOK, D], BF16)
        s("qs_c", [512, TOK], BF16); s("ks_c", [512, self.TOKK], BF16)
        self.PS = [nc.alloc_psum_tensor("ps%d" % i, [128, 512], F32) for i in range(8)]
        self.PB = [Buf("ps%d" % i, excl=True) for i in range(8)]
        self.pers = {}

        def pt(name, shape, dt, src=None):
            t = nc.alloc_sbuf_tensor("sb_" + name, list(shape), dt)
            b = Buf(name)
            self.pers[name] = (t, b)
            if src is not None:
                self.dma(t[:], src, w=[b])
            return t, b
        d = self.dr
        pt("ident_bf", [128, 128], BF16, d["ident_bf"]); pt("ident_f", [128, 128], F32, d["ident_f"])
        pt("uf", [128, 128], F32, d["uf"]); pt("onesf", [128, 128], F32, d["onesf"])
        pt("nu", [128, 128], BF16, d["nu"]); pt("nl", [128, 128], BF16, d["nl"]); pt("nones", [128, 128], BF16, d["nones"])
        pt("m_mla", [128, 128], BF16, d["m_mla"]); pt("m_fox", [128, 128], BF16, d["m_fox"]); pt("m_sb", [128, 128], BF16, d["m_sb"])
        pt("selh", [128, 8, 128], BF16, d["selh"])
        pt("ct3", [128, 8, 3], F32, d["ct3"])
        pt("condT", [128, 8, 3], BF16)
        pt("ce_p", [128, 8, 128], BF16); pt("ce_s", [128, 8, 64], BF16)
        pt("lall", [128, NTT, 8], F32)
        pt("lpast", [128, 2, 8, 8], F32)
        pt("ncum", [128, NTT, 8], F32)
        pt("ncump", [128, 2, 8, 8], F32)
        pt("cumt", [128, TOK], BF16)
        pt("halo", [128, NFC, 3, 2], F32)
        pt("small", [128, 64], F32)
        na = (nc.sbuf_bytes_remaining - 2048) // 2
        na = na // 16 * 16
        self.arena_t = nc.alloc_sbuf_tensor("arena", [128, na], BF16)
        self.ar = Arena(self.arena_t, na)
        ct3, bct3 = self.pers["ct3"]; condT, bcond = self.pers["condT"]
        cep, bcep = self.pers["ce_p"]; ces, bces = self.pers["ce_s"]
        self.act(condT[:], ct3[:], AF.Silu, [bct3], [bcond])
        self.memset("pool", ces[:], 0.0, [bces])
        for kc in range(8):
            self.cp("dve", cep[:, kc, :], condT[:, kc, 0:1].to_broadcast([128, 128]), [bcond], [bcep])
            self.cp("dve", ces[:, kc, 0:16], condT[:, kc, 1:2].to_broadcast([128, 16]), [bcond, bces], [bces])
            self.cp("dve", ces[:, kc, 32:48], condT[:, kc, 2:3].to_broadcast([128, 16]), [bcond, bces], [bces])
        self.kb.barrier()

    def P(self, name):
        return self.pers[name]

    def load_w(self, dst, bdst, src, ncols, stg):
        c0 = 0
        while c0 < ncols:
            n = min(2048, ncols - c0)
            s_ap, s_b = stg.next()
            self.dma(s_ap[:, 0:n], src[:, c0:c0 + n], w=[s_b])
            self._cast_i = getattr(self, "_cast_i", 0) + 1
            self.cp("dve" if self._cast_i % 2 else "act", dst[:, c0:c0 + n], s_ap[:, 0:n], [s_b], [bdst])
            c0 += n

    def mod_fm(self, l, kinds, stg, wbuf, out_tiles):
        condT, bcond = self.P("condT")
        w_ap, w_b = wbuf
        for kd, (o_ap, o_b) in zip(kinds, out_tiles):
            self.load_w(w_ap, w_b, self.dr["adaw"][l, kd], 8192, stg)
            bias_ap, bias_b = self.ar_small_fm
            self.dma(bias_ap, self.dr["adabf"][l, kd], w=[bias_b])
            ps, pb = self.PS[7], self.PB[7]
            for nck in range(8):
                for kc in range(8):
                    self.mm(ps[:, nck * 4:nck * 4 + 3], w_ap[:, kc * 1024 + nck * 128: kc * 1024 + nck * 128 + 128],
                            condT[:, kc, :], kc == 0, kc == 7, [w_b, bcond], [pb])
            for nck in range(8):
                self.ts("dve", o_ap[:, nck, :], ps[:, nck * 4:nck * 4 + 3], bias_ap[:, nck:nck + 1], None,
                        ALU.add, None, [pb, bias_b], [o_b])

    def mod_bc(self, l, kd, stg, wbuf, gp_ap, gp_b, gs_ap, gs_b, gain_row):
        cep, bcep = self.P("ce_p"); ces, bces = self.P("ce_s")
        w_ap, w_b = wbuf
        self.load_w(w_ap, w_b, self.dr["adaw"][l, kd], 8192, stg)
        t1, b1 = self.ar_bc1
        t2, b2 = self.ar_bc2
        self.dma(t1, self.dr["adab"][l, kd * 1024:(kd + 1) * 1024].partition_broadcast(128), w=[b1])
        self.dma(t2, gain_row.partition_broadcast(128), w=[b2])
        for (ce, bce, rows, g_ap, g_b) in ((cep, bcep, 128, gp_ap, gp_b), (ces, bces, 64, gs_ap, gs_b)):
            for half in range(2):
                ps, pb = self.PS[6 + half], self.PB[6 + half]
                for kc in range(8):
                    self.mm(ps[0:rows, :], ce[:, kc, 0:rows], w_ap[:, kc * 1024 + half * 512: kc * 1024 + half * 512 + 512],
                            kc == 0, kc == 7, [w_b, bce], [pb])
                sl = slice(half * 512, half * 512 + 512)
                self.tt("dve", g_ap[0:rows, sl], ps[0:rows, :], t1[0:rows, sl], ALU.add, [pb, b1], [g_b])
                self.tt("pool", g_ap[0:rows, sl], g_ap[0:rows, sl], t2[0:rows, sl], ALU.mult, [g_b, b2], [g_b])

    def tiles(self):
        return [(i, 128, i * 128) for i in range(self.NT)] + [(self.NT, 64, self.T)]

    def blocks(self):
        bl = [[(i, 128, i * 128) for i in range(b * 4, b * 4 + 4)] for b in range(self.NT // 4)]
        bl.append([(self.NT, 64, self.T)])
        return bl

    def xrows(self, src_p, src_s, i, rows):
        return src_p[i * 128:(i + 1) * 128, :] if i < self.NT else src_s[0:64, :]

    def rstd(self, ss_ap, ss_b, n, col, rows, inv_n):
        sm = ss_ap
        self.act(sm[0:rows, col + n:col + 2 * n], sm[0:rows, col:col + n], AF.Ln, [ss_b], [ss_b], scale=inv_n, bias=self.eps_ap[0:rows, :])
        self.act(sm[0:rows, col + 2 * n:col + 3 * n], sm[0:rows, col + n:col + 2 * n], AF.Exp, [ss_b], [ss_b], scale=-0.5)
        return sm[0:rows, col + 2 * n:col + 3 * n]

    def norm_tile(self, x_ap, x_b, rows, htb, htb_b, c0, afm, bfm, fm_b, is_sample, rs):
        idb, idb_b = self.P("ident_bf")
        sm, sm_b = rs["sm"].next()
        jk, jk_b = rs["jk"].next()
        xn, xn_b = rs["xn"].next()
        self.act(jk[0:rows, :], x_ap[0:rows, :], AF.Square, [x_b], [jk_b, sm_b], accum_out=sm[0:rows, 0:1])
        r = self.rstd(sm, sm_b, 1, 0, rows, 1.0 / D)
        self.ts("dve", xn[0:rows, :], x_ap[0:rows, :], r, None, ALU.mult, None, [x_b, sm_b], [xn_b])
        ps, pb = rs["pst"].next()
        pbf = ps[:].bitcast(BF16)
        for c in range(8):
            self.tr(pbf[:, c * 128:c * 128 + rows], xn[0:rows, c * 128:(c + 1) * 128], idb[0:rows, 0:rows], [xn_b, idb_b], [pb])
        segs = [(0, 128, 0)] if not is_sample else [(0, 32, 1), (32, 32, 2)]
        for c in range(8):
            for (cs, n, sq) in segs:
                o_ = htb[:, c, c0 + cs:c0 + cs + n]
                i_ = pbf[:, c * 128 + cs:c * 128 + cs + n]
                if c % 2 == 0:
                    self.act(o_, i_, AF.Identity, [pb, fm_b], [htb_b], scale=afm[:, c, sq:sq + 1], bias=bfm[:, c, sq:sq + 1])
                else:
                    self.ts("dve", o_, i_, afm[:, c, sq:sq + 1], bfm[:, c, sq:sq + 1], ALU.mult, ALU.add, [pb, fm_b], [htb_b])

    def pre_common(self, l):
        ar = self.ar
        ar.reset()
        A = lambda shape, dt, name="": ar.alloc(shape, dt, name)
        rs = {}
        rs["stg"] = Rot([A([128, 2048], F32, "stg%d" % i) for i in range(2)])
        self.ar_small_fm = A([128, 8], F32, "adabfm")
        eps, eps_b = A([128, 1], F32, "eps")
        self.memset("pool", eps, EPS, [eps_b])
        self.eps_ap = eps
        afm = A([128, 8, 3], F32, "afm")
        bfm = A([128, 8, 3], F32, "bfm")
        gfm = A([128, 8], F32, "gfm")
        wada = A([128, 8192], BF16, "wada")
        self.mod_fm(l, [1, 0], rs["stg"], wada, [afm, bfm])
        self.dma(gfm[0], self.dr["gpre_fm"][l, 0], w=[gfm[1]])
        for sq in range(3):
            self.stt("dve", afm[0][:, :, sq], afm[0][:, :, sq], 1.0, gfm[0], ALU.add, ALU.mult, [afm[1], gfm[1]], [afm[1]])
        fm_b = Buf("fmb")
        self.cp("dve", bfm[0][:, 0, 0:1], bfm[0][:, 0, 0:1], [afm[1], bfm[1]], [fm_b, bfm[1]])
        rs["afm"], rs["bfm"], rs["fm_b"] = afm[0], bfm[0], fm_b
        rs["x"] = Rot([A([128, D], F32, "x%d" % i) for i in range(3)])
        rs["sm"] = Rot([A([128, 16], F32, "sm%d" % i) for i in range(4)])
        rs["jk"] = Rot([A([128, D], BF16, "jk%d" % i) for i in range(2)])
        rs["xn"] = Rot([A([128, D], BF16, "xn%d" % i) for i in range(2)])
        rs["htb"] = Rot([A([128, 8, 512], BF16, "htb%d" % i) for i in range(2)])
        rs["pst"] = Rot([(self.PS[0], self.PB[0]), (self.PS[1], self.PB[1])])
        rs["psa"] = Rot([(self.PS[i], self.PB[i]) for i in (2, 3, 4, 5)])
        rs["psb"] = Rot([(self.PS[i], self.PB[i]) for i in (6, 7)])
        return rs, A

    def fm_proj(self, w_ap, w_b, wcol, kcn, wstride, rhs_fn, rhs_b, n, M, rs, scale, dst_dram, stg_rot, negate=False):
        ps, pb = rs["psa"].next()
        for kc in range(kcn):
            self.mm(ps[0:M, 0:n], w_ap[:, kc * wstride + wcol: kc * wstride + wcol + M], rhs_fn(kc), kc == 0, kc == kcn - 1,
                    [w_b, rhs_b], [pb])
        st, st_b = stg_rot.next()
        self.act(st[0:M, 0:n], ps[0:M, 0:n], AF.Copy, [pb], [st_b], scale=scale)
        self.dma(dst_dram, st[0:M, 0:n], r=[st_b])

    def past_kT(self, cache_ap, ntile, dst_dram_fn, rs, A_ld, A_bf, stgT):
        idb, idb_b = self.P("ident_bf")
        ld, ld_b = A_ld
        bf, bf_b = A_bf
        self.dma(ld[:, 0:ntile, :], cache_ap.rearrange("(m p) c -> p m c", p=128), w=[ld_b])
        self.cp("pool", bf[:, 0:ntile, :], ld[:, 0:ntile, :], [ld_b], [bf_b])
        for pr in range(4):
            for half in range(ntile // 4):
                ps, pb = rs["pst"].next()
                pbf = ps[:].bitcast(BF16)
                for m in range(4):
                    self.tr(pbf[:, m * 128:(m + 1) * 128], bf[:, half * 4 + m, pr * 128:(pr + 1) * 128], idb[:], [bf_b, idb_b], [pb])
                st, st_b = stgT.next()
                self.cp("dve", st[:, 0:512], pbf[:, 0:512], [pb], [st_b])
                self.dma(dst_dram_fn(pr, half * 512, 512), st[:, 0:512], r=[st_b])

    def past_v(self, cache_ap, ntile, dst_dram, A_ld, A_v):
        ld, ld_b = A_ld
        v, v_b = A_v
        self.dma(ld[:, 0:ntile, :], cache_ap.rearrange("(m p) c -> p m c", p=128), w=[ld_b])
        for m in range(ntile):
            self.cp("pool", v[:, m, :, 0:64], ld[:, m, :].rearrange("p (h d) -> p h d", d=64), [ld_b], [v_b])
        self.dma(dst_dram.rearrange("(m p) c -> p m c", p=128), v[:, 0:ntile].rearrange("p m h d -> p m (h d)"), r=[v_b])

    def tm_out(self, ps, pb, rows, ncols, out_dram, rot, eng):
        st, st_b = rot.next()
        self.cp(eng, st[0:rows, 0:ncols], ps[0:rows, 0:ncols], [pb], [st_b])
        self.dma(out_dram, st[0:rows, 0:ncols], r=[st_b])
        return st, st_b

    def v_store(self, ps, pb, rows, vrot, dst_dram):
        v, v_b = vrot.next()
        self.cp("dve", v[0:rows, :, 0:64], ps[0:rows, :].rearrange("p (h d) -> p h d", d=64), [pb], [v_b])
        self.dma(dst_dram, v[0:rows].rearrange("p h d -> p (h d)"), r=[v_b])

    def phase_pre0(self, xsrc_p, xsrc_s):
        T, NT, TOK = self.T, self.NT, self.TOK
        d = self.dr
        rs, A = self.pre_common(0)
        idb, idb_b = self.P("ident_bf")
        win = A([128, 8 * 1960], BF16, "win"); wuq = A([128, 2 * 1024], BF16, "wuq"); wukv = A([128, 1024], BF16, "wukv")
        self.load_w(win[0], win[1], d["win0"], 8 * 1960, rs["stg"])
        self.load_w(wuq[0], wuq[1], d["wuq"], 2048, rs["stg"])
        self.load_w(wukv[0], wukv[1], d["wukv"], 1024, rs["stg"])
        gq = A([128, 256], F32, "gq"); gkv = A([128, 128], F32, "gkv"); bfb = A([128, 8], F32, "bfb")
        self.dma(gq[0], d["q_a_g"].partition_broadcast(128), w=[gq[1]])
        self.dma(gkv[0], d["kv_a_g"].partition_broadcast(128), w=[gkv[1]])
        self.dma(bfb[0], d["b_f"].partition_broadcast(128), w=[bfb[1]])
        ct = A([128, self.NTT, 16], F32, "ct"); st_ = A([128, self.NTT, 16], F32, "st")
        self.dma(ct[0], d["ct"], w=[ct[1]]); self.dma(st_[0], d["st"], w=[st_[1]])
        c2 = Rot([A([128, 512], F32, "c2_%d" % i) for i in range(2)])
        s2 = Rot([A([128, 512], F32, "s2_%d" % i) for i in range(2)])
        o512 = Rot([A([128, 512], F32, "o512_%d" % i) for i in range(3)])
        o128 = Rot([A([128, 128], F32, "o128_%d" % i) for i in range(2)])
        o32 = Rot([A([128, 32], F32, "o32_%d" % i) for i in range(2)])
        tmp = Rot([A([128, 64], F32, "tmp%d" % i) for i in range(2)])
        ckvb = Rot([A([128, 128], BF16, "ckvb%d" % i) for i in range(2)])
        cqb = Rot([A([128, 256], BF16, "cqb%d" % i) for i in range(2)])
        krb = Rot([A([128, 32], BF16, "krb%d" % i) for i in range(2)])
        vst = [A([128, 8, 65], BF16, "vst%d" % i) for i in range(3)]
        for v, vb in vst:
            self.memset("pool", v[:], 1.0, [vb])
        vrot = Rot(vst)
        ckvt = Rot([A([128, 512], BF16, "ckvt%d" % i) for i in range(2)])
        cqt = Rot([A([128, 2, 512], BF16, "cqt%d" % i) for i in range(2)])
        krt = Rot([A([32, 512], BF16, "krt%d" % i) for i in range(2)])
        fst = Rot([A([128, 512], BF16, "fst%d" % i) for i in range(3)])
        rtmp = Rot([A([128, 512], F32, "rtmp%d" % i) for i in range(2)])
        lall, lall_b = self.P("lall")
        xs_p = ("a_ckv_p", "a_kr_p", "b_k_p", "b_v_p", "b_lf_p")

        ptasks = []
        for blk in self.blocks():
            cell = {}

            def blk_s1(blk=blk, cell=cell):
                is_s = blk[0][0] == NT
                n = 64 if is_s else 512
                cb0 = blk[0][2]
                htb, htb_b = rs["htb"].next()
                for (i, rows, c0) in blk:
                    x, x_b = rs["x"].next()
                    self.dma(x[0:rows, :], self.xrows(xsrc_p, xsrc_s, i, rows), w=[x_b])
                    self.norm_tile(x, x_b, rows, htb, htb_b, c0 - cb0, rs["afm"], rs["bfm"], rs["fm_b"], is_s, rs)
                cell["htb"] = (htb, htb_b)

            def blk_s2(blk=blk, cell=cell):
                is_s = blk[0][0] == NT
                n = 64 if is_s else 512
                cb0 = blk[0][2]
                ckvt_t, ckvt_b = ckvt.next()
                cqt_t, cqt_b = cqt.next()
                krt_t, krt_b = krt.next()
                htb, htb_b = cell["htb"]
                for (i, rows, c0) in blk:
                    lc = c0 - cb0
                    rsl = slice(i * 128, i * 128 + rows) if not is_s else slice(0, 64)
                    sfx = "_s" if is_s else "_p"
                    ps, pb = rs["psa"].next()
                    for kc in range(8):
                        self.mm(ps[0:rows, 0:424], htb[:, kc, lc:lc + rows], win[0][:, kc * 1960: kc * 1960 + 424], kc == 0, kc == 7,
                                [htb_b, win[1]], [pb])
                    sm, sm_b = rs["sm"].next()
                    jk, jk_b = rs["jk"].next()
                    self.act(jk[0:rows, 0:256], ps[0:rows, 0:256], AF.Square, [pb], [jk_b, sm_b], accum_out=sm[0:rows, 0:1])
                    self.act(jk[0:rows, 256:384], ps[0:rows, 256:384], AF.Square, [pb], [jk_b, sm_b], accum_out=sm[0:rows, 1:2])
                    self.act(sm[0:rows, 2:3], sm[0:rows, 0:1], AF.Ln, [sm_b], [sm_b], scale=1.0 / 256, bias=self.eps_ap[0:rows, :])
                    self.act(sm[0:rows, 3:4], sm[0:rows, 1:2], AF.Ln, [sm_b], [sm_b], scale=1.0 / 128, bias=self.eps_ap[0:rows, :])
                    self.act(sm[0:rows, 4:6], sm[0:rows, 2:4], AF.Exp, [sm_b], [sm_b], scale=-0.5)
                    oc, oc_b = o128.next()
                    self.stt("dve", oc[0:rows, :], ps[0:rows, 256:384], sm[0:rows, 5:6], gkv[0][0:rows, :], ALU.mult, ALU.mult,
                             [pb, sm_b, gkv[1]], [oc_b])
                    self.dma(d["a_ckv" + sfx][rsl, :], oc[0:rows, :], r=[oc_b])
                    cb_, cb_b = ckvb.next()
                    self.cp("pool", cb_[0:rows, :], oc[0:rows, :], [oc_b], [cb_b])
                    cq_, cq_b = cqb.next()
                    self.stt("dve", cq_[0:rows, :], ps[0:rows, 0:256], sm[0:rows, 4:5], gq[0][0:rows, :], ALU.mult, ALU.mult,
                             [pb, sm_b, gq[1]], [cq_b])
                    t_, t_b = tmp.next()
                    ok, ok_b = o32.next()
                    cs_ = ct[0][0:rows, i, :]; sn_ = st_[0][0:rows, i, :]
                    x1 = ps[0:rows, 384:400]; x2 = ps[0:rows, 400:416]
                    self.tt("dve", t_[0:rows, 0:16], x1, cs_, ALU.mult, [pb, ct[1]], [t_b])
                    self.tt("dve", t_[0:rows, 16:32], x2, sn_, ALU.mult, [pb, st_[1]], [t_b])
                    self.tt("dve", t_[0:rows, 32:48], x1, sn_, ALU.mult, [pb, st_[1]], [t_b])
                    self.tt("dve", t_[0:rows, 48:64], x2, cs_, ALU.mult, [pb, ct[1]], [t_b])
                    self.tt("pool", ok[0:rows, 0:16], t_[0:rows, 0:16], t_[0:rows, 16:32], ALU.subtract, [t_b], [ok_b])
                    self.tt("pool", ok[0:rows, 16:32], t_[0:rows, 32:48], t_[0:rows, 48:64], ALU.add, [t_b, ok_b], [ok_b])
                    self.dma(d["a_kr" + sfx][rsl, :], ok[0:rows, :], r=[ok_b])
                    kb_, kb_b = krb.next()
                    self.cp("pool", kb_[0:rows, :], ok[0:rows, :], [ok_b], [kb_b])
                    t2, t2_b = tmp.next()
                    self.tt("dve", t2[0:rows, 0:8], ps[0:rows, 416:424], bfb[0][0:rows, :], ALU.add, [pb, bfb[1]], [t2_b])
                    self.act(t2[0:rows, 8:16], t2[0:rows, 0:8], AF.Exp, [t2_b], [t2_b], scale=-1.0)
                    self.act(t2[0:rows, 16:24], t2[0:rows, 8:16], AF.Ln, [t2_b], [t2_b], scale=1.0, bias=1.0)
                    self.ts("dve", lall[0:rows, i, :], t2[0:rows, 16:24], -1.0, None, ALU.mult, None, [t2_b], [lall_b])
                    self.dma(d["b_lf" + sfx][rsl, :], lall[0:rows, i, :], r=[lall_b])
                    pt_, ptb = rs["pst"].next()
                    pbf = pt_[:].bitcast(BF16)
                    self.tr(pbf[:, 0:rows], cb_[0:rows, :], idb[0:rows, 0:rows], [cb_b, idb_b], [ptb])
                    self.tr(pbf[:, 128:128 + rows], cq_[0:rows, 0:128], idb[0:rows, 0:rows], [cq_b, idb_b], [ptb])
                    self.tr(pbf[:, 256:256 + rows], cq_[0:rows, 128:256], idb[0:rows, 0:rows], [cq_b, idb_b], [ptb])
                    self.tr(pbf[0:32, 384:384 + rows], kb_[0:rows, :], idb[0:rows, 0:rows], [kb_b, idb_b], [ptb])
                    self.cp("dve", ckvt_t[:, lc:lc + rows], pbf[:, 0:rows], [ptb], [ckvt_b])
                    self.cp("dve", cqt_t[:, 0, lc:lc + rows], pbf[:, 128:128 + rows], [ptb], [cqt_b])
                    self.cp("dve", cqt_t[:, 1, lc:lc + rows], pbf[:, 256:256 + rows], [ptb], [cqt_b])
                    self.cp("dve", krt_t[0:32, lc:lc + rows], pbf[0:32, 384:384 + rows], [ptb], [krt_b])
                    ps, pb = rs["psa"].next()
                    for kc in range(8):
                        self.mm(ps[0:rows, :], htb[:, kc, lc:lc + rows], win[0][:, kc * 1960 + 936: kc * 1960 + 1448], kc == 0, kc == 7,
                                [htb_b, win[1]], [pb])
                    self.tm_out(ps, pb, rows, 512, d["b_k" + sfx][rsl, :], o512, "act")
                    ps, pb = rs["psa"].next()
                    for kc in range(8):
                        self.mm(ps[0:rows, :], htb[:, kc, lc:lc + rows], win[0][:, kc * 1960 + 1448: kc * 1960 + 1960], kc == 0, kc == 7,
                                [htb_b, win[1]], [pb])
                    self.tm_out(ps, pb, rows, 512, d["b_v" + sfx][rsl, :], o512, "act")
                    self.v_store(ps, pb, rows, vrot, d["vs_b"][c0:c0 + rows, :])
                    ps, pb = rs["psa"].next()
                    self.mm(ps[0:rows, :], ckvt_t[:, lc:lc + rows], wukv[0][:, 512:1024], True, True, [ckvt_b, wukv[1]], [pb])
                    self.v_store(ps, pb, rows, vrot, d["vs_a"][c0:c0 + rows, :])
                self.dma(d["kr"][:, cb0:cb0 + n], krt_t[0:32, 0:n], r=[krt_b])
                for pr in range(4):
                    self.fm_proj(win[0], win[1], 424 + pr * 128, 8, 1960, lambda kc: htb[:, kc, 0:n], htb_b, n, 128, rs, 0.125,
                                 d["qs_b"][pr * 128:(pr + 1) * 128, cb0:cb0 + n], fst)
                    self.fm_proj(win[0], win[1], 936 + pr * 128, 8, 1960, lambda kc: htb[:, kc, 0:n], htb_b, n, 128, rs, 1.0,
                                 d["ks_b"][pr * 128:(pr + 1) * 128, cb0:cb0 + n], fst)
                    self.fm_proj(wukv[0], wukv[1], pr * 128, 1, 0, lambda kc: ckvt_t[:, 0:n], ckvt_b, n, 128, rs, 1.0,
                                 d["ks_an"][pr * 128:(pr + 1) * 128, cb0:cb0 + n], fst)
                c2t, c2b = c2.next(); s2t, s2b = s2.next()
                self.dma(c2t[64:96, 0:n], d["cos2"][:, cb0:cb0 + n], w=[c2b])
                self.dma(s2t[64:96, 0:n], d["sin2"][:, cb0:cb0 + n], w=[s2b])
                for h in range(8):
                    ps, pb = rs["psa"].next()
                    ps2, pb2 = rs["psb"].next()
                    for kc in range(2):
                        self.mm(ps[0:96, 0:n], wuq[0][:, kc * 1024 + h * 96: kc * 1024 + h * 96 + 96], cqt_t[:, kc, 0:n], kc == 0, kc == 1,
                                [wuq[1], cqt_b], [pb])
                    for kc in range(2):
                        self.mm(ps2[64:96, 0:n], wuq[0][:, kc * 1024 + 768 + h * 32: kc * 1024 + 768 + h * 32 + 32], cqt_t[:, kc, 0:n],
                                kc == 0, kc == 1, [wuq[1], cqt_b], [pb2])
                    qa, qa_b = fst.next()
                    self.act(qa[0:64, 0:n], ps[0:64, 0:n], AF.Copy, [pb], [qa_b], scale=float(96.0 ** -0.5))
                    r1, r1_b = rtmp.next()
                    r2, r2_b = rtmp.next()
                    self.tt("dve", r1[64:96, 0:n], ps[64:96, 0:n], c2t[64:96, 0:n], ALU.mult, [pb, c2b], [r1_b])
                    self.tt("dve", r2[64:96, 0:n], ps2[64:96, 0:n], s2t[64:96, 0:n], ALU.mult, [pb2, s2b], [r2_b])
                    self.tt("pool", qa[64:96, 0:n], r1[64:96, 0:n], r2[64:96, 0:n], ALU.add, [r1_b, r2_b, qa_b], [qa_b])
                    self.dma(d["qs_a"][h, :, cb0:cb0 + n], qa[0:96, 0:n], r=[qa_b])
            ptasks.append((blk_s1, blk_s2))
        self.run_pipeline(ptasks, 1, 0)

        ld = A([128, 8, 512], F32, "pld"); bf = A([128, 8, 512], BF16, "pbf")
        vpast = A([128, 8, 8, 65], BF16, "vpast")
        self.memset("pool", vpast[0][:], 1.0, [vpast[1]])
        lpast, lpast_b = self.P("lpast")
        for s_ in range(2):
            pc0 = TOK + s_ * PAST
            self.past_kT(d["cb_k"][s_], 8, lambda pr, c, w_: d["ks_b"][pr * 128:(pr + 1) * 128, pc0 + c:pc0 + c + w_], rs, ld, bf, fst)
            self.past_v(d["cb_v"][s_], 8, d["vs_b"][pc0:pc0 + PAST, :], ld, vpast)
            self.dma(lpast[:, s_, :, :], d["cb_lf"][s_].rearrange("(m p) h -> p m h", p=128), w=[lpast_b])
            self.dma(ld[0][:, :, 0:128], d["ca_ckv"][s_].rearrange("(m p) c -> p m c", p=128), w=[ld[1]])
            self.dma(ld[0][:, :, 128:160], d["ca_kr"][s_].rearrange("(m p) c -> p m c", p=128), w=[ld[1]])
            self.cp("pool", bf[0][:, :, 0:160], ld[0][:, :, 0:160], [ld[1]], [bf[1]])
            for half in range(2):
                ck_t, ck_b = ckvt.next()
                kr_t, kr_b = krt.next()
                ps, pb = rs["pst"].next()
                pbf = ps[:].bitcast(BF16)
                ps2, pb2 = rs["pst"].next()
                pbf2 = ps2[:].bitcast(BF16)
                for m in range(4):
                    self.tr(pbf[:, m * 128:(m + 1) * 128], bf[0][:, half * 4 + m, 0:128], idb[:], [bf[1], idb_b], [pb])
                    self.tr(pbf2[0:32, m * 128:(m + 1) * 128], bf[0][:, half * 4 + m, 128:160], idb[:], [bf[1], idb_b], [pb2])
                self.cp("dve", ck_t[:, 0:512], pbf[:, 0:512], [pb], [ck_b])
                self.cp("dve", kr_t[0:32, 0:512], pbf2[0:32, 0:512], [pb2], [kr_b])
                cc0 = pc0 + half * 512
                self.dma(d["kr"][:, cc0:cc0 + 512], kr_t[0:32, 0:512], r=[kr_b])
                for pr in range(4):
                    self.fm_proj(wukv[0], wukv[1], pr * 128, 1, 0, lambda kc: ck_t[:, 0:512], ck_b, 512, 128, rs, 1.0,
                                 d["ks_an"][pr * 128:(pr + 1) * 128, cc0:cc0 + 512], fst)
                for m in range(4):
                    ps3, pb3 = rs["psa"].next()
                    self.mm(ps3[:, :], ck_t[:, m * 128:(m + 1) * 128], wukv[0][:, 512:1024], True, True, [ck_b, wukv[1]], [pb3])
                    r0 = pc0 + (half * 4 + m) * 128
                    self.v_store(ps3, pb3, 128, vrot, d["vs_a"][r0:r0 + 128, :])
        self.fox_cum(rs, A)
        self.kb.barrier()

    def fox_cum(self, rs, A):
        NT, T = self.NT, self.T
        lall, lall_b = self.P("lall"); lpast, lpast_b = self.P("lpast")
        ncum, ncum_b = self.P("ncum"); ncump, ncump_b = self.P("ncump")
        cumt, cumt_b = self.P("cumt")
        uf, uf_b = self.P("uf"); onesf, onesf_b = self.P("onesf"); idf, idf_b = self.P("ident_f")
        tot = A([128, 33, 8], F32, "tot"); car = A([128, 34, 8], F32, "car")
        self.memset("pool", ncum[:], 0.0, [ncum_b])
        self.memset("pool", cumt[:], 0.0, [cumt_b])

        def seq_cum(l_ap, l_b, nt, out_ap, out_b):
            ps, pb = rs["psa"].next()
            ps2, pb2 = rs["psa"].next()
            lf = l_ap.rearrange("p m h -> p (m h)")
            self.mm(ps[:, 0:nt * 8], onesf[:], lf, True, True, [onesf_b, l_b], [pb])
            self.mm(ps2[:, 0:nt * 8], uf[:], lf, True, True, [uf_b, l_b], [pb2])
            self.cp("dve", tot[0][:, 0:nt, :], ps[:, 0:nt * 8].rearrange("p (m h) -> p m h", h=8), [pb], [tot[1]])
            self.memset("dve", car[0][:, 0, :], 0.0, [car[1]])
            for j in range(1, nt + 1):
                self.tt("dve", car[0][:, j, :], car[0][:, j - 1, :], tot[0][:, j - 1, :], ALU.add, [car[1], tot[1]], [car[1]])
            self.stt("dve", out_ap, ps2[:, 0:nt * 8].rearrange("p (m h) -> p m h", h=8), -1.0, car[0][:, 0:nt, :],
                     ALU.mult, ALU.subtract, [pb2, car[1]], [out_b])
        seq_cum(lall[:, 0:NT, :], lall_b, NT, ncum[:, 0:NT, :], ncum_b)
        for s_ in range(2):
            seq_cum(lpast[:, s_, :, :], lpast_b, 8, ncump[:, s_, :, :], ncump_b)
            ps, pb = rs["psa"].next()
            b0 = 32 * s_
            self.mm(ps[b0:b0 + 16, 0:8], uf[b0:b0 + 16, b0:b0 + 16], lall[b0:b0 + 16, NT, :], True, True, [uf_b, lall_b], [pb])
            self.stt("dve", ncum[b0:b0 + 16, NT, :], ps[b0:b0 + 16, 0:8], -1.0, car[0][b0:b0 + 16, 8, :], ALU.mult, ALU.subtract,
                     [pb, car[1]], [ncum_b])
        for g in range(0, self.NTT, 4):
            ps, pb = rs["psa"].next()
            cnt = min(4, self.NTT - g)
            wtot = 0
            for m in range(cnt):
                i = g + m
                rows = 128 if i < NT else 64
                self.tr(ps[0:8, m * 128:m * 128 + rows], ncum[0:rows, i, :], idf[0:rows, 0:rows], [ncum_b, idf_b], [pb])
                wtot = m * 128 + rows
            self.act(cumt[0:8, g * 128:g * 128 + wtot], ps[0:8, 0:wtot], AF.Copy, [pb], [cumt_b], scale=-1.0)

    def run_pipeline(self, tasks, L=2, L2=1):
        n = len(tasks)
        for i in range(n + L + L2):
            if i < n:
                tasks[i][0]()
            if 0 <= i - L < n:
                tasks[i - L][1]()
            if 0 <= i - L - L2 < n and len(tasks[i - L - L2]) > 2:
                tasks[i - L - L2][2]()

    def softmax_tasks(self, qT, Kd, nq, qtiles, ktiles, res, obuf_fn):
        st = {"first": True, "ob": None}
        nkt = len(ktiles)
        tasks = []
        for ti, kt in enumerate(ktiles):
            cell = {}

            def s1(kt=kt, cell=cell, ti=ti):
                if ti == 0:
                    st["ob"] = res["obank"].next()
                nk, pb0, clo = kt["nk"], kt["pbase"], kt["clo"]
                ps, pb = res["sbank"].next()
                cell["ps"] = (ps, pb)
                extra = kt.get("extra", [])
                masks = kt.get("masks", [])
                nmm = 1 + len(extra) + len(masks)
                k_ = 0
                self.mm(ps[pb0:pb0 + nk, clo:nq], kt["kT"], qT[:, clo:nq], True, nmm == 1, kt["rb"], [pb], skip_group_check=nmm > 1)
                for (l_ap, r_ap, rb) in extra:
                    k_ += 1
                    self.mm(ps[pb0:pb0 + nk, clo:nq], l_ap, r_ap[:, clo:nq], False, k_ == nmm - 1, rb, [pb], skip_group_check=True)
                for (co, wd, l_ap, r_ap, rb) in masks:
                    k_ += 1
                    self.mm(ps[pb0:pb0 + nk, co:co + wd], l_ap, r_ap, False, k_ == nmm - 1, rb, [pb], skip_group_check=True)

            def s2(kt=kt, cell=cell, ti=ti):
                nk, pb0, clo = kt["nk"], kt["pbase"], kt["clo"]
                ps, pb = cell["ps"]
                ob, ob_b = st["ob"]
                pt, pt_b = res["pt"].next()
                kw = {}
                rd = [pb]
                if kt.get("bias") is not None:
                    kw["bias"] = kt["bias"]
                    rd = rd + kt["bias_b"]
                self.act(pt[pb0:pb0 + nk, clo:nq], ps[pb0:pb0 + nk, clo:nq], AF.Exp, rd, [pt_b], **kw)
                for qi, (ql, qn, qpb) in enumerate(qtiles):
                    if ql + qn <= clo:
                        continue
                    self.mm(ob[qpb:qpb + qn, qi * 65:qi * 65 + 65], pt[pb0:pb0 + nk, ql:ql + qn], kt["v"], st["first"], False,
                            [pt_b] + kt["vb"], [ob_b], skip_group_check=True)
                    st["first"] = False
                if ti == nkt - 1:
                    for qi, (ql, qn, qpb) in enumerate(qtiles):
                        rc, rc_b = res["rc"].next()
                        self.kb.op("dve", lambda h, rc=rc, ob=ob, qpb=qpb, qn=qn, qi=qi: h.reciprocal(
                            out=rc[qpb:qpb + qn, 0:1], in_=ob[qpb:qpb + qn, qi * 65 + 64:qi * 65 + 65]), [ob_b], [rc_b])
                        dst, dst_b = obuf_fn(qi)
                        self.ts("dve", dst, ob[qpb:qpb + qn, qi * 65:qi * 65 + 64], rc[qpb:qpb + qn, 0:1], None, ALU.mult, None,
                                [ob_b, rc_b], [dst_b])
            tasks.append((s1, s2))
        return tasks

    def softmax_block(self, qT, Kd, nq, qtiles, ktiles, res, obuf_fn):
        self.run_pipeline(self.softmax_tasks(qT, Kd, nq, qtiles, ktiles, res, obuf_fn))

    def att_alloc(self):
        ar = self.ar
        ar.reset()
        A = lambda shape, dt, name="": ar.alloc(shape, dt, name)
        res = {}
        res["qt"] = Rot([A([128, self.TOK], BF16, "qt%d" % i) for i in range(2)])
        res["kt"] = Rot([A([128, self.TOKK], BF16, "kt%d" % i) for i in range(2)])
        nvt = self.NTT + 16
        res["vt"] = Rot([A([128, nvt, 4, 65], BF16, "vt%d" % i) for i in range(2)])
        res["obuf"] = Rot([A([128, self.NTT, 256], BF16, "obuf%d" % i) for i in range(2)])
        res["pt"] = Rot([A([128, 512], BF16, "pt%d" % i) for i in range(4)])
        res["rc"] = Rot([A([128, 1], F32, "rc%d" % i) for i in range(4)])
        res["sbank"] = Rot([(self.PS[i], self.PB[i]) for i in (0, 1, 2)])
        res["obank"] = Rot([(self.PS[i], self.PB[i]) for i in (3, 4)])
        res["xbank"] = Rot([(self.PS[i], self.PB[i]) for i in (5, 6)])
        res["zbank"] = Rot([(self.PS[i], self.PB[i]) for i in (7,)])
        return res, A

    def load_v(self, res, vs_name, hg):
        NT, T, TOK = self.NT, self.T, self.TOK
        vt, vt_b = res["vt"].next()
        src = self.dr[vs_name]
        cs = slice(hg * 260, hg * 260 + 260)
        self.dma(vt[:, 0:NT].rearrange("p m h d -> p m (h d)"), src[0:T, cs].rearrange("(m p) c -> p m c", p=128), w=[vt_b])
        self.dma(vt[0:64, NT].rearrange("p h d -> p (h d)"), src[T:T + 64, cs], w=[vt_b])
        self.dma(vt[:, NT + 1:NT + 17].rearrange("p m h d -> p m (h d)"),
                 src[TOK:TOK + 2 * PAST, cs].rearrange("(m p) c -> p m c", p=128), w=[vt_b])
        return vt, vt_b

    def flush_obuf(self, obuf, obuf_b, colbase):
        NT, T = self.NT, self.T
        dst = self.dr["os"]
        self.dma(dst[0:T, colbase:colbase + 256].rearrange("(m p) c -> p m c", p=128), obuf[:, 0:NT, :], r=[obuf_b])
        self.dma(dst[T:T + 64, colbase:colbase + 256], obuf[0:64, NT, :], r=[obuf_b])

    def head_plan(self, res, typ, h, qname, kname, vname):
        d = self.dr
        st = self._hp
        loads = []
        if h % 4 == 0:
            st["vt"] = res["vt"].next()
            st["obuf"] = res["obuf"].next()
            vt, vt_b = st["vt"]
            obuf, obuf_b = st["obuf"]
            loads.append(lambda vt=vt, vt_b=vt_b, hg=h // 4: self.load_v_into(vt, vt_b, vname, hg))
            loads.append(lambda obuf=obuf, obuf_b=obuf_b: self.memset("pool", obuf[:], 0.0, [obuf_b]))
        if typ == "mla":
            st["qt"] = res["qt"].next(); st["kt"] = res["kt"].next()
            qt, qt_b = st["qt"]; kt, kt_b = st["kt"]
            loads.append(lambda: self.dma(qt[0:96, :], d["qs_a"][h], w=[qt_b]))
            loads.append(lambda: self.dma(kt[0:64, :], d["ks_an"][h * 64:(h + 1) * 64, :], w=[kt_b]))
            loads.append(lambda: self.dma(kt[64:96, :], d["kr"], w=[kt_b]))
            Kd, p0 = 96, 0
        else:
            st["qt"] = res["qt"].next(); st["kt"] = res["kt"].next()
            qt, qt_b = st["qt"]; kt, kt_b = st["kt"]
            cumt, cumt_b = self.P("cumt")
            loads.append(lambda: self.dma(qt[0:64, :], d[qname][h * 64:(h + 1) * 64, :], w=[qt_b]))
            loads.append(lambda: self.dma(kt[0:64, :], d[kname][h * 64:(h + 1) * 64, :], w=[kt_b]))
            loads.append(lambda: self.memset("pool", kt[64:65, :], 1.0, [kt_b]))
            if typ == "fox":
                loads.append(lambda: self.dma(qt[64:65, :], cumt[h:h + 1, :], r=[cumt_b], w=[qt_b]))
            else:
                loads.append(lambda: self.memset("pool", qt[64:65, :], 0.0, [qt_b]))
            Kd, p0 = 65, 0
        qt, qt_b = st["qt"]; kt, kt_b = st["kt"]
        vt, vt_b = st["vt"]; obuf, obuf_b = st["obuf"]
        ctx = dict(qv=qt[p0:p0 + Kd, :], kv=kt[p0:p0 + Kd, :], qt_b=qt_b, kt_b=kt_b, vt=vt, vt_b=vt_b,
                   obuf=obuf, obuf_b=obuf_b, Kd=Kd, hh=h % 4)

        def emit_loads():
            for f in loads:
                f()
        ctx["emit_loads"] = emit_loads
        return ctx

    def load_v_into(self, vt, vt_b, vs_name, hg):
        NT, T, TOK = self.NT, self.T, self.TOK
        src = self.dr[vs_name]
        cs = slice(hg * 260, hg * 260 + 260)
        self.dma(vt[:, 0:NT].rearrange("p m h d -> p m (h d)"), src[0:T, cs].rearrange("(m p) c -> p m c", p=128), w=[vt_b])
        self.dma(vt[0:64, NT].rearrange("p h d -> p (h d)"), src[T:T + 64, cs], w=[vt_b])
        self.dma(vt[:, NT + 1:NT + 17].rearrange("p m h d -> p m (h d)"),
                 src[TOK:TOK + 2 * PAST, cs].rearrange("(m p) c -> p m c", p=128), w=[vt_b])

    def att_drive(self, res, typ, names, task_fn, oscol, L=2, post_fn=None):
        self._hp = {}
        ctxs = [self.head_plan(res, typ, h, *names) for h in range(8)]
        tasks = []
        ctxs[0]["emit_loads"]()
        for h in range(8):
            ht = task_fn(h, ctxs[h])
            if h + 1 < 8:
                s1 = ht[0][0]
                nxt = ctxs[h + 1]["emit_loads"]
                ht[0] = ((lambda s1=s1, nxt=nxt: (nxt(), s1())),) + tuple(ht[0][1:])
            c = ctxs[h]
            fl = (lambda c=c, h=h: self.flush_obuf(c["obuf"], c["obuf_b"], oscol + (h // 4) * 256)) if h % 4 == 3 else None
            if post_fn is not None:
                self.run_pipeline(ht, L)
                post_fn(h, c)
                if fl:
                    fl()
            else:
                if fl:
                    ht.append((lambda: None, lambda: None, fl))
                tasks += ht
        if tasks:
            self.run_pipeline(tasks, L)

    def phase_att0(self):
        T, NT, TOK = self.T, self.NT, self.TOK
        res, A = self.att_alloc()
        idb, idb_b = self.P("ident_bf")
        mm_, mm_b = self.P("m_mla"); mf_, mf_b = self.P("m_fox")
        selh, selh_b = self.P("selh"); cumt, cumt_b = self.P("cumt")
        ncum, ncum_b = self.P("ncum"); ncump, ncump_b = self.P("ncump")

        def mk(typ):
            msk, msk_b = (mm_, mm_b) if typ == "mla" else (mf_, mf_b)

            def task_fn(h, c):
                qv, kv, vt, vt_b, obuf, obuf_b, hh = c["qv"], c["kv"], c["vt"], c["vt_b"], c["obuf"], c["obuf_b"], c["hh"]
                rb = [c["kt_b"], c["qt_b"]]
                tasks = []
                for qb in range(NT // 4):
                    qc = qb * 512
                    ktl = []
                    for j in range(4 * qb + 4):
                        r = j - 4 * qb
                        kd = dict(nk=128, pbase=0, clo=max(r, 0) * 128, kT=kv[:, j * 128:(j + 1) * 128], rb=rb,
                                  v=vt[:, j, hh, :], vb=[vt_b])
                        if typ == "fox":
                            kd["bias"] = ncum[:, j, h:h + 1]; kd["bias_b"] = [ncum_b]
                        if r >= 0:
                            kd["masks"] = [(r * 128, 128, idb[:], msk[:], [idb_b, msk_b])]
                        ktl.append(kd)
                    tasks += self.softmax_tasks(qv[:, qc:qc + 512], c["Kd"], 512, [(m * 128, 128, 0) for m in range(4)], ktl, res,
                                                lambda qi, qb=qb: (obuf[:, 4 * qb + qi, hh * 64:(hh + 1) * 64], obuf_b))
                for s_ in range(2):
                    b0 = 32 * s_
                    qc = T + b0
                    ktl = []
                    for m in range(8):
                        kc0 = TOK + s_ * PAST + m * 128
                        kd = dict(nk=128, pbase=0, clo=0, kT=kv[:, kc0:kc0 + 128], rb=rb,
                                  v=vt[:, NT + 1 + s_ * 8 + m, hh, :], vb=[vt_b])
                        if typ == "fox":
                            kd["bias"] = ncump[:, s_, m, h:h + 1]; kd["bias_b"] = [ncump_b]
                        ktl.append(kd)
                    kd = dict(nk=16, pbase=b0, clo=0, kT=kv[:, qc:qc + 16], rb=rb, v=vt[b0:b0 + 16, NT, hh, :], vb=[vt_b])
                    if typ == "fox":
                        kd["bias"] = ncum[b0:b0 + 16, NT, h:h + 1]; kd["bias_b"] = [ncum_b]
                        kd["masks"] = [(0, 16, idb[:, b0:b0 + 16], msk[:, b0:b0 + 16], [idb_b, msk_b])]
                    ktl.append(kd)
                    tasks += self.softmax_tasks(qv[:, qc:qc + 16], c["Kd"], 16, [(0, 16, b0)], ktl, res,
                                                lambda qi, b0=b0: (obuf[b0:b0 + 16, NT, hh * 64:(hh + 1) * 64], obuf_b))
                return tasks
            return task_fn
        self.att_drive(res, "mla", (None, None, "vs_a"), mk("mla"), 0)
        self.att_drive(res, "fox", ("qs_b", "ks_b", "vs_b"), mk("fox"), 512)
        self.kb.barrier()

    def phase_proj(self, l, xsrc_p, xsrc_s, xdst):
        T, NT, TOK = self.T, self.NT, self.TOK
        d = self.dr
        ar = self.ar
        ar.reset()
        A = lambda shape, dt, name="": ar.alloc(shape, dt, name)
        wgu_dst, wgu_b = A([128, 2 * 8 * DFF], BF16, "wgu_prefetch")
        ar.off = max(ar.off, 48000)
        idb, idb_b = self.P("ident_bf")
        stg = Rot([A([128, 2048], F32, "stg%d" % i) for i in range(2)])
        pf = []
        for wi, nm in enumerate(("wg%d" % l, "wu%d" % l)):
            for c0 in range(0, 8 * DFF, 2048):
                n_ = min(2048, 8 * DFF - c0)
                pf.append((wgu_dst[:, wi * 8 * DFF + c0: wi * 8 * DFF + c0 + n_], d[nm][:, c0:c0 + n_], n_))

        def prefetch(k):
            for _ in range(k):
                if not pf:
                    return
                dst_, src_, n_ = pf.pop(0)
                s_ap, s_b = stg.next()
                self.dma(s_ap[:, 0:n_], src_, w=[s_b])
                self._cast_i = getattr(self, "_cast_i", 0) + 1
                self.cp("dve" if self._cast_i % 2 else "act", dst_, s_ap[:, 0:n_], [s_b], [wgu_b])
        eps, eps_b = A([128, 1], F32, "eps")
        self.memset("pool", eps, EPS, [eps_b])
        self.eps_ap = eps
        wada = A([128, 8192], BF16, "wada")
        self.ar_bc1 = A([128, D], F32, "bc1"); self.ar_bc2 = A([128, D], F32, "bc2")
        gp = A([128, D], F32, "gp"); gs = A([128, D], F32, "gs")
        self.mod_bc(l, 2, stg, wada, gp[0], gp[1], gs[0], gs[1], d["gpost"][l, 0])
        wout = A([128, 8192], BF16, "wout")
        self.load_w(wout[0], wout[1], d["wout%d" % l], 8192, stg)
        osb = Rot([A([128, D], BF16, "osb%d" % i) for i in range(2)])
        ot = Rot([A([128, 8, 128], BF16, "ot%d" % i) for i in range(2)])
        xr = Rot([A([128, D], F32, "x%d" % i) for i in range(3)])
        tmp = Rot([A([128, 512], F32, "t%d" % i) for i in range(2)])
        sm = Rot([A([128, 16], F32, "sm%d" % i) for i in range(4)])
        jk = Rot([A([128, 512], BF16, "jk%d" % i) for i in range(2)])
        pst = Rot([(self.PS[0], self.PB[0]), (self.PS[1], self.PB[1])])
        pso = Rot([(self.PS[i], self.PB[i]) for i in (2, 3, 4, 5)])
        ptasks = []
        for (i, rows, c0) in self.tiles():
            cell = {}

            def s1(i=i, rows=rows, c0=c0, cell=cell):
                prefetch(1)
                o_, o_b = osb.next()
                self.dma(o_[0:rows, :], d["os"][c0:c0 + rows, :], w=[o_b])
                x, x_b = xr.next()
                self.dma(x[0:rows, :], self.xrows(xsrc_p, xsrc_s, i, rows), w=[x_b])
                ps, pb = pst.next()
                pbf = ps[:].bitcast(BF16)
                for c in range(8):
                    self.tr(pbf[:, c * 128:c * 128 + rows], o_[0:rows, c * 128:(c + 1) * 128], idb[0:rows, 0:rows], [o_b, idb_b], [pb])
                ot_, ot_b = ot.next()
                self.cp("dve", ot_[:, :, 0:rows], pbf.rearrange("p (c t) -> p c t", t=128)[:, :, 0:rows], [pb], [ot_b])
                cell["st"] = self.resid_mm(lambda kc, half: (ot_[:, kc, 0:rows], wout[0][:, kc * 1024 + half * 512: kc * 1024 + half * 512 + 512]),
                                           8, [ot_b, wout[1]], rows, pso, sm, jk)
                cell["x"] = (x, x_b)

            def s2(i=i, rows=rows, c0=c0, cell=cell):
                is_s = i == NT
                g_ap, g_b = (gs if is_s else gp)
                x, x_b = cell["x"]
                self.resid_fin(cell["st"], rows, x, x_b, g_ap, g_b, tmp, xdst[c0:c0 + rows, :])
            ptasks.append((s1, s2))
        self.run_pipeline(ptasks, 1, 0)
        prefetch(len(pf))
        self.kb.barrier()

    def resid_out(self, opfn, nk, rb, rows, x, x_b, g_ap, g_b, pso, sm, jk, tmp, dst):
        stt_ = self.resid_mm(opfn, nk, rb, rows, pso, sm, jk)
        self.resid_fin(stt_, rows, x, x_b, g_ap, g_b, tmp, dst)

    def resid_mm(self, opfn, nk, rb, rows, pso, sm, jk):
        pss = []
        s_, s_b = sm.next()
        for half in range(2):
            ps, pb = pso.next()
            for kc in range(nk):
                l_ap, r_ap = opfn(kc, half)
                self.mm(ps[0:rows, :], l_ap, r_ap, kc == 0, kc == nk - 1, rb, [pb])
            j_, j_b = jk.next()
            self.act(j_[0:rows, 0:512], ps[0:rows, :], AF.Square, [pb], [j_b, s_b], accum_out=s_[0:rows, half:half + 1])
            pss.append((ps, pb))
        return (pss, s_, s_b)

    def resid_fin(self, stt_, rows, x, x_b, g_ap, g_b, tmp, dst):
        pss, s_, s_b = stt_
        self.tt("dve", s_[0:rows, 2:3], s_[0:rows, 0:1], s_[0:rows, 1:2], ALU.add, [s_b], [s_b])
        self.act(s_[0:rows, 3:4], s_[0:rows, 2:3], AF.Ln, [s_b], [s_b], scale=1.0 / D, bias=self.eps_ap[0:rows, :])
        self.act(s_[0:rows, 4:5], s_[0:rows, 3:4], AF.Exp, [s_b], [s_b], scale=-0.5)
        for half in range(2):
            ps, pb = pss[half]
            sl = slice(half * 512, half * 512 + 512)
            t_, t_b = tmp.next()
            self.stt("dve", t_[0:rows, :], ps[0:rows, :], s_[0:rows, 4:5], g_ap[0:rows, sl], ALU.mult, ALU.mult,
                     [pb, s_b, g_b], [t_b])
            self.tt("pool", x[0:rows, sl], x[0:rows, sl], t_[0:rows, :], ALU.add, [x_b, t_b], [x_b])
        self.dma(dst, x[0:rows, :], r=[x_b])

    def phase_ffn(self, l, xsrc, xdst_p, xdst_s):
        T, NT, TOK = self.T, self.NT, self.TOK
        d = self.dr
        ar = self.ar
        ar.reset()
        A = lambda shape, dt, name="": ar.alloc(shape, dt, name)
        idb, idb_b = self.P("ident_bf")
        halo, halo_b = self.P("halo")
        wg = A([128, 8 * DFF], BF16, "wg"); wu = A([128, 8 * DFF], BF16, "wu"); wd = A([128, NFC * D], BF16, "wd")
        gp = A([128, D], F32, "gp"); gs = A([128, D], F32, "gs")
        afm = A([128, 8, 3], F32, "afm"); bfm = A([128, 8, 3], F32, "bfm"); gfm = A([128, 8], F32, "gfm")
        cw = A([128, NFC, 3], F32, "cw"); cbi = A([128, NFC], F32, "cb")
        eps, eps_b = A([128, 1], F32, "eps")
        self.memset("pool", eps, EPS, [eps_b])
        self.eps_ap = eps
        mark = ar.off
        stg = Rot([A([128, 2048], F32, "stg%d" % i) for i in range(2)])
        wada = A([128, 8192], BF16, "wada")
        self.ar_small_fm = A([128, 8], F32, "adabfm")
        self.ar_bc1 = A([128, D], F32, "bc1"); self.ar_bc2 = A([128, D], F32, "bc2")
        self.mod_fm(l, [4, 3], stg, wada, [afm, bfm])
        self.dma(gfm[0], d["gpre_fm"][l, 1], w=[gfm[1]])
        for sq in range(3):
            self.stt("dve", afm[0][:, :, sq], afm[0][:, :, sq], 1.0, gfm[0], ALU.add, ALU.mult, [afm[1], gfm[1]], [afm[1]])
        fm_b = Buf("fmb")
        self.cp("dve", bfm[0][:, 0, 0:1], bfm[0][:, 0, 0:1], [afm[1], bfm[1]], [fm_b, bfm[1]])
        self.mod_bc(l, 5, stg, wada, gp[0], gp[1], gs[0], gs[1], d["gpost"][l, 1])
        self.load_w(wd[0], wd[1], d["wd%d" % l], NFC * D, stg)
        self.dma(cw[0], d["convw"][l], w=[cw[1]])
        self.dma(cbi[0], d["convb"][l], w=[cbi[1]])
        self.memset("pool", halo[:, :, 0, :], 0.0, [halo_b])
        for s_ in range(2):
            self.dma(halo[:, :, 1 + s_, :], d["st_conv_fm"][l, s_], w=[halo_b])
        self.kb.barrier()
        ar.off = mark
        NB = 256
        rs = {}
        rs["x"] = Rot([A([128, D], F32, "x%d" % i) for i in range(2)])
        rs["sm"] = Rot([A([128, 16], F32, "sm%d" % i) for i in range(4)])
        rs["jk"] = Rot([A([128, D], BF16, "jk%d" % i) for i in range(1)])
        rs["xn"] = Rot([A([128, D], BF16, "xn%d" % i) for i in range(2)])
        rs["pst"] = Rot([(self.PS[0], self.PB[0])])
        h2t, h2t_b = A([128, 8, NB], BF16, "h2t")
        gbr = Rot([A([128, NB + 2], F32, "gb%d" % i) for i in range(3)])
        a1r = Rot([A([128, NB], F32, "a1_%d" % i) for i in range(3)])
        a2r = Rot([A([128, NB], F32, "a2_%d" % i) for i in range(3)])
        actt, actt_b = A([128, NFC, NB], BF16, "actt")
        tmp = Rot([A([128, 512], F32, "t%d" % i) for i in range(2)])
        jk5 = rs["jk"]
        psg = Rot([(self.PS[i], self.PB[i]) for i in (1, 2)])
        psu = Rot([(self.PS[i], self.PB[i]) for i in (3, 4, 5, 6)])
        pso = Rot([(self.PS[i], self.PB[i]) for i in (7, 0)])
        blocks = [[(i, 128, i * 128) for i in range(b * 2, b * 2 + 2)] for b in range(NT // 2)] + [[(NT, 64, T)]]
        for bi, blk in enumerate(blocks):
            is_s = blk[0][0] == NT
            n = 64 if is_s else NB
            cb0 = blk[0][2]
            for (i, rows, c0) in blk:
                x, x_b = rs["x"].next()
                self.dma(x[0:rows, :], xsrc[c0:c0 + rows, :], w=[x_b])
                self.norm_tile(x, x_b, rows, h2t, h2t_b, c0 - cb0, afm[0], bfm[0], fm_b, is_s, rs)
            tasks = []
            for fc in range(NFC):
                cell = {}

                def s1(fc=fc, cell=cell, n=n, is_s=is_s):
                    pg, pgb = psg.next()
                    pu, pub = psu.next()
                    for kc in range(8):
                        self.mm(pg[:, 0:n], wg[0][:, kc * DFF + fc * 128: kc * DFF + fc * 128 + 128], h2t[:, kc, 0:n], kc == 0, kc == 7,
                                [wg[1], h2t_b], [pgb])
                    for kc in range(8):
                        self.mm(pu[:, 0:n], wu[0][:, kc * DFF + fc * 128: kc * DFF + fc * 128 + 128], h2t[:, kc, 0:n], kc == 0, kc == 7,
                                [wu[1], h2t_b], [pub])
                    gb, gb_b = gbr.next()
                    self.act(gb[:, 2:2 + n], pg[:, 0:n], AF.Copy, [pgb], [gb_b])
                    if not is_s:
                        self.cp("pool", gb[:, 0:2], halo[:, fc, 0, :], [halo_b, gb_b], [gb_b])
                        self.cp("pool", halo[:, fc, 0, :], gb[:, n:n + 2], [gb_b], [halo_b])
                    else:
                        self.cp("pool", gb[:, 0:2], halo[:, fc, 1, :], [halo_b, gb_b], [gb_b])
                        self.cp("pool", gb[:, 32:34], halo[:, fc, 2, :], [halo_b, gb_b], [gb_b])
                        self.cp("pool", halo[:, fc, 1, :], gb[:, 16:18], [gb_b], [halo_b])
                        self.cp("pool", halo[:, fc, 2, :], gb[:, 48:50], [gb_b], [halo_b])
                    a1, a1_b = a1r.next()
                    self.ts("pool", a1[:, 0:n], gb[:, 0:n], cw[0][:, fc, 0:1], cbi[0][:, fc:fc + 1], ALU.mult, ALU.add,
                            [gb_b, cw[1], cbi[1]], [a1_b])
                    cell.update(pu=(pu, pub), gb=(gb, gb_b), a1=(a1, a1_b))

                def s2(fc=fc, cell=cell, n=n):
                    gb, gb_b = cell["gb"]; a1, a1_b = cell["a1"]
                    a2, a2_b = a2r.next()
                    self.stt("dve", a2[:, 0:n], gb[:, 1:n + 1], cw[0][:, fc, 1:2], a1[:, 0:n], ALU.mult, ALU.add, [gb_b, cw[1], a1_b], [a2_b])
                    self.stt("dve", a1[:, 0:n], gb[:, 2:n + 2], cw[0][:, fc, 2:3], a2[:, 0:n], ALU.mult, ALU.add, [gb_b, cw[1], a2_b], [a1_b])
                    self.act(a2[:, 0:n], a1[:, 0:n], AF.Silu, [a1_b], [a2_b])
                    cell["a2"] = (a2, a2_b)

                def s3(fc=fc, cell=cell, n=n):
                    a2, a2_b = cell["a2"]; pu, pub = cell["pu"]
                    self.tt("dve", actt[:, fc, 0:n], a2[:, 0:n], pu[:, 0:n], ALU.mult, [a2_b, pub], [actt_b])
                tasks.append((s1, s2, s3))
            self.run_pipeline(tasks, 1, 1)
            for (i, rows, c0) in blk:
                lc = c0 - cb0
                x, x_b = rs["x"].next()
                self.dma(x[0:rows, :], xsrc[c0:c0 + rows, :], w=[x_b])
                g_ap, g_b = (gs if is_s else gp)
                dst = xdst_s[0:64, :] if is_s else xdst_p[c0:c0 + rows, :]
                self.resid_out(lambda kc, half: (actt[:, kc, lc:lc + rows], wd[0][:, kc * D + half * 512: kc * D + half * 512 + 512]),
                               NFC, [actt_b, wd[1]], rows, x, x_b, g_ap, g_b, pso, rs["sm"], jk5, tmp, dst)
        self.dma(d["conv_p"][l], halo[:, :, 0, :], r=[halo_b])
        for s_ in range(2):
            self.dma(d["conv_s"][l, s_], halo[:, :, 1 + s_, :], r=[halo_b])
        self.kb.barrier()

    def phase_pre1(self, xsrc_p, xsrc_s):
        T, NT, TOK = self.T, self.NT, self.TOK
        d = self.dr
        rs, A = self.pre_common(1)
        win = A([128, 8 * 3072], BF16, "win")
        self.load_w(win[0], win[1], d["win1"], 8 * 3072, rs["stg"])
        o512 = Rot([A([128, 512], F32, "o512_%d" % i) for i in range(3)])
        vst = [A([128, 8, 65], BF16, "vst%d" % i) for i in range(3)]
        for v, vb in vst:
            self.memset("pool", v[:], 1.0, [vb])
        vrot = Rot(vst)
        fst = Rot([A([128, 512], BF16, "fst%d" % i) for i in range(3)])
        ptasks = []
        for blk in self.blocks():
            cell = {}

            def blk_s1(blk=blk, cell=cell):
                is_s = blk[0][0] == NT
                n = 64 if is_s else 512
                cb0 = blk[0][2]
                htb, htb_b = rs["htb"].next()
                for (i, rows, c0) in blk:
                    x, x_b = rs["x"].next()
                    self.dma(x[0:rows, :], self.xrows(xsrc_p, xsrc_s, i, rows), w=[x_b])
                    self.norm_tile(x, x_b, rows, htb, htb_b, c0 - cb0, rs["afm"], rs["bfm"], rs["fm_b"], is_s, rs)
                cell["htb"] = (htb, htb_b)

            def blk_s2(blk=blk, cell=cell):
                is_s = blk[0][0] == NT
                n = 64 if is_s else 512
                cb0 = blk[0][2]
                htb, htb_b = cell["htb"]
                for (i, rows, c0) in blk:
                    lc = c0 - cb0
                    rsl = slice(i * 128, i * 128 + rows) if not is_s else slice(0, 64)
                    sfx = "_s" if is_s else "_p"
                    for (wc, oname, vname) in ((512, "c_k", None), (1024, "c_v", "vs_a"), (2048, "d_k", None), (2560, "d_v", "vs_b")):
                        ps, pb = rs["psa"].next()
                        for kc in range(8):
                            self.mm(ps[0:rows, :], htb[:, kc, lc:lc + rows], win[0][:, kc * 3072 + wc: kc * 3072 + wc + 512], kc == 0, kc == 7,
                                    [htb_b, win[1]], [pb])
                        self.tm_out(ps, pb, rows, 512, d[oname + sfx][rsl, :], o512, "act")
                        if vname is not None:
                            self.v_store(ps, pb, rows, vrot, d[vname][c0:c0 + rows, :])
                for pr in range(4):
                    for (wc, dn, sc_) in ((0, "qs_c", 0.125), (512, "ks_c", 1.0), (1536, "qs_b", 0.125), (2048, "ks_b", 1.0)):
                        self.fm_proj(win[0], win[1], wc + pr * 128, 8, 3072, lambda kc: htb[:, kc, 0:n], htb_b, n, 128, rs, sc_,
                                     d[dn][pr * 128:(pr + 1) * 128, cb0:cb0 + n], fst)
            ptasks.append((blk_s1, blk_s2))
        self.run_pipeline(ptasks, 1, 0)

        ld = A([128, 8, 512], F32, "pld"); bf = A([128, 8, 512], BF16, "pbf")
        vpast = A([128, 8, 8, 65], BF16, "vpast")
        self.memset("pool", vpast[0][:], 1.0, [vpast[1]])
        for s_ in range(2):
            pc0 = TOK + s_ * PAST
            self.past_kT(d["cc_k"][s_], 4, lambda pr, c, w_: d["ks_c"][pr * 128:(pr + 1) * 128, pc0 + c:pc0 + c + w_], rs, ld, bf, fst)
            self.past_v(d["cc_v"][s_], 4, d["vs_a"][pc0:pc0 + CPAST, :], ld, vpast)
            self.past_kT(d["cd_k"][s_], 8, lambda pr, c, w_: d["ks_b"][pr * 128:(pr + 1) * 128, pc0 + c:pc0 + c + w_], rs, ld, bf, fst)
            self.past_v(d["cd_v"][s_], 8, d["vs_b"][pc0:pc0 + PAST, :], ld, vpast)
        self.kb.barrier()

    def phase_att1(self):
        T, NT, TOK = self.T, self.NT, self.TOK
        d = self.dr
        res, A = self.att_alloc()
        idb, idb_b = self.P("ident_bf"); idf, idf_b = self.P("ident_f")
        msb, msb_b = self.P("m_sb")
        nu, nu_b = self.P("nu"); nl, nl_b = self.P("nl"); nones, nones_b = self.P("nones")
        mdt, mdt_b = A([128, 5, 128], F32, "md")
        self.dma(mdt, d["md"].rearrange("e k q -> k e q"), w=[mdt_b])
        bdh, bdh_b = A([128, 40, 128], BF16, "bdh")
        bdl, bdl_b = A([128, 40, 128], BF16, "bdl")
        bdt = Rot([A([128, 5, 128], F32, "bdt%d" % i) for i in range(2)])
        for g in range(8):
            bd, bd_b = bdt.next()
            sl = slice(g * 5, g * 5 + 5)
            self.dma(bd, d["bd"][g].rearrange("e k q -> k e q"), w=[bd_b])
            for e in (0, 4):
                self.tt("pool", bd[:, e, :], bd[:, e, :], mdt[:, e, :], ALU.add, [bd_b, mdt_b], [bd_b])
            self.cp("dve", bdh[:, sl, :], bd[:, :, :], [bd_b], [bdh_b])
            self.tt("pool", bd[:, :, :], bd[:, :, :], bdh[:, sl, :], ALU.subtract, [bd_b, bdh_b], [bd_b])
            self.cp("dve", bdl[:, sl, :], bd[:, :, :], [bd_b], [bdl_b])
        et = Rot([A([128, 512], F32, "e%d" % i) for i in range(5)])
        spt = Rot([A([128, 512], BF16, "sp%d" % i) for i in range(6)])
        ext = Rot([A([128, 512], F32, "ex%d" % i) for i in range(2)])

        def band_tasks(h, c):
            qv, kv, vt, vt_b, obuf, obuf_b, hh = c["qv"], c["kv"], c["vt"], c["vt_b"], c["obuf"], c["obuf_b"], c["hh"]
            rb = [c["kt_b"], c["qt_b"]]
            tasks = []
            for i in range(NT):
                ktl = []
                for j in range(max(0, i - 4), i + 1):
                    e = i - j
                    ktl.append(dict(nk=128, pbase=0, clo=0, kT=kv[:, j * 128:(j + 1) * 128], rb=rb, v=vt[:, j, hh, :], vb=[vt_b],
                                    masks=[(0, 128, idb[:], bdh[:, h * 5 + e, :], [idb_b, bdh_b]),
                                           (0, 128, idb[:], bdl[:, h * 5 + e, :], [idb_b, bdl_b])]))
                tasks += self.softmax_tasks(qv[:, i * 128:(i + 1) * 128], 64, 128, [(0, 128, 0)], ktl, res,
                                            lambda qi, i=i: (obuf[:, i, hh * 64:(hh + 1) * 64], obuf_b))
            for s_ in range(2):
                b0 = 32 * s_
                qc = T + b0
                ktl = []
                for m in range(4):
                    kc0 = TOK + s_ * PAST + m * 128
                    e = 1 if m == 3 else 2
                    ktl.append(dict(nk=128, pbase=0, clo=0, kT=kv[:, kc0:kc0 + 128], rb=rb,
                                    v=vt[:, NT + 1 + s_ * 8 + m, hh, :], vb=[vt_b],
                                    masks=[(0, 16, idb[:], bdh[:, h * 5 + e, 0:16], [idb_b, bdh_b]),
                                           (0, 16, idb[:], bdl[:, h * 5 + e, 0:16], [idb_b, bdl_b])]))
                ktl.append(dict(nk=16, pbase=b0, clo=0, kT=kv[:, qc:qc + 16], rb=rb, v=vt[b0:b0 + 16, NT, hh, :], vb=[vt_b],
                                masks=[(0, 16, idb[:, b0:b0 + 16], bdh[:, h * 5, b0:b0 + 16], [idb_b, bdh_b]),
                                       (0, 16, idb[:, b0:b0 + 16], bdl[:, h * 5, b0:b0 + 16], [idb_b, bdl_b])]))
                tasks += self.softmax_tasks(qv[:, qc:qc + 16], 64, 16, [(0, 16, b0)], ktl, res,
                                            lambda qi, b0=b0: (obuf[b0:b0 + 16, NT, hh * 64:(hh + 1) * 64], obuf_b))
            return tasks
        self.att_drive(res, "band", ("qs_c", "ks_c", "vs_a"), band_tasks, 0)

        def sb_tasks(h, c):
            qv, kv, vt, vt_b, obuf, obuf_b, hh = c["qv"], c["kv"], c["vt"], c["vt_b"], c["obuf"], c["obuf_b"], c["hh"]
            rb = [c["kt_b"], c["qt_b"]]
            tasks = []
            for qb in range(NT // 4):
                qc = qb * 512
                st = {"firstx": True, "firsto": True, "prev": None, "xb": None, "ob": None}
                js = list(range(4 * qb + 3, -1, -1))
                for ji, j in enumerate(js):
                    r = j - 4 * qb
                    clo = max(r, 0) * 128
                    cell = {}

                    def s1(j=j, r=r, clo=clo, cell=cell, st=st, ji=ji, qc=qc):
                        if ji == 0:
                            st["xb"] = res["xbank"].next()
                            st["ob"] = res["obank"].next()
                        zb, zb_b = res["sbank"].next()
                        self.mm(zb[:, clo:512], kv[:, j * 128:(j + 1) * 128], qv[:, qc + clo:qc + 512], True, r < 0, rb, [zb_b],
                                skip_group_check=r >= 0)
                        if r >= 0:
                            self.mm(zb[:, clo:clo + 128], idb[:], msb[:], False, True, [idb_b, msb_b], [zb_b], skip_group_check=True)
                        e_, e_b = et.next()
                        sp_, sp_b = spt.next()
                        self.act(e_[:, clo:512], zb[:, clo:512], AF.Exp, [zb_b], [e_b])
                        self.act(sp_[:, clo:512], e_[:, clo:512], AF.Ln, [e_b], [sp_b], scale=1.0, bias=1.0)
                        cell["e"] = (e_, e_b); cell["sp"] = (sp_, sp_b)

                    def s2(j=j, r=r, clo=clo, cell=cell, st=st, ji=ji):
                        xb, xb_b = st["xb"]
                        e_, e_b = cell["e"]; sp_, sp_b = cell["sp"]
                        if st["prev"] is not None:
                            psp, psp_b, pclo = st["prev"]
                            self.mm(xb[:, pclo:512], nl[:], psp[:, pclo:512], st["firstx"], False, [nl_b, psp_b], [xb_b], skip_group_check=True)
                            st["firstx"] = False
                        self.mm(xb[:, clo:512], nu[:], sp_[:, clo:512], st["firstx"], False, [nu_b, sp_b], [xb_b], skip_group_check=True)
                        st["firstx"] = False
                        st["prev"] = (sp_, sp_b, clo)
                        ex_, ex_b = ext.next()
                        self.act(ex_[:, clo:512], xb[:, clo:512], AF.Exp, [xb_b], [ex_b])
                        w_, w_b = res["pt"].next()
                        self.tt("dve", w_[:, clo:512], ex_[:, clo:512], e_[:, clo:512], ALU.mult, [ex_b, e_b], [w_b])
                        cell["w"] = (w_, w_b)

                    def s3(j=j, r=r, cell=cell, st=st, last=(ji == len(js) - 1), qb=qb):
                        ob, ob_b = st["ob"]
                        w_, w_b = cell["w"]
                        for m in range(max(r, 0), 4):
                            self.mm(ob[:, m * 65:m * 65 + 65], w_[:, m * 128:(m + 1) * 128], vt[:, j, hh, :], st["firsto"], False,
                                    [w_b, vt_b], [ob_b], skip_group_check=True)
                            st["firsto"] = False
                        if last:
                            for m in range(4):
                                self.cp("dve", obuf[:, 4 * qb + m, hh * 64:(hh + 1) * 64], ob[:, m * 65:m * 65 + 64], [ob_b], [obuf_b])
                    tasks.append((s1, s2, s3))

            return tasks

        def sb_post(h, c):
            qv, kv, vt, vt_b, obuf, obuf_b, hh = c["qv"], c["kv"], c["vt"], c["vt_b"], c["obuf"], c["obuf_b"], c["hh"]
            rb = [c["kt_b"], c["qt_b"]]

            def samp(s_):
                b0 = 32 * s_
                qc = T + b0
                e_, e_b = et.next()
                sp_, sp_b = spt.next()
                tl = []
                for m in range(8):
                    kc0 = TOK + s_ * PAST + m * 128
                    tl.append((kv[:, kc0:kc0 + 128], 0, 128, vt[:, NT + 1 + s_ * 8 + m, hh, :]))
                tl.append((kv[:, qc:qc + 16], b0, 16, vt[b0:b0 + 16, NT, hh, :]))
                for t, (kT, pb0, nk, v_) in enumerate(tl):
                    zb, zb_b = res["sbank"].next()
                    self.mm(zb[pb0:pb0 + nk, 0:16], kT, qv[:, qc:qc + 16], True, t < 8, rb, [zb_b], skip_group_check=t == 8)
                    if t == 8:
                        self.mm(zb[pb0:pb0 + nk, 0:16], idb[:, b0:b0 + 16], msb[:, b0:b0 + 16], False, True,
                                [idb_b, msb_b], [zb_b], skip_group_check=True)
                    self.act(e_[pb0:pb0 + nk, t * 16:t * 16 + 16], zb[pb0:pb0 + nk, 0:16], AF.Exp, [zb_b], [e_b])
                    self.act(sp_[pb0:pb0 + nk, t * 16:t * 16 + 16], e_[pb0:pb0 + nk, t * 16:t * 16 + 16], AF.Ln, [e_b], [sp_b],
                             scale=1.0, bias=1.0)
                ob, ob_b = res["obank"].next()
                firsto = True
                for t, (kT, pb0, nk, v_) in enumerate(tl):
                    xb, xb_b = res["xbank"].next()
                    self.mm(xb[pb0:pb0 + nk, 0:16], nu[pb0:pb0 + nk, pb0:pb0 + nk], sp_[pb0:pb0 + nk, t * 16:t * 16 + 16], True, t == 8,
                            [nu_b, sp_b], [xb_b], skip_group_check=t < 8)
                    if t < 8:
                        for t2 in range(t + 1, 9):
                            p2, n2 = tl[t2][1], tl[t2][2]
                            self.mm(xb[0:128, 0:16], nones[p2:p2 + n2, 0:128], sp_[p2:p2 + n2, t2 * 16:t2 * 16 + 16], False, t2 == 8,
                                    [nones_b, sp_b], [xb_b], skip_group_check=True)
                    ex_, ex_b = ext.next()
                    self.act(ex_[pb0:pb0 + nk, 0:16], xb[pb0:pb0 + nk, 0:16], AF.Exp, [xb_b], [ex_b])
                    w_, w_b = res["pt"].next()
                    self.tt("dve", w_[pb0:pb0 + nk, 0:16], ex_[pb0:pb0 + nk, 0:16], e_[pb0:pb0 + nk, t * 16:t * 16 + 16], ALU.mult,
                            [ex_b, e_b], [w_b])
                    self.mm(ob[b0:b0 + 16, 0:65], w_[pb0:pb0 + nk, 0:16], v_, firsto, False, [w_b, vt_b], [ob_b], skip_group_check=True)
                    firsto = False
                self.cp("dve", obuf[b0:b0 + 16, NT, hh * 64:(hh + 1) * 64], ob[b0:b0 + 16, 0:64], [ob_b], [obuf_b])
            for s_ in range(2):
                samp(s_)
        self.att_drive(res, "sb", ("qs_b", "ks_b", "vs_b"), sb_tasks, 512, post_fn=sb_post)
        self.kb.barrier()

    def build(self, consts, shared, core):
        self.setup(consts, shared, core)
        d = self.dr
        T = self.T
        import os
        stop = int(os.environ.get("KSTOP", "99"))
        phases = [lambda: self.phase_pre0(d["xp"], d["xs"]),
                  lambda: self.phase_att0(),
                  lambda: self.phase_proj(0, d["xp"], d["xs"], d["x1"]),
                  lambda: self.phase_ffn(0, d["x1"], d["x2"][0:T, :], d["x2"][T:T + 64, :]),
                  lambda: self.phase_pre1(d["x2"][0:T, :], d["x2"][T:T + 64, :]),
                  lambda: self.phase_att1(),
                  lambda: self.phase_proj(1, d["x2"][0:T, :], d["x2"][T:T + 64, :], d["x1"]),
                  lambda: self.phase_ffn(1, d["x1"], d["y_p"], d["y_s"])]
        pnames = ["pre0", "att0", "proj0", "ffn0", "pre1", "att1", "proj1", "ffn1"]
        for i, ph in enumerate(phases):
            if i < stop:
                self.kb.phase = pnames[i]
                ph()
        print("ops:", {e: len(v) for e, v in self.kb.ops.items()}, "dma:", self.kb.dma_n, flush=True)
        self.kb.finish()
        self.kb.emit()
        return self.nc


def kernel(**inputs):
    inp = {k: np.asarray(v) for k, v in inputs.items()}
    B, T = inp["x_prompt"].shape[0], inp["x_prompt"].shape[1]
    ncores = 8
    assert B == ncores
    consts = _consts(T)
    shared = _shared_inputs(inp)
    cores = []
    for b in range(ncores):
        m = _core_inputs(inp, b, T)
        stc = m.pop("st_conv")
        m["st_conv_fm"] = np.ascontiguousarray(stc.reshape(2, 2, 2, NFC, 128).transpose(0, 1, 4, 3, 2))
        cores.append(m)
    prog = Prog(T)
    nc = prog.build(consts, shared, cores[0])
    in_maps = []
    for b in range(ncores):
        m = {}
        m.update(consts)
        m.update(shared)
        m.update(cores[b])
        in_maps.append(m)
    res = run_bass_kernel_spmd(nc, in_maps, core_ids=list(range(ncores)))
    R = res.results

    def samp(name, tail):
        out = []
        for b in range(ncores):
            a = R[b][name]
            out.append(a[0:16]); out.append(a[32:48])
        return np.stack(out, 0).reshape((2 * ncores, 16) + tail)

    def prm(name, tail, last=None):
        a = np.stack([R[b][name] for b in range(ncores)], 0)
        if last is not None:
            a = a[:, T - last:]
        return a.reshape((ncores, a.shape[1]) + tail)

    keep = min(512, T)
    outs = [prm("y_p", (D,)), samp("y_s", (D,)),
            prm("a_ckv_p", (128,))[None], samp("a_ckv_s", (128,))[None],
            prm("a_kr_p", (32,))[None], samp("a_kr_s", (32,))[None],
            prm("b_k_p", (8, 64))[None], samp("b_k_s", (8, 64))[None],
            prm("b_v_p", (8, 64))[None], samp("b_v_s", (8, 64))[None],
            prm("b_lf_p", (8,))[None], samp("b_lf_s", (8,))[None],
            prm("c_k_p", (8, 64), keep)[None], samp("c_k_s", (8, 64))[None],
            prm("c_v_p", (8, 64), keep)[None], samp("c_v_s", (8, 64))[None],
            prm("d_k_p", (8, 64))[None], samp("d_k_s", (8, 64))[None],
            prm("d_v_p", (8, 64))[None], samp("d_v_s", (8, 64))[None]]
    cp_ = np.stack([R[b]["conv_p"] for b in range(ncores)], 1)
    outs.append(np.ascontiguousarray(cp_.transpose(0, 1, 4, 3, 2).reshape(2, ncores, 2, DFF)))
    cs_ = np.stack([R[b]["conv_s"] for b in range(ncores)], 1)
    cs_ = cs_.transpose(0, 1, 2, 5, 4, 3).reshape(2, 2 * ncores, 2, DFF)
    outs.append(np.ascontiguousarray(cs_))
    return tuple(np.ascontiguousarray(o, dtype=np.float32) for o in outs)
```
